# Optimizing a Trainium2 kernel written in Bass

```python
import jax, jax.numpy as jnp
from jax import lax
import numpy as np

D_MODEL = 1024
BATCH = 4
SEQ = 8192
DEPTH = 2

BRANCH_WIDTH = D_MODEL // 2
N_BRANCH = 3
LN_EPS = 1e-5
RW_HEAD_DIM = 64
RW_HEADS = BRANCH_WIDTH // RW_HEAD_DIM
RW_DECAY_RANK = 64
RW_ICLR_RANK = 64
RW_GN_EPS = 64e-5
M2_HEAD_DIM = 64
M2_HEADS = BRANCH_WIDTH // M2_HEAD_DIM
M2_GROUPS = 2
M2_STATE = 128
M2_CONV = 4
M2_CHUNK = 128
M2_EPS = 1e-5
GLA_HEADS = 4
GLA_KEY_DIM = BRANCH_WIDTH // 2
GLA_VAL_DIM = BRANCH_WIDTH
GLA_GATE_RANK = 16
GLA_GATE_TAU = 16.0
GLA_CHUNK = 64
GLA_EPS = 1e-5
DEEPNORM_ALPHA = (2.0 * DEPTH) ** 0.25
DEEPNORM_BETA = (8.0 * DEPTH) ** -0.25
RW_SHIFT_WIDTH = 3 * BRANCH_WIDTH + RW_DECAY_RANK + RW_ICLR_RANK
M2_XBC_WIDTH = BRANCH_WIDTH + 2 * M2_GROUPS * M2_STATE
IN_SIZES = (RW_SHIFT_WIDTH, BRANCH_WIDTH,
            M2_XBC_WIDTH, M2_HEADS, BRANCH_WIDTH,
            GLA_KEY_DIM, GLA_KEY_DIM, GLA_VAL_DIM, GLA_GATE_RANK, BRANCH_WIDTH,
            N_BRANCH * D_MODEL)
IN_WIDTH = sum(IN_SIZES)

kernel_name = "hybrid_rwkv7_mamba2_gla_deepnorm_adaln"


def _layer_norm(x, eps):
    xf = x.astype(jnp.float32)
    mu = jnp.mean(xf, axis=-1, keepdims=True)
    var = jnp.mean(jnp.square(xf - mu), axis=-1, keepdims=True)
    return ((xf - mu) * lax.rsqrt(var + eps)).astype(x.dtype)


def _rms_norm(x, eps):
    xf = x.astype(jnp.float32)
    return (xf * lax.rsqrt(jnp.mean(jnp.square(xf), axis=-1, keepdims=True) + eps)).astype(x.dtype)


def _token_shift(x):
    return jnp.pad(x, ((0, 0), (1, 0), (0, 0)))[:, :-1]


def _causal_mask(n):
    return jnp.arange(n)[:, None] >= jnp.arange(n)[None, :]


def _rwkv7_mixer(r, k, v, wd, ad, w0, w_up, a0, a_up, k_k, k_a, r_k, ln_g, ln_b):
    out_dtype = r.dtype
    bsz, seq, _ = r.shape
    f32 = lambda t: t.astype(jnp.float32)
    heads = lambda t: f32(t).reshape(bsz, seq, RW_HEADS, RW_HEAD_DIM)
    w = f32(w0 + jnp.tanh(wd) @ w_up)
    decay = jnp.exp(-jnp.exp(-jax.nn.softplus(-w) - 0.5))
    a = jax.nn.sigmoid(f32(a0 + ad @ a_up))
    kk = heads(k * k_k)
    kk = kk / jnp.maximum(jnp.sqrt(jnp.sum(kk * kk, axis=-1, keepdims=True)), 1e-12)
    k = f32(k) * (1.0 + (a - 1.0) * f32(k_a))
    r, k, v, a, decay = heads(r), heads(k), heads(v), heads(a), heads(decay)

    def step(state, inp):
        r_t, w_t, k_t, v_t, kk_t, a_t = inp
        sa = jnp.einsum('bhij,bhj->bhi', state, -kk_t)
        state = (state * w_t[:, :, None, :]
                 + sa[..., None] * (kk_t * a_t)[:, :, None, :]
                 + v_t[..., None] * k_t[:, :, None, :])
        return state, jnp.einsum('bhij,bhj->bhi', state, r_t)

    state0 = jnp.zeros((bsz, RW_HEADS, RW_HEAD_DIM, RW_HEAD_DIM), jnp.float32)
    sm = lambda t: jnp.moveaxis(t, 1, 0)
    _, y = lax.scan(step, state0, (sm(r), sm(decay), sm(k), sm(v), sm(kk), sm(a)))
    y = jnp.moveaxis(y, 0, 1)
    y = _layer_norm(y, RW_GN_EPS).reshape(bsz, seq, BRANCH_WIDTH) * f32(ln_g) + f32(ln_b)
    bonus = jnp.sum(r * k * f32(r_k), axis=-1, keepdims=True) * v
    return (y + bonus.reshape(bsz, seq, BRANCH_WIDTH)).astype(out_dtype)


def _causal_depthwise_conv(x, w, b):
    y = lax.conv_general_dilated(x, w[:, None, :].astype(x.dtype), (1,), ((w.shape[0] - 1, 0),),
                                 dimension_numbers=('NWC', 'WIO', 'NWC'),
                                 feature_group_count=x.shape[-1])
    return y + b


def _mamba2_mixer(xbc, dt_raw, z, conv_w, conv_b, dt_bias, a_log, d_skip, norm_w):
    out_dtype = z.dtype
    bsz, seq, _ = xbc.shape
    f32 = lambda t: t.astype(jnp.float32)
    nc, hpg = seq // M2_CHUNK, M2_HEADS // M2_GROUPS
    xbc = jax.nn.silu(f32(_causal_depthwise_conv(xbc, conv_w, conv_b)))
    xs, bm, cm = jnp.split(xbc, [BRANCH_WIDTH, BRANCH_WIDTH + M2_GROUPS * M2_STATE], axis=-1)
    xs = xs.reshape(bsz, nc, M2_CHUNK, M2_GROUPS, hpg, M2_HEAD_DIM)
    bm = bm.reshape(bsz, nc, M2_CHUNK, M2_GROUPS, M2_STATE)
    cm = cm.reshape(bsz, nc, M2_CHUNK, M2_GROUPS, M2_STATE)
    dt = jax.nn.softplus(f32(dt_raw) + f32(dt_bias)).reshape(bsz, nc, M2_CHUNK, M2_GROUPS, hpg)
    a_head = -jnp.exp(f32(a_log)).reshape(M2_GROUPS, hpg)
    a_cs = jnp.moveaxis(jnp.cumsum(dt * a_head, axis=2), 2, -1)
    xdt = xs * dt[..., None]
    diff = a_cs[..., :, None] - a_cs[..., None, :]
    decay_in = jnp.exp(jnp.where(_causal_mask(M2_CHUNK), diff, -jnp.inf))
    cb = jnp.einsum('bclgn,bcsgn->bcgls', cm, bm)
    y_diag = jnp.einsum('bcghls,bcsghp->bclghp', cb[:, :, :, None] * decay_in, xdt)
    decay_to_end = jnp.exp(a_cs[..., -1:] - a_cs)
    chunk_states = jnp.einsum('bclgn,bcghl,bclghp->bcghpn', bm, decay_to_end, xdt)
    chunk_decay = jnp.exp(a_cs[..., -1])

    def step(state, inp):
        s_c, d_c = inp
        return state * d_c[..., None, None] + s_c, state

    state0 = jnp.zeros((bsz, M2_GROUPS, hpg, M2_HEAD_DIM, M2_STATE), jnp.float32)
    _, prev = lax.scan(step, state0, (jnp.moveaxis(chunk_states, 1, 0), jnp.moveaxis(chunk_decay, 1, 0)))
    prev = jnp.moveaxis(prev, 0, 1)
    y_off = jnp.einsum('bclgn,bcghpn,bcghl->bclghp', cm, prev, jnp.exp(a_cs))
    y = y_diag + y_off + xs * f32(d_skip).reshape(M2_GROUPS, hpg)[:, :, None]
    y = y.reshape(bsz, seq, BRANCH_WIDTH) * jax.nn.silu(f32(z))
    y = _rms_norm(y.reshape(bsz, seq, M2_GROUPS, BRANCH_WIDTH // M2_GROUPS), M2_EPS)
    return (y.reshape(bsz, seq, BRANCH_WIDTH) * f32(norm_w)).astype(out_dtype)


def _gla_mixer(q, k, v, gk_down, g, gk_up, gk_b, norm_w):
    out_dtype = q.dtype
    bsz, seq, _ = q.shape
    f32 = lambda t: t.astype(jnp.float32)
    nc = seq // GLA_CHUNK
    dk, dv = GLA_KEY_DIM // GLA_HEADS, GLA_VAL_DIM // GLA_HEADS
    log_alpha = jax.nn.log_sigmoid(f32(gk_down @ gk_up + gk_b)) / GLA_GATE_TAU
    chunk = lambda t, d: f32(t).reshape(bsz, nc, GLA_CHUNK, GLA_HEADS, d)
    q = chunk(q, dk) * (dk ** -0.5)
    k = chunk(k, dk)
    v = chunk(v, dv)
    g_cs = jnp.cumsum(chunk(log_alpha, dk), axis=2)
    g_last = g_cs[:, :, -1:]
    q_g = q * jnp.exp(g_cs)
    att = jnp.einsum('bclhd,bcshd->bchls', q_g, k * jnp.exp(-g_cs))
    att = jnp.where(_causal_mask(GLA_CHUNK), att, 0.0)
    o_intra = jnp.einsum('bchls,bcshe->bclhe', att, v)
    chunk_kv = jnp.einsum('bclhd,bclhe->bchde', k * jnp.exp(g_last - g_cs), v)
    chunk_decay = jnp.exp(g_last[:, :, 0])

    def step(state, inp):
        kv_c, d_c = inp
        return state * d_c[..., None] + kv_c, state

    state0 = jnp.zeros((bsz, GLA_HEADS, dk, dv), jnp.float32)
    _, prev = lax.scan(step, state0, (jnp.moveaxis(chunk_kv, 1, 0), jnp.moveaxis(chunk_decay, 1, 0)))
    prev = jnp.moveaxis(prev, 0, 1)
    o = o_intra + jnp.einsum('bclhd,bchde->bclhe', q_g, prev)
    o = _rms_norm(o.reshape(bsz, seq, GLA_HEADS, dv), GLA_EPS) * f32(norm_w)
    return (o.reshape(bsz, seq, GLA_VAL_DIM) * jax.nn.silu(f32(g))).astype(out_dtype)


def _hybrid_layer(x, c, ada_w, ada_b, w_in, rw_mu, rw_w0, rw_w_up, rw_a0, rw_a_up, rw_k_k,
                  rw_k_a, rw_r_k, rw_ln_g, rw_ln_b, m2_conv_w, m2_conv_b, m2_dt_bias, m2_a_log,
                  m2_d_skip, m2_norm_w, gla_gk_up, gla_gk_b, gla_norm_w, w_branch, w_out,
                  post_g, post_b):
    bsz, seq, _ = x.shape
    shift, scale, gate = jnp.split(jax.nn.silu(c) @ ada_w + ada_b, 3, axis=-1)
    h = _layer_norm(x, LN_EPS) * (1.0 + scale[:, None, :]) + shift[:, None, :]
    proj = h @ w_in
    (rw_in, rw_gate, m2_xbc, m2_dt, m2_z, gla_q, gla_k, gla_v, gla_gk, gla_gate,
     merge_logits) = jnp.split(proj, np.cumsum(IN_SIZES)[:-1].tolist(), axis=-1)
    rw_in = rw_in + rw_mu * (_token_shift(rw_in) - rw_in)
    r, k, v, wd, ad = jnp.split(
        rw_in, np.cumsum([BRANCH_WIDTH, BRANCH_WIDTH, BRANCH_WIDTH, RW_DECAY_RANK]).tolist(), axis=-1)
    y_rw = _rwkv7_mixer(r, k, v, wd, ad, rw_w0, rw_w_up, rw_a0, rw_a_up, rw_k_k, rw_k_a, rw_r_k,
                        rw_ln_g, rw_ln_b) * jax.nn.silu(rw_gate)
    y_m2 = _mamba2_mixer(m2_xbc, m2_dt, m2_z, m2_conv_w, m2_conv_b, m2_dt_bias, m2_a_log,
                         m2_d_skip, m2_norm_w)
    y_gla = _gla_mixer(gla_q, gla_k, gla_v, gla_gk, gla_gate, gla_gk_up, gla_gk_b, gla_norm_w)
    gates = jax.nn.sigmoid(merge_logits.astype(jnp.float32)).astype(x.dtype)
    gates = gates.reshape(bsz, seq, N_BRANCH, D_MODEL)
    merged = (gates[:, :, 0] * (y_rw @ w_branch[0])
              + gates[:, :, 1] * (y_m2 @ w_branch[1])
              + gates[:, :, 2] * (y_gla @ w_branch[2]))
    y = merged @ w_out
    res = DEEPNORM_ALPHA * x + (1.0 + gate[:, None, :]) * y
    return _layer_norm(res, LN_EPS) * post_g + post_b


def setup_inputs(seed: int = 0) -> dict:
    key = jax.random.key(seed)
    ks = iter(jax.random.split(key, 32))
    nrm = lambda shape, s: jax.random.normal(next(ks), shape, jnp.float32) * s
    uni = lambda shape, lo, hi: jax.random.uniform(next(ks), shape, jnp.float32, lo, hi)
    L, D, W = DEPTH, D_MODEL, BRANCH_WIDTH
    x = nrm((BATCH, SEQ, D), 1.0)
    c = nrm((BATCH, D), 1.0)
    ada_w = nrm((L, D, 3 * D), 0.5 * D ** -0.5)
    ada_b = nrm((L, 3 * D), 0.01)
    w_in = nrm((L, D, IN_WIDTH), D ** -0.5)
    rw_mu = uni((L, RW_SHIFT_WIDTH), 0.0, 1.0)
    rw_w0 = uni((L, W), -6.0, -1.0)
    rw_w_up = nrm((L, RW_DECAY_RANK, W), RW_DECAY_RANK ** -0.5)
    rw_a0 = nrm((L, W), 0.1)
    rw_a_up = nrm((L, RW_ICLR_RANK, W), RW_ICLR_RANK ** -0.5)
    rw_k_k = 0.85 + nrm((L, W), 0.05)
    rw_k_a = 1.0 + nrm((L, W), 0.05)
    rw_r_k = nrm((L, RW_HEADS, RW_HEAD_DIM), 0.1)
    rw_ln_g = 1.0 + nrm((L, W), 0.02)
    rw_ln_b = nrm((L, W), 0.02)
    m2_conv_w = nrm((L, M2_CONV, M2_XBC_WIDTH), M2_CONV ** -0.5)
    m2_conv_b = nrm((L, M2_XBC_WIDTH), 0.01)
    dt0 = jnp.exp(uni((L, M2_HEADS), float(np.log(1e-3)), float(np.log(1e-1))))
    m2_dt_bias = dt0 + jnp.log(-jnp.expm1(-dt0))
    m2_a_log = jnp.log(uni((L, M2_HEADS), 1.0, 16.0))
    m2_d_skip = 1.0 + nrm((L, M2_HEADS), 0.02)
    m2_norm_w = 1.0 + nrm((L, W), 0.02)
    gla_gk_up = nrm((L, GLA_GATE_RANK, GLA_KEY_DIM), GLA_GATE_RANK ** -0.5)
    gla_gk_b = nrm((L, GLA_KEY_DIM), 0.1)
    gla_norm_w = 1.0 + nrm((L, GLA_VAL_DIM // GLA_HEADS), 0.02)
    w_branch = nrm((L, N_BRANCH, W, D), DEEPNORM_BETA * W ** -0.5)
    w_out = nrm((L, D, D), DEEPNORM_BETA * D ** -0.5)
    post_g = 1.0 + nrm((L, D), 0.02)
    post_b = nrm((L, D), 0.02)
    return {"x": x, "c": c, "ada_w": ada_w, "ada_b": ada_b, "w_in": w_in, "rw_mu": rw_mu,
            "rw_w0": rw_w0, "rw_w_up": rw_w_up, "rw_a0": rw_a0, "rw_a_up": rw_a_up,
            "rw_k_k": rw_k_k, "rw_k_a": rw_k_a, "rw_r_k": rw_r_k, "rw_ln_g": rw_ln_g,
            "rw_ln_b": rw_ln_b, "m2_conv_w": m2_conv_w, "m2_conv_b": m2_conv_b,
            "m2_dt_bias": m2_dt_bias, "m2_a_log": m2_a_log, "m2_d_skip": m2_d_skip,
            "m2_norm_w": m2_norm_w, "gla_gk_up": gla_gk_up, "gla_gk_b": gla_gk_b,
            "gla_norm_w": gla_norm_w, "w_branch": w_branch, "w_out": w_out,
            "post_g": post_g, "post_b": post_b}


def reference(x, c, ada_w, ada_b, w_in, rw_mu, rw_w0, rw_w_up, rw_a0, rw_a_up, rw_k_k, rw_k_a,
              rw_r_k, rw_ln_g, rw_ln_b, m2_conv_w, m2_conv_b, m2_dt_bias, m2_a_log, m2_d_skip,
              m2_norm_w, gla_gk_up, gla_gk_b, gla_norm_w, w_branch, w_out, post_g, post_b):
    for l in range(DEPTH):
        x = _hybrid_layer(x, c, ada_w[l], ada_b[l], w_in[l], rw_mu[l], rw_w0[l], rw_w_up[l],
                          rw_a0[l], rw_a_up[l], rw_k_k[l], rw_k_a[l], rw_r_k[l], rw_ln_g[l],
                          rw_ln_b[l], m2_conv_w[l], m2_conv_b[l], m2_dt_bias[l], m2_a_log[l],
                          m2_d_skip[l], m2_norm_w[l], gla_gk_up[l], gla_gk_b[l], gla_norm_w[l],
                          w_branch[l], w_out[l], post_g[l], post_b[l])
    return x
```

```python
import numpy as np
from contextlib import ExitStack
import concourse.bass as bass
import concourse.mybir as mybir
from concourse.bass_utils import run_bass_kernel_spmd
import ml_dtypes

F32 = mybir.dt.float32
BF16 = mybir.dt.bfloat16
AF = mybir.ActivationFunctionType
ALU = mybir.AluOpType

D = 1024
W = 512
IN_WIDTH = 8344
DEPTH = 2
LN_EPS = 1e-5
RW_GN_EPS = 64e-5
M2_EPS = 1e-5
GLA_EPS = 1e-5
ALPHA = (2.0 * DEPTH) ** 0.25
C0 = float(np.exp(-0.5))
NFM = 16
NCOL_FM = 15 * 128 + 16
NCOL_TM = 772
NCOL_A = NCOL_FM + NCOL_TM
TB = 512
DBG_STOP = 99

CO_ID, CO_MS, CO_MI, CO_LS, CO_ONE, CO_BO, CO_IND, CO_RST = 0, 128, 256, 384, 512, 640, 768, 776
NCONST = CO_RST + 512
PF_MU, PF_W0, PF_A0, PF_KK, PF_KA, PF_RK, PF_CW, PF_CB, PF_GKB, PF_GNW = 0, 7, 9, 11, 13, 15, 17, 33, 37, 38
NPF = 39
PT_LNG, PT_LNB, PT_NW, PT_DS, PT_DTB, PT_ALOG = 0, 256, 512, 768, 1024, 1028
NPT = 1032


def make_consts():
    c = np.zeros((128, NCONST), np.float32)
    i = np.arange(128)
    c[:, CO_ID:CO_ID + 128] = np.eye(128)
    c[:, CO_MS:CO_MS + 128] = (i[:, None] < i[None, :])
    c[:, CO_MI:CO_MI + 128] = (i[:, None] <= i[None, :])
    c[:, CO_LS:CO_LS + 128] = (i[:, None] > i[None, :])
    c[:, CO_ONE:CO_ONE + 128] = 1.0
    c[:, CO_BO:CO_BO + 128] = ((i[:, None] // 64) == (i[None, :] // 64))
    c[:64, CO_IND + 0] = 1.0
    c[64:, CO_IND + 1] = 1.0
    r = np.ones(512, np.float32)
    r[::128] = 0.0
    c[:, CO_RST:CO_RST + 512] = r[None, :]
    return c


class Tk:
    def __init__(self, t, name="", psum=False):
        self.t = t
        self.w = {}
        self.r = {}
        self.name = name
        self.psum = psum

    def __getitem__(self, k):
        return V(self, self.t[k])


class V:
    def __init__(self, tk, ap):
        self.tk = tk
        self.ap = ap

    def __getitem__(self, k):
        return V(self.tk, self.ap[k])

    def re(self, s, **kw):
        return V(self.tk, self.ap.rearrange(s, **kw))

    def bc(self, shape):
        return V(self.tk, self.ap.to_broadcast(shape))

    def un(self, ax):
        return V(self.tk, self.ap.unsqueeze(ax))


def _ap(x):
    return x.ap if isinstance(x, V) else x


class Em:
    NDMA = 24

    def __init__(self, nc, es):
        self.nc = nc
        self.es = es
        self.eng = {"pe": nc.tensor, "act": nc.scalar, "dve": nc.vector, "pool": nc.gpsimd, "sp": nc.sync}
        self.semh = {}
        self.cnt = {}
        self.ep = -1
        self.new_epoch()
        self.dtot = []
        for i in range(self.NDMA):
            self.semh[("d", i)] = es.enter_context(nc.semaphore("s_d%d" % i))
            self.dtot.append(0)
        self.drr = 0
        self.waited = {}
        self.ninst = 0

    def new_epoch(self):
        self.ep += 1
        for e in ("pe", "act", "dve", "pool"):
            k = (e, self.ep)
            self.semh[k] = self.es.enter_context(self.nc.semaphore("s_%s_%d" % (e, self.ep)))
            self.cnt[k] = 0

    def _deps(self, e, Rs, Ws):
        deps = {}
        for v in Rs:
            if isinstance(v, V):
                for k, c in v.tk.w.items():
                    deps[k] = max(deps.get(k, 0), c)
                if v.tk.psum:
                    for k, c in v.tk.r.items():
                        if k[0] != e:
                            deps[k] = max(deps.get(k, 0), c)
        for v in Ws:
            if isinstance(v, V):
                for k, c in v.tk.w.items():
                    deps[k] = max(deps.get(k, 0), c)
                for k, c in v.tk.r.items():
                    deps[k] = max(deps.get(k, 0), c)
        for k, c in deps.items():
            if e == "pe" and k[0] == "pe":
                continue
            if self.waited.get((e, k), 0) >= c:
                continue
            self.eng[e].wait_ge(self.semh[k], c)
            self.waited[(e, k)] = c
            self.ninst += 1

    def op(self, e, fn, Rs, Ws):
        self._deps(e, Rs, Ws)
        inst = fn(self.eng[e])
        k = (e, self.ep)
        self.cnt[k] += 1
        inst.then_inc(self.semh[k], 1)
        c = self.cnt[k]
        for v in Rs:
            if isinstance(v, V):
                v.tk.r[k] = c
        for v in Ws:
            if isinstance(v, V):
                v.tk.w[k] = c
        self.ninst += 1
        return inst

    def dma(self, q, out, in_, **kw):
        self._deps(q, [in_], [out])
        s = self.drr
        self.drr = (self.drr + 1) % self.NDMA
        k = ("d", s)
        if self.waited.get((q, k), 0) < self.dtot[s]:
            self.eng[q].wait_ge(self.semh[k], self.dtot[s])
            self.waited[(q, k)] = self.dtot[s]
        self.eng[q].dma_start(out=_ap(out), in_=_ap(in_), **kw).then_inc(self.semh[k], 16)
        self.dtot[s] += 16
        if isinstance(in_, V):
            in_.tk.r[k] = self.dtot[s]
        if isinstance(out, V):
            out.tk.w[k] = self.dtot[s]
        self.ninst += 1

    def wait_all(self, q, tks):
        for tk in tks:
            for k, c in list(tk.w.items()):
                if self.waited.get((q, k), 0) < c:
                    self.eng[q].wait_ge(self.semh[k], c)
                    self.waited[(q, k)] = c

    def barrier(self):
        for e in ("pe", "act", "dve", "pool", "sp"):
            for k in list(self.semh.keys()):
                c = self.cnt[k] if k in self.cnt else self.dtot[k[1]]
                if c > 0 and self.waited.get((e, k), 0) < c:
                    self.eng[e].wait_ge(self.semh[k], c)
                    self.waited[(e, k)] = c
        self.new_epoch()

    def collectives(self, kind, pairs, groups):
        if not hasattr(self, "ccsem"):
            self.ccsem = self.es.enter_context(self.nc.semaphore("s_cc"))
            self.ccn = 0
        self.barrier()
        for in_ap, out_ap in pairs:
            self.nc.gpsimd.collective_compute(kind, ALU.bypass, replica_groups=groups, ins=[in_ap], outs=[out_ap]).then_inc(self.ccsem, 1)
            self.ccn += 1
        for e in ("pe", "act", "dve", "pool", "sp"):
            self.eng[e].wait_ge(self.ccsem, self.ccn)

    def mm(self, out, lhsT, rhs, start=True, stop=True):
        return self.op("pe", lambda e: e.matmul(out.ap, lhsT.ap, rhs.ap, start=start, stop=stop, skip_group_check=True), [lhsT, rhs], [out])

    def tr(self, out, in_, ident):
        return self.op("pe", lambda e: e.transpose(out.ap, in_.ap, ident.ap), [in_, ident], [out])

    def act(self, out, in_, func, bias=None, scale=1.0, accum=None, eng="act"):
        kw = {}
        if bias is not None:
            kw["bias"] = _ap(bias)
        if accum is not None:
            kw["accum_out"] = accum.ap
        kw["scale"] = _ap(scale)
        Ws = [out] + ([accum] if accum is not None else [])
        return self.op(eng, lambda e: e.activation(out.ap, in_.ap, func, **kw), [in_, bias, scale], Ws)

    def tt(self, eng, out, a, b, op):
        return self.op(eng, lambda e: e.tensor_tensor(out.ap, a.ap, b.ap, op), [a, b], [out])

    def ts(self, eng, out, a, s1, op0, s2=None, op1=None):
        if op1 is None:
            return self.op(eng, lambda e: e.tensor_scalar(out.ap, a.ap, _ap(s1), None, op0), [a, s1], [out])
        return self.op(eng, lambda e: e.tensor_scalar(out.ap, a.ap, _ap(s1), _ap(s2), op0, op1), [a, s1, s2], [out])

    def stt(self, eng, out, a, s, b, op0, op1):
        return self.op(eng, lambda e: e.scalar_tensor_tensor(out.ap, a.ap, _ap(s), b.ap, op0, op1), [a, s, b], [out])

    def copy(self, eng, out, a):
        if eng == "act":
            return self.op(eng, lambda e: e.copy(out.ap, a.ap), [a], [out])
        return self.op(eng, lambda e: e.tensor_copy(out.ap, a.ap), [a], [out])

    def scan(self, out, d0, d1, init, op0, op1):
        return self.op("dve", lambda e: e.tensor_tensor_scan(out.ap, d0.ap, d1.ap, init, op0, op1), [d0, d1], [out])

    def memset(self, eng, out, val):
        return self.op(eng, lambda e: e.memset(out.ap, val), [], [out])

    def recip(self, out, a):
        return self.op("dve", lambda e: e.reciprocal(out.ap, a.ap), [a], [out])

    def bn_stats(self, out, a):
        return self.op("dve", lambda e: e.bn_stats(out.ap, a.ap), [a], [out])

    def bn_aggr(self, out, a):
        return self.op("dve", lambda e: e.bn_aggr(out.ap, a.ap), [a], [out])


class Pool:
    def __init__(self, nc, es, name, n, shape, dtype, psum=False):
        self.free = []
        self.name = name
        _UID[0] += 1
        name = "%s_%d_" % (name, _UID[0])
        for i in range(n):
            if psum:
                t = es.enter_context(nc.psum_tensor("%s%d" % (name, i), shape, dtype))
            else:
                t = es.enter_context(nc.sbuf_tensor("%s%d" % (name, i), shape, dtype))
            self.free.append(Tk(t, "%s%d" % (name, i), psum=psum))
        self.n = n
        self.minfree = n

    def get(self):
        assert self.free, "pool %s exhausted" % self.name
        tk = self.free.pop(0)
        self.minfree = min(self.minfree, len(self.free))
        return tk

    def put(self, *tks):
        for tk in tks:
            assert tk not in self.free
            self.free.append(tk)


_UID = [0]


def sb(nc, es, name, shape, dtype=F32):
    _UID[0] += 1
    return Tk(es.enter_context(nc.sbuf_tensor("sb%d_%s" % (_UID[0], name), shape, dtype)), name)


class Ctx:
    pass


class _Stop(Exception):
    pass


def chk(x):
    if DBG_STOP <= x:
        raise _Stop()


def lastc(t):
    return t[:, 3:515].re("p (c t) -> p c t", c=4)[:, :, 127]


def setup_common(nc, es, em, dr):
    cx = Ctx()
    cx.nc, cx.es, cx.em = nc, es, em
    cx.const = sb(nc, es, "const", [128, NCONST])
    em.dma("sp", cx.const[:, :], dr["consts"])
    cx.identb = sb(nc, es, "identb", [128, 128], BF16)
    em.copy("dve", cx.identb[:, :], cx.const[:, CO_ID:CO_ID + 128])
    cx.ps = Pool(nc, es, "ps", 7, [128, 512], F32, psum=True)
    cx.psb = Tk(es.enter_context(nc.psum_tensor("psb", [128, 1024], BF16)), "psb", psum=True)
    cx.ident = cx.const[:, CO_ID:CO_ID + 128]
    cx.ms = cx.const[:, CO_MS:CO_MS + 128]
    cx.mi = cx.const[:, CO_MI:CO_MI + 128]
    cx.mask2 = cx.const[:, CO_MS:CO_MS + 256]
    cx.ls = cx.const[:, CO_LS:CO_LS + 128]
    cx.ones = cx.const[:, CO_ONE:CO_ONE + 128]
    cx.bones = cx.const[:, CO_BO:CO_BO + 128]
    cx.ind2 = cx.const[:, CO_IND:CO_IND + 2]
    cx.rst = cx.const[:, CO_RST:CO_RST + 512]
    return cx


def emit_ada(cx, dr, L, est):
    nc, es, em = cx.nc, cx.es, cx.em
    cx.sc1 = [sb(nc, es, "sc1fm%d" % l, [128, 8]) for l in range(L)]
    cx.sh = [sb(nc, es, "shfm%d" % l, [128, 8]) for l in range(L)]
    cx.g1 = [sb(nc, es, "g1bc%d" % l, [128, 1024]) for l in range(L)]
    cvec = sb(nc, est, "cvec", [128, 8])
    em.dma("sp", cvec[:, :], dr["cvec"])
    sc = sb(nc, est, "sc", [128, 8])
    em.act(sc[:, :], cvec[:, :], AF.Silu)
    scb = sb(nc, est, "scb", [128, 8, 128])
    em.copy("dve", scb[:, :, :], sc[:, :].un(2).bc([128, 8, 128]))
    adaw = [sb(nc, est, "adaw%d" % i, [128, 3072]) for i in range(2)]
    adab_fm = sb(nc, est, "adabfm", [128, 24])
    adab_bc = sb(nc, est, "adabbc", [128, 1024])
    for l in range(L):
        em.dma("sp", adab_fm[:, :], dr["adab_fm"][l])
        em.dma("sp", adab_bc[:, :], dr["adab_g"][l].partition_broadcast(128))
        psm = cx.ps.get()
        psg0 = cx.ps.get()
        psg1 = cx.ps.get()
        em.memset("dve", psm[:, 0:16], 0.0)
        for kc in range(8):
            aw = adaw[kc % 2]
            em.dma("sp", aw[:, :], dr["ada_w"][l, kc * 128:(kc + 1) * 128, :])
            for j in range(16):
                em.mm(psm[:, j:j + 1], aw[:, j * 128:(j + 1) * 128], sc[:, kc:kc + 1], start=False, stop=(kc == 7))
            em.mm(psg0[:, :], scb[:, kc, :], aw[:, 2048:2560], start=(kc == 0), stop=(kc == 7))
            em.mm(psg1[:, :], scb[:, kc, :], aw[:, 2560:3072], start=(kc == 0), stop=(kc == 7))
        sh, sc1, g1 = cx.sh[l], cx.sc1[l], cx.g1[l]
        em.tt("dve", sh[:, :], psm[:, 0:8], adab_fm[:, 0:8], ALU.add)
        em.tt("dve", sc1[:, :], psm[:, 8:16], adab_fm[:, 8:16], ALU.add)
        em.ts("dve", sc1[:, :], sc1[:, :], 1.0, ALU.add)
        em.tt("dve", g1[:, 0:512], psg0[:, :], adab_bc[:, 0:512], ALU.add)
        em.tt("dve", g1[:, 512:1024], psg1[:, :], adab_bc[:, 512:1024], ALU.add)
        em.ts("dve", g1[:, :], g1[:, :], 1.0, ALU.add)
        cx.ps.put(psm, psg0, psg1)


def emit_ln_transpose(cx, xt, hT, ti, sc1, sh, scr):
    em = cx.em
    st = cx.small.get()
    for hf in range(2):
        em.bn_stats(st[:, hf * 6:(hf + 1) * 6], xt[:, hf * 512:(hf + 1) * 512])
    em.bn_aggr(st[:, 12:14], st[:, 0:12].re("p (a b) -> p a b", a=2))
    em.act(st[:, 14:15], st[:, 13:14], AF.Sqrt, bias=cx.eps_ln[:, 0:1])
    em.recip(st[:, 15:16], st[:, 14:15])
    em.stt("dve", st[:, 16:17], st[:, 12:13], -1.0, st[:, 15:16], ALU.mult, ALU.mult)
    xn = scr
    em.act(xn[:, :], xt[:, :], AF.Identity, bias=st[:, 16:17], scale=st[:, 15:16])
    for kc in range(8):
        em.tr(cx.psb[:, kc * 128:(kc + 1) * 128], xn[:, kc * 128:(kc + 1) * 128], cx.identb[:, :])
    tmp = cx.lntmp
    em.tt("dve", tmp[:, :].re("p (k t) -> p k t", k=8), cx.psb[:, :].re("p (k t) -> p k t", k=8),
          sc1[:, :].un(2).bc([128, 8, 128]), ALU.mult)
    em.tt("pool", hT[:, :, ti * 128:(ti + 1) * 128], tmp[:, :].re("p (k t) -> p k t", k=8),
          sh[:, :].un(2).bc([128, 8, 128]), ALU.add)
    cx.small.put(st)
    return st


def setup_phaseA(cx, S):
    nc, es, em = cx.nc, cx.es, cx.em
    cx.wA = sb(nc, es, "wA", [128, 8, NCOL_A], BF16)
    cx.pfm = sb(nc, es, "pfm", [128, NPF + 8])
    cx.ptm = sb(nc, es, "ptm", [128, NPT + 8])
    cx.smallw = sb(nc, es, "smallw", [128, 384])
    cx.big = Pool(nc, es, "big", 30, [128, 515], F32)
    cx.tm = Pool(nc, es, "tm", 27, [128, 256], F32)
    cx.small = Pool(nc, es, "sm", 8, [128, 32], F32)
    cx.xt = [sb(nc, es, "xt%d" % i, [128, 1024]) for i in range(2)]
    cx.xn = sb(nc, es, "xn", [128, 1024], BF16)
    cx.lntmp = sb(nc, es, "lntmp", [128, 1024])
    cx.hT = sb(nc, es, "hT", [128, 8, TB], BF16)
    cx.AR = [sb(nc, es, "AR%d" % i, [128, 4, 256]) for i in range(2)]
    cx.carry = sb(nc, es, "carry", [128, NFM, 3])
    cx.yT = [sb(nc, es, "yT%d" % i, [128, 6, TB], BF16) for i in range(2)]
    cx.ST = [[sb(nc, es, "ST%d_%d" % (ct, i), [128, 128]) for i in range(2)] for ct in range(2)]
    cx.SM = [sb(nc, es, "SM%d" % i, [128, 256]) for i in range(2)]
    cx.SG = [sb(nc, es, "SG%d" % i, [128, 128]) for i in range(2)]
    cx.eps_ln = sb(nc, es, "epsln", [128, 4])
    em.memset("dve", cx.eps_ln[:, 0:1], LN_EPS)
    em.memset("dve", cx.eps_ln[:, 1:2], RW_GN_EPS)
    em.memset("dve", cx.eps_ln[:, 2:3], M2_EPS)
    em.memset("dve", cx.eps_ln[:, 3:4], GLA_EPS)
    cx.junk = sb(nc, es, "junk", [128, 256])
    cx.QB = sb(nc, es, "QB", [128, 4, 2, 128])
    em.memset("dve", cx.QB[:, :, :, :], 0.0)


def phaseA(cx, dr, l, S, x_src, y_dst):
    try:
        _phaseA(cx, dr, l, S, x_src, y_dst)
    except _Stop:
        cx.em.dma("sp", y_dst(0), cx.yT[0][:, :, :])


def _phaseA(cx, dr, l, S, x_src, y_dst):
    nc, es, em = cx.nc, cx.es, cx.em
    G, PG = cx.big.get, cx.big.put
    T, PT = cx.tm.get, cx.tm.put
    P, PP = cx.ps.get, cx.ps.put
    for kc in range(8):
        em.dma("pool", cx.wA[:, kc, :], dr["wA"][l][kc * 128:(kc + 1) * 128, :])
    em.dma("sp", cx.pfm[:, 0:NPF], dr["pfm"][l])
    em.dma("sp", cx.ptm[:, 0:NPT], dr["ptm"][l].partition_broadcast(128))
    em.dma("sp", cx.smallw[:, :], dr["smallw"][l])
    pf = lambda c: cx.pfm[:, c:c + 1]
    em.ts("dve", cx.pfm[:, NPF:NPF + 2], cx.pfm[:, PF_KA:PF_KA + 2], -1.0, ALU.mult, 1.0, ALU.add)
    em.ts("dve", cx.pfm[:, NPF + 2:NPF + 3], cx.pfm[:, PF_GKB:PF_GKB + 1], -1.0, ALU.mult)
    em.act(cx.ptm[:, NPT:NPT + 4], cx.ptm[:, PT_ALOG:PT_ALOG + 4], AF.Exp)
    em.ts("dve", cx.ptm[:, NPT:NPT + 4], cx.ptm[:, NPT:NPT + 4], -1.0, ALU.mult)
    ah_bc = cx.ptm[:, NPT:NPT + 4]
    chk(0)
    em.memset("dve", cx.carry[:, :, :], 0.0)
    for ct in range(2):
        em.memset("dve", cx.ST[ct][0][:, :], 0.0)
        em.memset("dve", cx.ST[ct][1][:, :], 0.0)
    em.memset("dve", cx.SM[0][:, :], 0.0)
    em.memset("dve", cx.SG[0][:, :], 0.0)
    stp = [0, 0]
    smp = 0
    sgp = 0
    nblk = S // TB
    for blk in range(nblk):
        yT = cx.yT[blk % 2]
        for ti in range(4):
            xt = cx.xt[ti % 2]
            r0 = blk * TB + ti * 128
            em.dma("sp", xt[:, :], x_src(r0))
            emit_ln_transpose(cx, xt, cx.hT, ti, cx.sc1[l], cx.sh[l], cx.xn)
        chk(1)
        raw = []
        for j in range(NFM):
            nco = 128 if j < 15 else 16
            ps = P()
            for kc in range(8):
                em.mm(ps[0:nco, :], cx.wA[:, kc, j * 128:j * 128 + nco], cx.hT[:, kc, :], start=(kc == 0), stop=(kc == 7))
            g = G()
            need_hist = j <= 10
            if need_hist:
                em.copy("pool", g[0:nco, 0:3], cx.carry[0:nco, j, :])
            em.copy("act", g[0:nco, 3:515], ps[0:nco, :])
            if need_hist:
                em.copy("pool", cx.carry[0:nco, j, :], g[0:nco, 512:515])
            PP(ps)
            raw.append(g)
        rwg, zs, vg, dtr = [], [], [], []
        for ti in range(4):
            ps1, ps2 = P(), P()
            for kc in range(8):
                em.mm(ps1[:, :], cx.hT[:, kc, ti * 128:(ti + 1) * 128], cx.wA[:, kc, NCOL_FM:NCOL_FM + 512], start=(kc == 0), stop=(kc == 7))
            for kc in range(8):
                em.mm(ps2[:, 0:260], cx.hT[:, kc, ti * 128:(ti + 1) * 128], cx.wA[:, kc, NCOL_FM + 512:NCOL_FM + 772], start=(kc == 0), stop=(kc == 7))
            a, b_, c_, d_ = T(), T(), T(), cx.small.get()
            em.act(a[:, :], ps1[:, 0:256], AF.Silu)
            em.act(b_[:, :], ps1[:, 256:512], AF.Silu)
            em.copy("dve", c_[:, :], ps2[:, 0:256])
            em.tt("dve", d_[:, 0:4], ps2[:, 256:260], cx.ptm[:, PT_DTB:PT_DTB + 4], ALU.add)
            PP(ps1, ps2)
            rwg.append(a); zs.append(b_); vg.append(c_); dtr.append(d_)
        chk(2)
        wd = raw[6]
        wdl = G()
        dtmp = G()
        em.tt("dve", dtmp[:, 3:515], wd[:, 2:514], wd[:, 3:515], ALU.subtract)
        em.stt("dve", wdl[:, 3:515], dtmp[:, 3:515], pf(PF_MU + 6), wd[:, 3:515], ALU.mult, ALU.add)
        em.act(wdl[0:64, 3:515], wdl[0:64, 3:515], AF.Tanh)
        PG(dtmp, wd)
        chk(2.01)
        for ct in range(2):
            def lerp(rawt, mucol):
                d_ = G()
                o = G()
                em.tt("dve", d_[:, 3:515], rawt[:, 2:514], rawt[:, 3:515], ALU.subtract)
                em.stt("dve", o[:, 3:515], d_[:, 3:515], pf(mucol), rawt[:, 3:515], ALU.mult, ALU.add)
                PG(d_, rawt)
                return o
            r = lerp(raw[0 + ct], PF_MU + 0 + ct)
            k = lerp(raw[2 + ct], PF_MU + 2 + ct)
            v = lerp(raw[4 + ct], PF_MU + 4 + ct)
            chk(2.02)
            D3 = slice(3, 515)
            AR = cx.AR[ct]
            ps = P()
            em.mm(ps[:, :], cx.smallw[0:64, ct * 128:(ct + 1) * 128], wdl[0:64, D3])
            sg = G()
            em.act(sg[:, D3], ps[:, :], AF.Sigmoid, bias=pf(PF_W0 + ct))
            PP(ps)
            ps = P()
            em.mm(ps[:, :], cx.smallw[64:128, ct * 128:(ct + 1) * 128], wdl[64:128, D3])
            alr = G()
            em.act(alr[:, D3], ps[:, :], AF.Sigmoid, bias=pf(PF_A0 + ct))
            PP(ps)
            chk(2.03)
            CS = G()
            em.scan(CS[:, D3], cx.rst, sg[:, D3], 0.0, ALU.mult, ALU.add)
            CSx = G()
            em.tt("pool", CSx[:, D3], CS[:, D3], sg[:, D3], ALU.subtract)
            PG(sg)
            eL, enL, eLx, eD = G(), G(), G(), G()
            em.act(eL[:, D3], CS[:, D3], AF.Exp, scale=-C0)
            em.act(enL[:, D3], CS[:, D3], AF.Exp, scale=C0)
            em.act(eLx[:, D3], CSx[:, D3], AF.Exp, scale=-C0)
            PG(CSx)
            chk(2.04)
            sm = cx.small.get()
            em.ts("dve", sm[:, 0:4], lastc(CS), -C0, ALU.mult)
            for ci in range(4):
                cs3 = slice(3 + ci * 128, 3 + (ci + 1) * 128)
                em.act(eD[:, cs3], CS[:, cs3], AF.Exp, bias=sm[:, ci:ci + 1], scale=C0)
            em.copy("dve", sm[:, 4:8], lastc(eL))
            PCt = sm
            PG(CS)
            chk(2.05)
            kk = G()
            em.ts("dve", kk[:, D3], k[:, D3], pf(PF_KK + ct), ALU.mult)
            kk2 = G()
            em.act(kk2[:, D3], kk[:, D3], AF.Square)
            ps = P()
            em.mm(ps[:, :], cx.bones, kk2[:, D3])
            em.act(kk2[:, D3], ps[:, :], AF.Sqrt)
            PP(ps)
            em.ts("dve", kk2[:, D3], kk2[:, D3], 1e-12, ALU.max)
            em.recip(kk2[:, D3], kk2[:, D3])
            em.tt("dve", kk[:, D3], kk[:, D3], kk2[:, D3], ALU.mult)
            PG(kk2)
            chk(2.06)
            f = G()
            em.ts("pool", f[:, D3], alr[:, D3], pf(PF_KA + ct), ALU.mult, cx.pfm[:, NPF + ct:NPF + ct + 1], ALU.add)
            kp = G()
            em.tt("pool", kp[:, D3], k[:, D3], f[:, D3], ALU.mult)
            PG(f, k)
            bv = G()
            em.tt("dve", bv[:, D3], kk[:, D3], alr[:, D3], ALU.mult)
            PG(alr)
            chk(2.07)
            ARv = AR[:, :, :]
            em.stt("dve", AR[:, :, 0:128], kk[:, D3].re("p (c t) -> p c t", c=4), -1.0,
                   eLx[:, D3].re("p (c t) -> p c t", c=4), ALU.mult, ALU.mult)
            em.tt("pool", AR[:, :, 128:256], r[:, D3].re("p (c t) -> p c t", c=4), eL[:, D3].re("p (c t) -> p c t", c=4), ALU.mult)
            PG(kk, eLx, eL)
            chk(2.08)
            bt, kt, bd, kd = G(), G(), G(), G()
            em.tt("dve", bt[:, D3], bv[:, D3], enL[:, D3], ALU.mult)
            em.tt("pool", kt[:, D3], kp[:, D3], enL[:, D3], ALU.mult)
            em.tt("dve", bd[:, D3], bv[:, D3], eD[:, D3], ALU.mult)
            em.tt("pool", kd[:, D3], kp[:, D3], eD[:, D3], ALU.mult)
            PG(bv, enL, eD)
            rk = G()
            em.stt("dve", rk[:, D3], r[:, D3], pf(PF_RK + ct), kp[:, D3], ALU.mult, ALU.mult)
            PG(r, kp)
            chk(2.1)
            for ci in range(4):
                cs3 = slice(3 + ci * 128, 3 + (ci + 1) * 128)
                cs = slice(ci * 128, (ci + 1) * 128)
                ST = cx.ST[ct][stp[ct]]
                STn = cx.ST[ct][1 - stp[ct]]
                stp[ct] = 1 - stp[ct]
                psT = P()
                em.tr(psT[:, 0:128], v[:, cs3], cx.ident)
                chk(2.11)
                em.tr(psT[:, 128:256], bd[:, cs3], cx.ident)
                em.tr(psT[:, 256:384], kd[:, cs3], cx.ident)
                em.tr(psT[:, 384:512], AR[:, ci, 0:128], cx.ident)
                chk(2.12)
                vtm, bdkd, Z = T(), T(), T()
                Z4 = Z[:, :].re("p (h w c) -> p h w c", h=2, w=2)
                em.copy("act", vtm[:, 0:128], psT[:, 0:128])
                chk(2.13)
                em.copy("act", bdkd[:, 0:256], psT[:, 128:384])
                chk(2.14)
                em.copy("act", Z4[:, :, 0, :], psT[:, 384:512].re("p (h c) -> p h c", h=2))
                PP(psT)
                chk(2.2)
                M1s, M2s = [], []
                for h in range(2):
                    pb = slice(64 * h, 64 * h + 64)
                    psA = P()
                    em.mm(psA[:, 0:256], bt[pb, cs3], AR[pb, ci, :])
                    em.mm(psA[:, 256:512], kt[pb, cs3], AR[pb, ci, :])
                    M1, M2_ = T(), T()
                    em.tt("dve", M1[:, :], psA[:, 0:256], cx.mask2, ALU.mult)
                    em.tt("dve", M2_[:, :], psA[:, 256:512], cx.mask2, ALU.mult)
                    PP(psA)
                    psQ = P()
                    em.mm(psQ[:, 0:128], AR[pb, ci, 0:128], bt[pb, cs3])
                    em.mm(psQ[:, 128:192], M2_[:, 0:128], vtm[:, h * 64:(h + 1) * 64])
                    Q1 = T()
                    em.tt("dve", Q1[:, 0:128], psQ[:, 0:128], cx.ls, ALU.mult)
                    em.copy("act", Z4[:, h, 1, :], psQ[:, 128:192])
                    PP(psQ)
                    chk(2.3)
                    Pk, Qk = M1[:, 0:128], Q1[:, 0:128]
                    Zh = Z[:, h * 128:(h + 1) * 128]
                    prevPQ = Q1
                    for lev in range(7):
                        psZ = P()
                        em.mm(psZ[:, 0:128], Pk, Zh)
                        em.tt("dve", Zh, psZ[:, 0:128], Zh, ALU.add)
                        PP(psZ)
                        if lev < 6:
                            psS = P()
                            em.mm(psS[:, 0:128], Qk, Pk)
                            PQ = T()
                            if lev < 5:
                                em.mm(psS[:, 128:256], Pk, Qk)
                                em.copy("act", PQ[:, 0:256], psS[:, 0:256])
                            else:
                                em.copy("act", PQ[:, 0:128], psS[:, 0:128])
                            PP(psS)
                            PT(prevPQ)
                            prevPQ = PQ
                            Pk, Qk = PQ[:, 0:128], PQ[:, 128:256]
                    PT(prevPQ)
                    M1s.append(M1); M2s.append(M2_)
                chk(2.4)
                psX = P()
                em.tr(psX[:, 0:128], Z[:, 0:128], cx.ident)
                em.tr(psX[:, 128:256], Z[:, 64:192], cx.ident)
                ApT = T()
                em.copy("act", ApT[0:64, 0:128], psX[0:64, 0:128])
                em.copy("act", ApT[64:128, 0:128], psX[64:128, 128:256])
                PP(psX)
                chk(2.41)
                psU = P()
                em.mm(psU[:, 0:128], ApT[:, 0:128], ST[:, :])
                chk(2.42)
                U2 = T()
                em.tt("dve", U2[:, 0:128].re("p (h c) -> p h c", h=2), psU[:, 0:128].re("p (h c) -> p h c", h=2), Z4[:, :, 1, :], ALU.add)
                PP(psU)
                PT(ApT, Z)
                chk(2.5)
                psY = P()
                em.mm(psY[:, 0:128], AR[:, ci, 128:256], ST[:, :], start=True, stop=False)
                for h in range(2):
                    hs = slice(h * 64, (h + 1) * 64)
                    em.mm(psY[:, hs], M1s[h][:, 128:256], U2[:, hs], start=False, stop=False)
                    em.mm(psY[:, hs], M2s[h][:, 128:256], vtm[:, hs], start=False, stop=True)
                psS = P()
                em.mm(psS[:, 0:128], bdkd[:, 0:128], U2[:, 0:128], start=True, stop=False)
                em.mm(psS[:, 0:128], bdkd[:, 128:256], vtm[:, 0:128], start=False, stop=True)
                for h in range(2):
                    pb = slice(64 * h, 64 * h + 64)
                    hs = slice(h * 64, (h + 1) * 64)
                    em.stt("dve", STn[pb, hs], ST[pb, hs], PCt[pb, 4 + ci:5 + ci], psS[pb, hs], ALU.mult, ALU.add)
                PP(psS)
                PT(M1s[0], M1s[1], M2s[0], M2s[1], bdkd, U2)
                chk(2.6)
                st = cx.small.get()
                for h in range(2):
                    em.bn_stats(st[:, h * 6:(h + 1) * 6], psY[:, h * 64:(h + 1) * 64])
                for h in range(2):
                    em.bn_aggr(st[:, 12 + 2 * h:14 + 2 * h], st[:, h * 6:(h + 1) * 6])
                mv = st[:, 12:16].re("p (h two) -> p h two", h=2)
                em.act(st[:, 16:18], mv[:, :, 1], AF.Sqrt, bias=cx.eps_ln[:, 1:2])
                em.recip(st[:, 18:20], st[:, 16:18])
                yn = T()
                yn3 = yn[:, 0:128].re("p (h c) -> p h c", h=2)
                em.tt("dve", yn3, psY[:, 0:128].re("p (h c) -> p h c", h=2), mv[:, :, 0:1].bc([128, 2, 64]), ALU.subtract)
                PP(psY)
                em.tt("dve", yn3, yn3, st[:, 18:20].un(2).bc([128, 2, 64]), ALU.mult)
                em.tt("pool", yn[:, 0:128], yn[:, 0:128], cx.ptm[:, PT_LNG + ct * 128:PT_LNG + (ct + 1) * 128], ALU.mult)
                em.tt("pool", yn[:, 0:128], yn[:, 0:128], cx.ptm[:, PT_LNB + ct * 128:PT_LNB + (ct + 1) * 128], ALU.add)
                chk(2.7)
                psB = P()
                em.mm(psB[:, 0:2], rk[:, cs3], cx.ind2)
                em.copy("act", st[:, 20:22], psB[:, 0:2])
                PP(psB)
                bo = T()
                em.tt("dve", bo[:, 0:128].re("p (h c) -> p h c", h=2), vtm[:, 0:128].re("p (h c) -> p h c", h=2),
                      st[:, 20:22].un(2).bc([128, 2, 64]), ALU.mult)
                em.tt("dve", yn[:, 0:128], yn[:, 0:128], bo[:, 0:128], ALU.add)
                em.tt("dve", yn[:, 0:128], yn[:, 0:128], rwg[ci][:, ct * 128:(ct + 1) * 128], ALU.mult)
                psT2 = P()
                em.tr(psT2[:, 0:128], yn[:, 0:128], cx.ident)
                em.copy("act", yT[:, ct, cs], psT2[:, 0:128])
                PP(psT2)
                PT(yn, bo, vtm)
                cx.small.put(st)
            cx.small.put(PCt)
            PG(v, bt, kt, bd, kd, rk)
        PG(wdl)
        for ti in range(4):
            PT(rwg[ti])
        chk(3)
        conv = []
        for q_ in range(4):
            rt = raw[7 + q_]
            acc = G()
            cw = lambda j: pf(PF_CW + q_ * 4 + j)
            em.ts("pool", acc[:, 3:515], rt[:, 0:512], cw(0), ALU.mult, pf(PF_CB + q_), ALU.add)
            em.stt("dve", acc[:, 3:515], rt[:, 1:513], cw(1), acc[:, 3:515], ALU.mult, ALU.add)
            em.stt("dve", acc[:, 3:515], rt[:, 2:514], cw(2), acc[:, 3:515], ALU.mult, ALU.add)
            em.stt("dve", acc[:, 3:515], rt[:, 3:515], cw(3), acc[:, 3:515], ALU.mult, ALU.add)
            em.act(acc[:, 3:515], acc[:, 3:515], AF.Silu)
            PG(rt)
            conv.append(acc)
        xs0, xs1, Bc, Cc = conv
        for ci in range(4):
            cs3 = slice(3 + ci * 128, 3 + (ci + 1) * 128)
            cs = slice(ci * 128, (ci + 1) * 128)
            SM = cx.SM[smp]
            SMn = cx.SM[1 - smp]
            smp = 1 - smp
            psT = P()
            em.tr(psT[:, 0:128], xs0[:, cs3], cx.ident)
            em.tr(psT[:, 128:256], xs1[:, cs3], cx.ident)
            em.tr(psT[:, 256:384], Bc[:, cs3], cx.ident)
            xtm, btm = T(), T()
            em.copy("act", xtm[:, :], psT[:, 0:256])
            em.copy("dve", btm[:, 0:128], psT[:, 256:384])
            PP(psT)
            sm = dtr[ci]
            em.act(sm[:, 4:8], sm[:, 0:4], AF.Exp)
            em.act(sm[:, 8:12], sm[:, 4:8], AF.Ln, bias=1.0)
            dt = sm[:, 8:12]
            em.tt("dve", sm[:, 12:16], dt, ah_bc, ALU.mult)
            da = sm[:, 12:16]
            R = G()
            em.tt("dve", R[:, 0:512].re("p (h t) -> p h t", h=4), cx.mi.un(1).bc([128, 4, 128]), da.un(2).bc([128, 4, 128]), ALU.mult)
            psD = P()
            em.mm(psD[:, :], cx.ls, R[:, 0:512])
            Lm = G()
            em.act(Lm[:, 0:512], psD[:, :], AF.Exp)
            PP(psD)
            PG(R)
            psm = P()
            em.mm(psm[:, 0:4], cx.mi, da)
            em.mm(psm[:, 4:8], cx.ls, da)
            em.mm(psm[:, 8:12], cx.ones, da)
            em.act(sm[:, 16:28], psm[:, 0:12], AF.Exp)
            ex = sm[:, 16:28]
            PP(psm)
            psC = P()
            em.mm(psC[:, 0:128], Bc[:, cs3], Cc[:, cs3])
            CBm = T()
            em.tt("dve", CBm[:, 0:128], psC[:, 0:128], cx.mi, ALU.mult)
            PP(psC)
            Mt = G()
            em.tt("dve", Mt[:, 0:512].re("p (h t) -> p h t", h=4), Lm[:, 0:512].re("p (h t) -> p h t", h=4),
                  CBm[:, 0:128].un(1).bc([128, 4, 128]), ALU.mult)
            PG(Lm)
            PT(CBm)
            xdt = T()
            h4 = lambda v_: v_.re("p (h c) -> p h c", h=4)
            em.tt("pool", h4(xdt[:, :]), h4(xtm[:, :]), dt.un(2).bc([128, 4, 64]), ALU.mult)
            psY = P()
            for h in range(4):
                em.mm(psY[:, h * 64:(h + 1) * 64], Mt[:, h * 128:(h + 1) * 128], xdt[:, h * 64:(h + 1) * 64])
            psO = P()
            em.mm(psO[:, 0:256], Cc[:, cs3], SM[:, :])
            t1, t2, y = T(), T(), T()
            em.tt("dve", h4(t1[:, :]), h4(psO[:, 0:256]), ex[:, 0:4].un(2).bc([128, 4, 64]), ALU.mult)
            PP(psO)
            em.tt("pool", t2[:, :], xtm[:, :], cx.ptm[:, PT_DS:PT_DS + 256], ALU.mult)
            em.tt("pool", t2[:, :], t2[:, :], t1[:, :], ALU.add)
            em.tt("dve", y[:, :], psY[:, 0:256], t2[:, :], ALU.add)
            PP(psY)
            PG(Mt)
            xdd = t1
            em.tt("pool", h4(xdd[:, :]), h4(xdt[:, :]), ex[:, 4:8].un(2).bc([128, 4, 64]), ALU.mult)
            psS = P()
            em.mm(psS[:, 0:256], btm[:, 0:128], xdd[:, :])
            em.tt("pool", h4(t2[:, :]), h4(SM[:, :]), ex[:, 8:12].un(2).bc([128, 4, 64]), ALU.mult)
            em.tt("dve", SMn[:, :], t2[:, :], psS[:, 0:256], ALU.add)
            PP(psS)
            PT(xdt, btm)
            em.tt("dve", y[:, :], y[:, :], zs[ci][:, :], ALU.mult)
            em.act(cx.junk[:, :], y[:, :], AF.Square, accum=sm[:, 28:29])
            em.act(sm[:, 29:30], sm[:, 28:29], AF.Sqrt, bias=cx.eps_ln[:, 2:3], scale=1.0 / 256.0)
            em.recip(sm[:, 30:31], sm[:, 29:30])
            em.stt("dve", y[:, :], y[:, :], sm[:, 30:31], cx.ptm[:, PT_NW:PT_NW + 256], ALU.mult, ALU.mult)
            psT2 = P()
            em.tr(psT2[:, 0:128], y[:, 0:128], cx.ident)
            em.tr(psT2[:, 128:256], y[:, 128:256], cx.ident)
            em.copy("act", yT[:, 2:4, cs], psT2[:, 0:256].re("p (a t) -> p a t", a=2))
            PP(psT2)
            PT(t1, t2, y, xtm, zs[ci])
            cx.small.put(sm)
        PG(xs0, xs1, Bc, Cc)
        chk(4)
        qr, kr, g0, g1_, gkr = raw[11], raw[12], raw[13], raw[14], raw[15]
        D3 = slice(3, 515)
        ps = P()
        em.mm(ps[:, :], cx.smallw[0:16, 256:384], gkr[0:16, D3])
        e_ = G()
        em.act(e_[:, D3], ps[:, :], AF.Exp, bias=cx.pfm[:, NPF + 2:NPF + 3], scale=-1.0)
        PP(ps)
        em.act(e_[:, D3], e_[:, D3], AF.Ln, bias=1.0)
        chk(4.1)
        Gc = G()
        em.scan(Gc[:, D3], cx.rst, e_[:, D3], 0.0, ALU.mult, ALU.add)
        PG(e_, gkr)
        eg, eng_ = G(), G()
        em.act(eg[:, D3], Gc[:, D3], AF.Exp, scale=-1.0 / 16.0)
        em.act(eng_[:, D3], Gc[:, D3], AF.Exp, scale=1.0 / 16.0)
        kg, kdT = G(), G()
        QB = cx.QB
        for h in range(2):
            pb = slice(64 * h, 64 * h + 64)
            em.stt("dve", QB[pb, :, h, :], qr[pb, D3].re("p (c t) -> p c t", c=4), 0.125,
                   eg[pb, D3].re("p (c t) -> p c t", c=4), ALU.mult, ALU.mult)
        em.tt("pool", kg[:, D3], kr[:, D3], eng_[:, D3], ALU.mult)
        smg = cx.small.get()
        em.ts("dve", smg[:, 0:4], lastc(Gc), -1.0 / 16.0, ALU.mult)
        for ci in range(4):
            cs3 = slice(3 + ci * 128, 3 + (ci + 1) * 128)
            em.act(kdT[:, cs3], Gc[:, cs3], AF.Exp, bias=smg[:, ci:ci + 1], scale=1.0 / 16.0)
        em.tt("pool", kdT[:, D3], kdT[:, D3], kr[:, D3], ALU.mult)
        em.copy("dve", smg[:, 4:8], lastc(eg))
        PG(Gc, eg, eng_, qr, kr)
        sgs = []
        for gt in (g0, g1_):
            em.act(gt[:, D3], gt[:, D3], AF.Silu)
            em.ts("dve", gt[:, D3], gt[:, D3], pf(PF_GNW), ALU.mult)
            sgs.append(gt)
        chk(4.2)
        for ci in range(4):
            cs3 = slice(3 + ci * 128, 3 + (ci + 1) * 128)
            cs = slice(ci * 128, (ci + 1) * 128)
            SG = cx.SG[sgp]
            SGn = cx.SG[1 - sgp]
            sgp = 1 - sgp
            psT = P()
            em.tr(psT[:, 0:128], kdT[:, cs3], cx.ident)
            kd_ = T()
            em.copy("act", kd_[:, 0:128], psT[:, 0:128])
            PP(psT)
            chk(4.3)
            psA = P()
            em.mm(psA[:, 0:256], kg[:, cs3], QB[:, ci, :, :].re("p h t -> p (h t)"))
            attT = T()
            em.tt("dve", attT[:, :].re("p (h t) -> p h t", h=2), psA[:, 0:256].re("p (h t) -> p h t", h=2),
                  cx.mi.un(1).bc([128, 2, 128]), ALU.mult)
            PP(psA)
            chk(4.4)
            psO = P()
            for h in range(2):
                pb = slice(64 * h, 64 * h + 64)
                hs = slice(h * 128, (h + 1) * 128)
                em.mm(psO[:, hs], vg[ci][:, hs], attT[:, hs], start=True, stop=False)
                em.mm(psO[:, hs], SG[:, :], QB[:, ci, h, :], start=False, stop=True)
            chk(4.5)
            sq = T()
            em.act(sq[:, :], psO[:, 0:256], AF.Square)
            psN = P()
            em.mm(psN[:, 0:256], cx.ones, sq[:, :])
            em.act(sq[:, :], psN[:, 0:256], AF.Sqrt, bias=cx.eps_ln[:, 3:4], scale=1.0 / 128.0)
            PP(psN)
            em.recip(sq[:, :], sq[:, :])
            chk(4.6)
            o = T()
            em.tt("dve", o[:, :], psO[:, 0:256], sq[:, :], ALU.mult)
            PP(psO)
            for h in range(2):
                em.tt("pool", yT[:, 4 + h, cs], o[:, h * 128:(h + 1) * 128], sgs[h][:, cs3], ALU.mult)
            chk(4.7)
            psS = P()
            em.mm(psS[:, 0:256], kd_[:, 0:128], vg[ci][:, :])
            for h in range(2):
                pb = slice(64 * h, 64 * h + 64)
                em.stt("dve", SGn[pb, :], SG[pb, :], smg[pb, 4 + ci:5 + ci], psS[pb, h * 128:(h + 1) * 128], ALU.mult, ALU.add)
            PP(psS)
            PT(kd_, attT, sq, o, vg[ci])
        cx.small.put(smg)
        PG(kg, kdT, g0, g1_)
        em.dma("sp", y_dst(blk), yT[:, :, :])


def pack_phaseA(inp, l, hh):
    w_in = inp["w_in"][l]
    cols = []
    o_rw = 0
    for base in (0, 512, 1024):
        cols.append(np.arange(base + 256 * hh, base + 256 * hh + 256))
    cols.append(np.arange(1536, 1664))
    o_xbc = 2176
    cols.append(np.arange(o_xbc + 256 * hh, o_xbc + 256 * hh + 256))
    cols.append(np.arange(o_xbc + 512 + 128 * hh, o_xbc + 512 + 128 * hh + 128))
    cols.append(np.arange(o_xbc + 768 + 128 * hh, o_xbc + 768 + 128 * hh + 128))
    cols.append(np.arange(3720 + 128 * hh, 3720 + 128 * hh + 128))
    cols.append(np.arange(3976 + 128 * hh, 3976 + 128 * hh + 128))
    cols.append(np.arange(4760 + 256 * hh, 4760 + 256 * hh + 256))
    cols.append(np.arange(4744, 4760))
    cols.append(np.arange(1664 + 256 * hh, 1664 + 256 * hh + 256))
    cols.append(np.arange(3208 + 256 * hh, 3208 + 256 * hh + 256))
    cols.append(np.arange(4232 + 256 * hh, 4232 + 256 * hh + 256))
    cols.append(np.arange(3200 + 4 * hh, 3200 + 4 * hh + 4))
    cols = np.concatenate(cols)
    assert cols.shape[0] == NCOL_A
    wA = np.ascontiguousarray(w_in[:, cols])
    pfm = np.zeros((128, NPF), np.float32)
    mu = inp["rw_mu"][l]
    ch = 256 * hh
    for i in range(2):
        sl = slice(ch + 128 * i, ch + 128 * i + 128)
        pfm[:, PF_MU + 0 + i] = mu[0:512][sl]
        pfm[:, PF_MU + 2 + i] = mu[512:1024][sl]
        pfm[:, PF_MU + 4 + i] = mu[1024:1536][sl]
        pfm[:, PF_W0 + i] = inp["rw_w0"][l][sl]
        pfm[:, PF_A0 + i] = inp["rw_a0"][l][sl]
        pfm[:, PF_KK + i] = inp["rw_k_k"][l][sl]
        pfm[:, PF_KA + i] = inp["rw_k_a"][l][sl]
        pfm[:, PF_RK + i] = inp["rw_r_k"][l].reshape(512)[sl]
    pfm[:, PF_MU + 6] = mu[1536:1664]
    cw = inp["m2_conv_w"][l]
    cb = inp["m2_conv_b"][l]
    xidx = [np.arange(ch, ch + 128), np.arange(ch + 128, ch + 256),
            np.arange(512 + 128 * hh, 512 + 128 * hh + 128), np.arange(768 + 128 * hh, 768 + 128 * hh + 128)]
    for q in range(4):
        for j in range(4):
            pfm[:, PF_CW + q * 4 + j] = cw[j, xidx[q]]
        pfm[:, PF_CB + q] = cb[xidx[q]]
    pfm[:, PF_GKB] = inp["gla_gk_b"][l][128 * hh:128 * hh + 128]
    pfm[:, PF_GNW] = inp["gla_norm_w"][l]
    ptm = np.zeros((1, NPT), np.float32)
    ptm[0, PT_LNG:PT_LNG + 256] = inp["rw_ln_g"][l][ch:ch + 256]
    ptm[0, PT_LNB:PT_LNB + 256] = inp["rw_ln_b"][l][ch:ch + 256]
    ptm[0, PT_NW:PT_NW + 256] = inp["m2_norm_w"][l][ch:ch + 256]
    ptm[0, PT_DS:PT_DS + 256] = np.repeat(inp["m2_d_skip"][l][4 * hh:4 * hh + 4], 64)
    ptm[0, PT_DTB:PT_DTB + 4] = inp["m2_dt_bias"][l][4 * hh:4 * hh + 4]
    ptm[0, PT_ALOG:PT_ALOG + 4] = inp["m2_a_log"][l][4 * hh:4 * hh + 4]
    smallw = np.zeros((128, 384), np.float32)
    smallw[0:64, 0:256] = inp["rw_w_up"][l][:, ch:ch + 256]
    smallw[64:128, 0:256] = inp["rw_a_up"][l][:, ch:ch + 256]
    smallw[0:16, 256:384] = inp["gla_gk_up"][l][:, 128 * hh:128 * hh + 128]
    return wA, pfm, ptm, smallw


def pack_ada(inp, b, L):
    cvec = np.ascontiguousarray(inp["c"][b].reshape(8, 128).T)
    adab_fm = np.stack([np.ascontiguousarray(inp["ada_b"][l][0:2048].reshape(16, 128).T) for l in range(L)])
    adab_fm = np.concatenate([adab_fm, np.zeros((L, 128, 8), np.float32)], axis=2)
    adab_g = np.stack([inp["ada_b"][l][2048:3072].reshape(1, 1024) for l in range(L)])
    return cvec, adab_fm, adab_g


def setup_phaseB(cx, S):
    nc, es, em = cx.nc, cx.es, cx.em
    cx.wM = sb(nc, es, "wM", [128, 8, 3072], BF16)
    cx.wbr = sb(nc, es, "wbr", [128, 12, 1024], BF16)
    cx.wout = sb(nc, es, "wout", [128, 8, 1024], BF16)
    cx.pbt = sb(nc, es, "pbt", [128, 2048])
    cx.small = Pool(nc, es, "smB", 8, [128, 32], F32)
    cx.xtB = [sb(nc, es, "xtB%d" % i, [128, 1024]) for i in range(4)]
    cx.xn = sb(nc, es, "xnB", [128, 1024], BF16)
    cx.lntmp = sb(nc, es, "lntmpB", [128, 1024])
    cx.hT = sb(nc, es, "hTB", [128, 8, TB], BF16)
    cx.yb = sb(nc, es, "yb", [128, 12, TB], BF16)
    cx.mT = sb(nc, es, "mT", [128, 8, TB], BF16)
    cx.wk = Pool(nc, es, "wk", 6, [128, 512], F32)
    cx.res = [sb(nc, es, "res%d" % i, [128, 1024]) for i in range(2)]
    cx.eps_ln = sb(nc, es, "epslnB", [128, 4])
    em.memset("dve", cx.eps_ln[:, 0:1], LN_EPS)


def phaseB(cx, dr, l, S, x_src, y_src, out_dst):
    nc, es, em = cx.nc, cx.es, cx.em
    P, PP = cx.ps.get, cx.ps.put
    for kc in range(8):
        em.dma("pool", cx.wM[:, kc, :], dr["wM"][l][kc * 128:(kc + 1) * 128, :])
        em.dma("pool", cx.wout[:, kc, :], dr["wout"][l][kc * 128:(kc + 1) * 128, :])
    for kt in range(12):
        em.dma("pool", cx.wbr[:, kt, :], dr["wbr"][l][kt * 128:(kt + 1) * 128, :])
    em.dma("sp", cx.pbt[:, :], dr["pbt"][l].partition_broadcast(128))
    g1 = cx.g1[l]
    nblk = S // TB
    for blk in range(nblk):
        for ti in range(4):
            r0 = blk * TB + ti * 128
            em.dma("sp", cx.xtB[ti][:, :], x_src(r0))
            emit_ln_transpose(cx, cx.xtB[ti], cx.hT, ti, cx.sc1[l], cx.sh[l], cx.xn)
        em.dma("sp", cx.yb[:, :, :], y_src(blk * TB))
        for f in range(8):
            fs = slice(f * 128, (f + 1) * 128)
            macc = cx.wk.get()
            for br in range(3):
                psL = P()
                for kc in range(8):
                    em.mm(psL[:, :], cx.wM[:, kc, br * 1024 + f * 128:br * 1024 + (f + 1) * 128], cx.hT[:, kc, :],
                          start=(kc == 0), stop=(kc == 7))
                g = cx.wk.get()
                em.act(g[:, :], psL[:, :], AF.Sigmoid)
                PP(psL)
                psY = P()
                kts = [br * 2, br * 2 + 1, 6 + br * 2, 6 + br * 2 + 1]
                for i, kt in enumerate(kts):
                    em.mm(psY[:, :], cx.wbr[:, kt, fs], cx.yb[:, kt, :], start=(i == 0), stop=(i == 3))
                if br == 0:
                    em.tt("dve", macc[:, :], psY[:, :], g[:, :], ALU.mult)
                elif br == 1:
                    em.tt("dve", g[:, :], psY[:, :], g[:, :], ALU.mult)
                    em.tt("pool", macc[:, :], macc[:, :], g[:, :], ALU.add)
                else:
                    em.tt("dve", g[:, :], psY[:, :], g[:, :], ALU.mult)
                    em.tt("pool", cx.mT[:, f, :], macc[:, :], g[:, :], ALU.add)
                PP(psY)
                cx.wk.put(g)
            cx.wk.put(macc)
        for ti in range(4):
            res = cx.res[ti % 2]
            xt = cx.xtB[ti]
            for half in range(2):
                hs = slice(half * 512, (half + 1) * 512)
                psR = P()
                for kc in range(8):
                    em.mm(psR[:, :], cx.mT[:, kc, ti * 128:(ti + 1) * 128], cx.wout[:, kc, hs], start=(kc == 0), stop=(kc == 7))
                em.tt("dve", res[:, hs], psR[:, :], g1[:, hs], ALU.mult)
                PP(psR)
            em.stt("dve", res[:, :], xt[:, :], ALPHA, res[:, :], ALU.mult, ALU.add)
            st = cx.small.get()
            for hf in range(2):
                em.bn_stats(st[:, hf * 6:(hf + 1) * 6], res[:, hf * 512:(hf + 1) * 512])
            em.bn_aggr(st[:, 12:14], st[:, 0:12].re("p (a b) -> p a b", a=2))
            em.act(st[:, 14:15], st[:, 13:14], AF.Sqrt, bias=cx.eps_ln[:, 0:1])
            em.recip(st[:, 15:16], st[:, 14:15])
            em.stt("dve", st[:, 16:17], st[:, 12:13], -1.0, st[:, 15:16], ALU.mult, ALU.mult)
            em.act(res[:, :], res[:, :], AF.Identity, bias=st[:, 16:17], scale=st[:, 15:16])
            em.tt("dve", res[:, :], res[:, :], cx.pbt[:, 0:1024], ALU.mult)
            em.tt("pool", res[:, :], res[:, :], cx.pbt[:, 1024:2048], ALU.add)
            cx.small.put(st)
            r0 = blk * TB + ti * 128
            em.dma("sp", out_dst[r0:r0 + 128, :], res[:, :])


def pack_phaseB(inp, l):
    wM = np.ascontiguousarray(inp["w_in"][l][:, 5272:8344])
    wb = inp["w_branch"][l]
    rows = []
    for hh in range(2):
        for br in range(3):
            rows.append(wb[br, 256 * hh:256 * hh + 256, :])
    wbr = np.ascontiguousarray(np.concatenate(rows, axis=0))
    wout = np.ascontiguousarray(inp["w_out"][l])
    pbt = np.concatenate([inp["post_g"][l], inp["post_b"][l]]).reshape(1, 2048).astype(np.float32)
    return wM, wbr, wout, pbt


def _din(nc, name, shape, dt=F32):
    return nc.dram_tensor(name, shape, dt, kind="ExternalInput").ap()


def build_prog_A(S):
    nc = bass.Bass("TRN2", target_bir_lowering=False)
    dr = dict(consts=_din(nc, "consts", [128, NCONST]), cvec=_din(nc, "cvec", [128, 8]), adab_fm=_din(nc, "adab_fm", [1, 128, 24]),
              adab_g=_din(nc, "adab_g", [1, 1, 1024]), ada_w=_din(nc, "ada_w", [1, 1024, 3072]), wA=_din(nc, "wA", [1, 1024, NCOL_A]),
              pfm=_din(nc, "pfm", [1, 128, NPF]), ptm=_din(nc, "ptm", [1, 1, NPT]), smallw=_din(nc, "smallw", [1, 128, 384]))
    x = _din(nc, "x", [S, 1024])
    yT = nc.dram_tensor("yT", [768, S], BF16, kind="ExternalOutput").ap()
    with ExitStack() as es:
        em = Em(nc, es)
        cx = setup_common(nc, es, em, dr)
        with ExitStack() as est:
            emit_ada(cx, dr, 1, est)
            em.barrier()
        with ExitStack() as esA:
            cx.es = esA
            setup_phaseA(cx, S)
            ytk = Tk(yT, "yT")
            phaseA(cx, dr, 0, S, lambda r0: x[r0:r0 + 128, :],
                   lambda blk: ytk[:, blk * TB:(blk + 1) * TB].re("(a p) t -> p a t", p=128))
            em.barrier()
    return nc


def build_prog_B(S):
    nc = bass.Bass("TRN2", target_bir_lowering=False)
    dr = dict(consts=_din(nc, "consts", [128, NCONST]), cvec=_din(nc, "cvec", [128, 8]), adab_fm=_din(nc, "adab_fm", [1, 128, 24]),
              adab_g=_din(nc, "adab_g", [1, 1, 1024]), ada_w=_din(nc, "ada_w", [1, 1024, 3072]),
              wM=_din(nc, "wM", [1, 1024, 3072]), wbr=_din(nc, "wbr", [1, 1536, 1024]), wout=_din(nc, "wout", [1, 1024, 1024]),
              pbt=_din(nc, "pbt", [1, 1, 2048]))
    x = _din(nc, "x", [S, 1024])
    yT = _din(nc, "yTin", [1536, S], BF16)
    out = nc.dram_tensor("out", [S, 1024], F32, kind="ExternalOutput").ap()
    with ExitStack() as es:
        em = Em(nc, es)
        cx = setup_common(nc, es, em, dr)
        with ExitStack() as est:
            emit_ada(cx, dr, 1, est)
            em.barrier()
        with ExitStack() as esB:
            cx.es = esB
            setup_phaseB(cx, S)
            phaseB(cx, dr, 0, S, lambda r0: x[r0:r0 + 128, :],
                   lambda c0: yT[:, c0:c0 + TB].rearrange("(a p) t -> p a t", p=128), Tk(out, "out"))
            em.barrier()
    return nc


def build_prog_fused(S, L=DEPTH):
    SB = S // 2
    CY = min(1024, SB)
    NCH = S // CY
    XR = min(512, SB)
    NXC = SB // XR
    nc = bass.Bass("TRN2", target_bir_lowering=False)
    dr = dict(consts=_din(nc, "consts", [128, NCONST]), cvec=_din(nc, "cvec", [128, 8]), adab_fm=_din(nc, "adab_fm", [L, 128, 24]),
              adab_g=_din(nc, "adab_g", [L, 1, 1024]), ada_w=_din(nc, "ada_w", [L, 1024, 3072]), wA=_din(nc, "wA", [L, 1024, NCOL_A]),
              pfm=_din(nc, "pfm", [L, 128, NPF]), ptm=_din(nc, "ptm", [L, 1, NPT]), smallw=_din(nc, "smallw", [L, 128, 384]),
              wM=_din(nc, "wM", [L, 1024, 3072]), wbr=_din(nc, "wbr", [L, 1536, 1024]), wout=_din(nc, "wout", [L, 1024, 1024]),
              pbt=_din(nc, "pbt", [L, 1, 2048]))
    x = _din(nc, "x", [S, 1024])
    xh = _din(nc, "xh", [SB, 1024])
    out = nc.dram_tensor("out", [SB, 1024], F32, kind="ExternalOutput").ap()
    ysend = nc.dram_tensor("ysend", [NCH, 768, CY], BF16).ap()
    yall = nc.dram_tensor("yall", [NCH, 1536, CY], BF16).ap()
    yh = nc.dram_tensor("yh_buf", [NCH // 2, 1536, CY], BF16).ap()
    o0 = nc.dram_tensor("o_half", [SB, 1024], F32).ap()
    x1 = nc.dram_tensor("x_next", [NXC, 2 * XR, 1024], F32).ap()
    groups = [[0, 1], [2, 3], [4, 5], [6, 7]]
    half_elems = (NCH // 2) * 1536 * CY
    FW = 16384
    hrows = half_elems // FW
    yall_f = yall.rearrange("c r t -> (c r t)").rearrange("(a f) -> a f", f=FW)
    yh_f = yh.rearrange("c r t -> (c r t)").rearrange("(a f) -> a f", f=FW)
    with ExitStack() as es:
        em = Em(nc, es)
        ybase = nc.scalar.snap((nc.scalar.partition_id() % 2) * hrows)
        cx = setup_common(nc, es, em, dr)
        with ExitStack() as est:
            emit_ada(cx, dr, L, est)
            em.barrier()

        def x1row(t0):
            row0 = ((t0 % SB) // XR) * (2 * XR) + (t0 // SB) * XR + (t0 % XR)
            return x1.rearrange("c r d -> (c r) d")[row0:row0 + 128, :]

        xsrcA = lambda r0: x[r0:r0 + 128, :]
        for l in range(L):
            ys_tk = Tk(ysend, "ysend%d" % l)
            with ExitStack() as esA:
                cx.es = esA
                setup_phaseA(cx, S)
                phaseA(cx, dr, l, S, xsrcA,
                       lambda blk, ys_tk=ys_tk: ys_tk[(blk * TB) // CY, :, (blk * TB) % CY:(blk * TB) % CY + TB].re("(a p) t -> p a t", p=128))
            em.collectives("AllGather", [(ysend[c], yall[c]) for c in range(NCH)], groups)
            yh_tk = Tk(yh, "yh%d" % l)
            em.dma("act", V(yh_tk, yh_f), yall_f[bass.ds(ybase, hrows), :])
            dst = out if l == L - 1 else o0
            if l == 0:
                xsrc = lambda r0: xh[r0:r0 + 128, :]
            else:
                xsrc = lambda r0: o0[r0:r0 + 128, :]
            with ExitStack() as esB:
                cx.es = esB
                setup_phaseB(cx, SB)
                phaseB(cx, dr, l, SB, xsrc,
                       lambda c0, yh_tk=yh_tk: yh_tk[c0 // CY, :, c0 % CY:c0 % CY + TB].re("(a p) t -> p a t", p=128), Tk(dst, "dst%d" % l))
            if l < L - 1:
                em.collectives("AllGather", [(o0[c * XR:(c + 1) * XR, :], x1[c]) for c in range(NXC)], groups)
                xsrcA = x1row
        em.barrier()
        print("fused ninst", em.ninst, flush=True)
    return nc


def pack_ada_all(inp, b, L):
    cvec = np.ascontiguousarray(inp["c"][b].reshape(8, 128).T)
    fms, gs = [], []
    for l in range(L):
        fm = np.ascontiguousarray(inp["ada_b"][l][0:2048].reshape(16, 128).T)
        fms.append(np.concatenate([fm, np.zeros((128, 8), np.float32)], axis=1))
        gs.append(inp["ada_b"][l][2048:3072].reshape(1, 1024))
    return cvec, np.stack(fms), np.stack(gs)


_PROGS = {}


def kernel(**inputs):
    inp = {k: np.asarray(v) for k, v in inputs.items()}
    B, S = inp["x"].shape[0], inp["x"].shape[1]
    L = inp["w_in"].shape[0]
    SB = S // 2
    consts = make_consts()
    key = ("F", S, L)
    if key not in _PROGS:
        _PROGS[key] = build_prog_fused(S, L)
    prog = _PROGS[key]
    packsA = [[pack_phaseA(inp, l, hh) for l in range(L)] for hh in range(2)]
    stA = [[np.stack([packsA[hh][l][i] for l in range(L)]) for i in range(4)] for hh in range(2)]
    packsB = [pack_phaseB(inp, l) for l in range(L)]
    stB = [np.stack([packsB[l][i] for l in range(L)]) for i in range(4)]
    ada_w = np.ascontiguousarray(inp["ada_w"][:L])
    maps = []
    ncore = 2 * B
    for core in range(ncore):
        b, hh = core // 2, core % 2
        cvec, adab_fm, adab_g = pack_ada_all(inp, b, L)
        xb = np.ascontiguousarray(inp["x"][b])
        maps.append(dict(consts=consts, cvec=cvec, adab_fm=adab_fm, adab_g=adab_g, ada_w=ada_w,
                         wA=stA[hh][0], pfm=stA[hh][1], ptm=stA[hh][2], smallw=stA[hh][3],
                         wM=stB[0], wbr=stB[1], wout=stB[2], pbt=stB[3],
                         x=xb, xh=np.ascontiguousarray(xb[hh * SB:(hh + 1) * SB])))
    res = run_bass_kernel_spmd(prog, maps, core_ids=list(range(ncore)))
    outs = [np.asarray(r["out"]) for r in res.results]
    return np.stack([np.concatenate([outs[2 * b], outs[2 * b + 1]], axis=0) for b in range(B)]).astype(np.float32)
```

```python
import numpy as np
from contextlib import ExitStack
import concourse.bass as bass
import concourse.mybir as mybir
from concourse.bass_utils import run_bass_kernel_spmd
import ml_dtypes

F32 = mybir.dt.float32
BF16 = mybir.dt.bfloat16
AF = mybir.ActivationFunctionType
ALU = mybir.AluOpType

D = 1024
W = 512
IN_WIDTH = 8344
DEPTH = 2
LN_EPS = 1e-5
RW_GN_EPS = 64e-5
M2_EPS = 1e-5
GLA_EPS = 1e-5
ALPHA = (2.0 * DEPTH) ** 0.25
C0 = float(np.exp(-0.5))
NFM = 16
NCOL_FM = 15 * 128 + 16
NCOL_TM = 772
NCOL_A = NCOL_FM + NCOL_TM
TB = 512
DBG_STOP = 99

CO_ID, CO_MS, CO_MI, CO_LS, CO_ONE, CO_BO, CO_IND, CO_RST = 0, 128, 256, 384, 512, 640, 768, 776
NCONST = CO_RST + 512
PF_MU, PF_W0, PF_A0, PF_KK, PF_KA, PF_RK, PF_CW, PF_CB, PF_GKB, PF_GNW = 0, 7, 9, 11, 13, 15, 17, 33, 37, 38
NPF = 39
PT_LNG, PT_LNB, PT_NW, PT_DS, PT_DTB, PT_ALOG = 0, 256, 512, 768, 1024, 1028
NPT = 1032


def make_consts():
    c = np.zeros((128, NCONST), np.float32)
    i = np.arange(128)
    c[:, CO_ID:CO_ID + 128] = np.eye(128)
    c[:, CO_MS:CO_MS + 128] = (i[:, None] < i[None, :])
    c[:, CO_MI:CO_MI + 128] = (i[:, None] <= i[None, :])
    c[:, CO_LS:CO_LS + 128] = (i[:, None] > i[None, :])
    c[:, CO_ONE:CO_ONE + 128] = 1.0
    c[:, CO_BO:CO_BO + 128] = ((i[:, None] // 64) == (i[None, :] // 64))
    c[:64, CO_IND + 0] = 1.0
    c[64:, CO_IND + 1] = 1.0
    r = np.ones(512, np.float32)
    r[::128] = 0.0
    c[:, CO_RST:CO_RST + 512] = r[None, :]
    return c


class Tk:
    def __init__(self, t, name="", psum=False):
        self.t = t
        self.w = {}
        self.r = {}
        self.name = name
        self.psum = psum

    def __getitem__(self, k):
        return V(self, self.t[k])


class V:
    def __init__(self, tk, ap):
        self.tk = tk
        self.ap = ap

    def __getitem__(self, k):
        return V(self.tk, self.ap[k])

    def re(self, s, **kw):
        return V(self.tk, self.ap.rearrange(s, **kw))

    def bc(self, shape):
        return V(self.tk, self.ap.to_broadcast(shape))

    def un(self, ax):
        return V(self.tk, self.ap.unsqueeze(ax))


def _ap(x):
    return x.ap if isinstance(x, V) else x


class Em:
    NDMA = 24

    def __init__(self, nc, es):
        self.nc = nc
        self.es = es
        self.eng = {"pe": nc.tensor, "act": nc.scalar, "dve": nc.vector, "pool": nc.gpsimd, "sp": nc.sync}
        self.semh = {}
        self.cnt = {}
        self.ep = -1
        self.new_epoch()
        self.dtot = []
        for i in range(self.NDMA):
            self.semh[("d", i)] = es.enter_context(nc.semaphore("s_d%d" % i))
            self.dtot.append(0)
        self.drr = 0
        self.waited = {}
        self.ninst = 0

    def new_epoch(self):
        self.ep += 1
        for e in ("pe", "act", "dve", "pool"):
            k = (e, self.ep)
            self.semh[k] = self.es.enter_context(self.nc.semaphore("s_%s_%d" % (e, self.ep)))
            self.cnt[k] = 0

    def _deps(self, e, Rs, Ws):
        deps = {}
        for v in Rs:
            if isinstance(v, V):
                for k, c in v.tk.w.items():
                    deps[k] = max(deps.get(k, 0), c)
                if v.tk.psum:
                    for k, c in v.tk.r.items():
                        if k[0] != e:
                            deps[k] = max(deps.get(k, 0), c)
        for v in Ws:
            if isinstance(v, V):
                for k, c in v.tk.w.items():
                    deps[k] = max(deps.get(k, 0), c)
                for k, c in v.tk.r.items():
                    deps[k] = max(deps.get(k, 0), c)
        for k, c in deps.items():
            if e == "pe" and k[0] == "pe":
                continue
            if self.waited.get((e, k), 0) >= c:
                continue
            self.eng[e].wait_ge(self.semh[k], c)
            self.waited[(e, k)] = c
            self.ninst += 1

    def op(self, e, fn, Rs, Ws):
        self._deps(e, Rs, Ws)
        inst = fn(self.eng[e])
        k = (e, self.ep)
        self.cnt[k] += 1
        inst.then_inc(self.semh[k], 1)
        c = self.cnt[k]
        for v in Rs:
            if isinstance(v, V):
                v.tk.r[k] = c
        for v in Ws:
            if isinstance(v, V):
                v.tk.w[k] = c
        self.ninst += 1
        return inst

    def dma(self, q, out, in_, **kw):
        self._deps(q, [in_], [out])
        s = self.drr
        self.drr = (self.drr + 1) % self.NDMA
        k = ("d", s)
        if self.waited.get((q, k), 0) < self.dtot[s]:
            self.eng[q].wait_ge(self.semh[k], self.dtot[s])
            self.waited[(q, k)] = self.dtot[s]
        self.eng[q].dma_start(out=_ap(out), in_=_ap(in_), **kw).then_inc(self.semh[k], 16)
        self.dtot[s] += 16
        if isinstance(in_, V):
            in_.tk.r[k] = self.dtot[s]
        if isinstance(out, V):
            out.tk.w[k] = self.dtot[s]
        self.ninst += 1

    def wait_all(self, q, tks):
        for tk in tks:
            for k, c in list(tk.w.items()):
                if self.waited.get((q, k), 0) < c:
                    self.eng[q].wait_ge(self.semh[k], c)
                    self.waited[(q, k)] = c

    def barrier(self):
        for e in ("pe", "act", "dve", "pool", "sp"):
            for k in list(self.semh.keys()):
                c = self.cnt[k] if k in self.cnt else self.dtot[k[1]]
                if c > 0 and self.waited.get((e, k), 0) < c:
                    self.eng[e].wait_ge(self.semh[k], c)
                    self.waited[(e, k)] = c
        self.new_epoch()

    def collectives(self, kind, pairs, groups):
        if not hasattr(self, "ccsem"):
            self.ccsem = self.es.enter_context(self.nc.semaphore("s_cc"))
            self.ccn = 0
        self.barrier()
        for in_ap, out_ap in pairs:
            self.nc.gpsimd.collective_compute(kind, ALU.bypass, replica_groups=groups, ins=[in_ap], outs=[out_ap]).then_inc(self.ccsem, 1)
            self.ccn += 1
        for e in ("pe", "act", "dve", "pool", "sp"):
            self.eng[e].wait_ge(self.ccsem, self.ccn)

    def mm(self, out, lhsT, rhs, start=True, stop=True):
        return self.op("pe", lambda e: e.matmul(out.ap, lhsT.ap, rhs.ap, start=start, stop=stop, skip_group_check=True), [lhsT, rhs], [out])

    def tr(self, out, in_, ident):
        return self.op("pe", lambda e: e.transpose(out.ap, in_.ap, ident.ap), [in_, ident], [out])

    def act(self, out, in_, func, bias=None, scale=1.0, accum=None, eng="act"):
        kw = {}
        if bias is not None:
            kw["bias"] = _ap(bias)
        if accum is not None:
            kw["accum_out"] = accum.ap
        kw["scale"] = _ap(scale)
        Ws = [out] + ([accum] if accum is not None else [])
        return self.op(eng, lambda e: e.activation(out.ap, in_.ap, func, **kw), [in_, bias, scale], Ws)

    def tt(self, eng, out, a, b, op):
        return self.op(eng, lambda e: e.tensor_tensor(out.ap, a.ap, b.ap, op), [a, b], [out])

    def ts(self, eng, out, a, s1, op0, s2=None, op1=None):
        if op1 is None:
            return self.op(eng, lambda e: e.tensor_scalar(out.ap, a.ap, _ap(s1), None, op0), [a, s1], [out])
        return self.op(eng, lambda e: e.tensor_scalar(out.ap, a.ap, _ap(s1), _ap(s2), op0, op1), [a, s1, s2], [out])

    def stt(self, eng, out, a, s, b, op0, op1):
        return self.op(eng, lambda e: e.scalar_tensor_tensor(out.ap, a.ap, _ap(s), b.ap, op0, op1), [a, s, b], [out])

    def copy(self, eng, out, a):
        if eng == "act":
            return self.op(eng, lambda e: e.copy(out.ap, a.ap), [a], [out])
        return self.op(eng, lambda e: e.tensor_copy(out.ap, a.ap), [a], [out])

    def scan(self, out, d0, d1, init, op0, op1):
        return self.op("dve", lambda e: e.tensor_tensor_scan(out.ap, d0.ap, d1.ap, init, op0, op1), [d0, d1], [out])

    def memset(self, eng, out, val):
        return self.op(eng, lambda e: e.memset(out.ap, val), [], [out])

    def recip(self, out, a):
        return self.op("dve", lambda e: e.reciprocal(out.ap, a.ap), [a], [out])

    def bn_stats(self, out, a):
        return self.op("dve", lambda e: e.bn_stats(out.ap, a.ap), [a], [out])

    def bn_aggr(self, out, a):
        return self.op("dve", lambda e: e.bn_aggr(out.ap, a.ap), [a], [out])


class Pool:
    def __init__(self, nc, es, name, n, shape, dtype, psum=False):
        self.free = []
        self.name = name
        _UID[0] += 1
        name = "%s_%d_" % (name, _UID[0])
        for i in range(n):
            if psum:
                t = es.enter_context(nc.psum_tensor("%s%d" % (name, i), shape, dtype))
            else:
                t = es.enter_context(nc.sbuf_tensor("%s%d" % (name, i), shape, dtype))
            self.free.append(Tk(t, "%s%d" % (name, i), psum=psum))
        self.n = n
        self.minfree = n

    def get(self):
        assert self.free, "pool %s exhausted" % self.name
        tk = self.free.pop(0)
        self.minfree = min(self.minfree, len(self.free))
        return tk

    def put(self, *tks):
        for tk in tks:
            assert tk not in self.free
            self.free.append(tk)


_UID = [0]


def sb(nc, es, name, shape, dtype=F32):
    _UID[0] += 1
    return Tk(es.enter_context(nc.sbuf_tensor("sb%d_%s" % (_UID[0], name), shape, dtype)), name)


class Ctx:
    pass


class _Stop(Exception):
    pass


def chk(x):
    if DBG_STOP <= x:
        raise _Stop()


def lastc(t):
    return t[:, 3:515].re("p (c t) -> p c t", c=4)[:, :, 127]


def setup_common(nc, es, em, dr):
    cx = Ctx()
    cx.nc, cx.es, cx.em = nc, es, em
    cx.const = sb(nc, es, "const", [128, NCONST])
    em.dma("sp", cx.const[:, :], dr["consts"])
    cx.identb = sb(nc, es, "identb", [128, 128], BF16)
    em.copy("dve", cx.identb[:, :], cx.const[:, CO_ID:CO_ID + 128])
    cx.ps = Pool(nc, es, "ps", 7, [128, 512], F32, psum=True)
    cx.psb = Tk(es.enter_context(nc.psum_tensor("psb", [128, 1024], BF16)), "psb", psum=True)
    cx.ident = cx.const[:, CO_ID:CO_ID + 128]
    cx.ms = cx.const[:, CO_MS:CO_MS + 128]
    cx.mi = cx.const[:, CO_MI:CO_MI + 128]
    cx.mask2 = cx.const[:, CO_MS:CO_MS + 256]
    cx.ls = cx.const[:, CO_LS:CO_LS + 128]
    cx.ones = cx.const[:, CO_ONE:CO_ONE + 128]
    cx.bones = cx.const[:, CO_BO:CO_BO + 128]
    cx.ind2 = cx.const[:, CO_IND:CO_IND + 2]
    cx.rst = cx.const[:, CO_RST:CO_RST + 512]
    return cx


def emit_ada(cx, dr, L, est):
    nc, es, em = cx.nc, cx.es, cx.em
    cx.sc1 = [sb(nc, es, "sc1fm%d" % l, [128, 8]) for l in range(L)]
    cx.sh = [sb(nc, es, "shfm%d" % l, [128, 8]) for l in range(L)]
    cx.g1 = [sb(nc, es, "g1bc%d" % l, [128, 1024]) for l in range(L)]
    cvec = sb(nc, est, "cvec", [128, 8])
    em.dma("sp", cvec[:, :], dr["cvec"])
    sc = sb(nc, est, "sc", [128, 8])
    em.act(sc[:, :], cvec[:, :], AF.Silu)
    scb = sb(nc, est, "scb", [128, 8, 128])
    em.copy("dve", scb[:, :, :], sc[:, :].un(2).bc([128, 8, 128]))
    adaw = [sb(nc, est, "adaw%d" % i, [128, 3072]) for i in range(2)]
    adab_fm = sb(nc, est, "adabfm", [128, 24])
    adab_bc = sb(nc, est, "adabbc", [128, 1024])
    for l in range(L):
        em.dma("sp", adab_fm[:, :], dr["adab_fm"][l])
        em.dma("sp", adab_bc[:, :], dr["adab_g"][l].partition_broadcast(128))
        psm = cx.ps.get()
        psg0 = cx.ps.get()
        psg1 = cx.ps.get()
        em.memset("dve", psm[:, 0:16], 0.0)
        for kc in range(8):
            aw = adaw[kc % 2]
            em.dma("sp", aw[:, :], dr["ada_w"][l, kc * 128:(kc + 1) * 128, :])
            for j in range(16):
                em.mm(psm[:, j:j + 1], aw[:, j * 128:(j + 1) * 128], sc[:, kc:kc + 1], start=False, stop=(kc == 7))
            em.mm(psg0[:, :], scb[:, kc, :], aw[:, 2048:2560], start=(kc == 0), stop=(kc == 7))
            em.mm(psg1[:, :], scb[:, kc, :], aw[:, 2560:3072], start=(kc == 0), stop=(kc == 7))
        sh, sc1, g1 = cx.sh[l], cx.sc1[l], cx.g1[l]
        em.tt("dve", sh[:, :], psm[:, 0:8], adab_fm[:, 0:8], ALU.add)
        em.tt("dve", sc1[:, :], psm[:, 8:16], adab_fm[:, 8:16], ALU.add)
        em.ts("dve", sc1[:, :], sc1[:, :], 1.0, ALU.add)
        em.tt("dve", g1[:, 0:512], psg0[:, :], adab_bc[:, 0:512], ALU.add)
        em.tt("dve", g1[:, 512:1024], psg1[:, :], adab_bc[:, 512:1024], ALU.add)
        em.ts("dve", g1[:, :], g1[:, :], 1.0, ALU.add)
        cx.ps.put(psm, psg0, psg1)


def emit_ln_transpose(cx, xt, hT, ti, sc1, sh, scr):
    em = cx.em
    st = cx.small.get()
    for hf in range(2):
        em.bn_stats(st[:, hf * 6:(hf + 1) * 6], xt[:, hf * 512:(hf + 1) * 512])
    em.bn_aggr(st[:, 12:14], st[:, 0:12].re("p (a b) -> p a b", a=2))
    em.act(st[:, 14:15], st[:, 13:14], AF.Sqrt, bias=cx.eps_ln[:, 0:1])
    em.recip(st[:, 15:16], st[:, 14:15])
    em.stt("dve", st[:, 16:17], st[:, 12:13], -1.0, st[:, 15:16], ALU.mult, ALU.mult)
    xn = scr
    em.act(xn[:, :], xt[:, :], AF.Identity, bias=st[:, 16:17], scale=st[:, 15:16])
    for kc in range(8):
        em.tr(cx.psb[:, kc * 128:(kc + 1) * 128], xn[:, kc * 128:(kc + 1) * 128], cx.identb[:, :])
    tmp = cx.lntmp
    em.tt("dve", tmp[:, :].re("p (k t) -> p k t", k=8), cx.psb[:, :].re("p (k t) -> p k t", k=8),
          sc1[:, :].un(2).bc([128, 8, 128]), ALU.mult)
    em.tt("pool", hT[:, :, ti * 128:(ti + 1) * 128], tmp[:, :].re("p (k t) -> p k t", k=8),
          sh[:, :].un(2).bc([128, 8, 128]), ALU.add)
    cx.small.put(st)
    return st


def setup_phaseA(cx, S):
    nc, es, em = cx.nc, cx.es, cx.em
    cx.wA = sb(nc, es, "wA", [128, 8, NCOL_A], BF16)
    cx.pfm = sb(nc, es, "pfm", [128, NPF + 8])
    cx.ptm = sb(nc, es, "ptm", [128, NPT + 8])
    cx.smallw = sb(nc, es, "smallw", [128, 384])
    cx.big = Pool(nc, es, "big", 30, [128, 515], F32)
    cx.tm = Pool(nc, es, "tm", 27, [128, 256], F32)
    cx.small = Pool(nc, es, "sm", 8, [128, 32], F32)
    cx.xt = [sb(nc, es, "xt%d" % i, [128, 1024]) for i in range(2)]
    cx.xn = sb(nc, es, "xn", [128, 1024], BF16)
    cx.lntmp = sb(nc, es, "lntmp", [128, 1024])
    cx.hT = sb(nc, es, "hT", [128, 8, TB], BF16)
    cx.AR = [sb(nc, es, "AR%d" % i, [128, 4, 256]) for i in range(2)]
    cx.carry = sb(nc, es, "carry", [128, NFM, 3])
    cx.yT = [sb(nc, es, "yT%d" % i, [128, 6, TB], BF16) for i in range(2)]
    cx.ST = [[sb(nc, es, "ST%d_%d" % (ct, i), [128, 128]) for i in range(2)] for ct in range(2)]
    cx.SM = [sb(nc, es, "SM%d" % i, [128, 256]) for i in range(2)]
    cx.SG = [sb(nc, es, "SG%d" % i, [128, 128]) for i in range(2)]
    cx.eps_ln = sb(nc, es, "epsln", [128, 4])
    em.memset("dve", cx.eps_ln[:, 0:1], LN_EPS)
    em.memset("dve", cx.eps_ln[:, 1:2], RW_GN_EPS)
    em.memset("dve", cx.eps_ln[:, 2:3], M2_EPS)
    em.memset("dve", cx.eps_ln[:, 3:4], GLA_EPS)
    cx.junk = sb(nc, es, "junk", [128, 256])
    cx.QB = sb(nc, es, "QB", [128, 4, 2, 128])
    em.memset("dve", cx.QB[:, :, :, :], 0.0)


def phaseA(cx, dr, l, S, x_src, y_dst):
    try:
        _phaseA(cx, dr, l, S, x_src, y_dst)
    except _Stop:
        cx.em.dma("sp", y_dst(0), cx.yT[0][:, :, :])


def _phaseA(cx, dr, l, S, x_src, y_dst):
    nc, es, em = cx.nc, cx.es, cx.em
    G, PG = cx.big.get, cx.big.put
    T, PT = cx.tm.get, cx.tm.put
    P, PP = cx.ps.get, cx.ps.put
    for kc in range(8):
        em.dma("pool", cx.wA[:, kc, :], dr["wA"][l][kc * 128:(kc + 1) * 128, :])
    em.dma("sp", cx.pfm[:, 0:NPF], dr["pfm"][l])
    em.dma("sp", cx.ptm[:, 0:NPT], dr["ptm"][l].partition_broadcast(128))
    em.dma("sp", cx.smallw[:, :], dr["smallw"][l])
    pf = lambda c: cx.pfm[:, c:c + 1]
    em.ts("dve", cx.pfm[:, NPF:NPF + 2], cx.pfm[:, PF_KA:PF_KA + 2], -1.0, ALU.mult, 1.0, ALU.add)
    em.ts("dve", cx.pfm[:, NPF + 2:NPF + 3], cx.pfm[:, PF_GKB:PF_GKB + 1], -1.0, ALU.mult)
    em.act(cx.ptm[:, NPT:NPT + 4], cx.ptm[:, PT_ALOG:PT_ALOG + 4], AF.Exp)
    em.ts("dve", cx.ptm[:, NPT:NPT + 4], cx.ptm[:, NPT:NPT + 4], -1.0, ALU.mult)
    ah_bc = cx.ptm[:, NPT:NPT + 4]
    chk(0)
    em.memset("dve", cx.carry[:, :, :], 0.0)
    for ct in range(2):
        em.memset("dve", cx.ST[ct][0][:, :], 0.0)
        em.memset("dve", cx.ST[ct][1][:, :], 0.0)
    em.memset("dve", cx.SM[0][:, :], 0.0)
    em.memset("dve", cx.SG[0][:, :], 0.0)
    stp = [0, 0]
    smp = 0
    sgp = 0
    nblk = S // TB
    for blk in range(nblk):
        yT = cx.yT[blk % 2]
        for ti in range(4):
            xt = cx.xt[ti % 2]
            r0 = blk * TB + ti * 128
            em.dma("sp", xt[:, :], x_src(r0))
            emit_ln_transpose(cx, xt, cx.hT, ti, cx.sc1[l], cx.sh[l], cx.xn)
        chk(1)
        raw = []
        for j in range(NFM):
            nco = 128 if j < 15 else 16
            ps = P()
            for kc in range(8):
                em.mm(ps[0:nco, :], cx.wA[:, kc, j * 128:j * 128 + nco], cx.hT[:, kc, :], start=(kc == 0), stop=(kc == 7))
            g = G()
            need_hist = j <= 10
            if need_hist:
                em.copy("pool", g[0:nco, 0:3], cx.carry[0:nco, j, :])
            em.copy("act", g[0:nco, 3:515], ps[0:nco, :])
            if need_hist:
                em.copy("pool", cx.carry[0:nco, j, :], g[0:nco, 512:515])
            PP(ps)
            raw.append(g)
        rwg, zs, vg, dtr = [], [], [], []
        for ti in range(4):
            ps1, ps2 = P(), P()
            for kc in range(8):
                em.mm(ps1[:, :], cx.hT[:, kc, ti * 128:(ti + 1) * 128], cx.wA[:, kc, NCOL_FM:NCOL_FM + 512], start=(kc == 0), stop=(kc == 7))
            for kc in range(8):
                em.mm(ps2[:, 0:260], cx.hT[:, kc, ti * 128:(ti + 1) * 128], cx.wA[:, kc, NCOL_FM + 512:NCOL_FM + 772], start=(kc == 0), stop=(kc == 7))
            a, b_, c_, d_ = T(), T(), T(), cx.small.get()
            em.act(a[:, :], ps1[:, 0:256], AF.Silu)
            em.act(b_[:, :], ps1[:, 256:512], AF.Silu)
            em.copy("dve", c_[:, :], ps2[:, 0:256])
            em.tt("dve", d_[:, 0:4], ps2[:, 256:260], cx.ptm[:, PT_DTB:PT_DTB + 4], ALU.add)
            PP(ps1, ps2)
            rwg.append(a); zs.append(b_); vg.append(c_); dtr.append(d_)
        chk(2)
        wd = raw[6]
        wdl = G()
        dtmp = G()
        em.tt("dve", dtmp[:, 3:515], wd[:, 2:514], wd[:, 3:515], ALU.subtract)
        em.stt("dve", wdl[:, 3:515], dtmp[:, 3:515], pf(PF_MU + 6), wd[:, 3:515], ALU.mult, ALU.add)
        em.act(wdl[0:64, 3:515], wdl[0:64, 3:515], AF.Tanh)
        PG(dtmp, wd)
        chk(2.01)
        for ct in range(2):
            def lerp(rawt, mucol):
                d_ = G()
                o = G()
                em.tt("dve", d_[:, 3:515], rawt[:, 2:514], rawt[:, 3:515], ALU.subtract)
                em.stt("dve", o[:, 3:515], d_[:, 3:515], pf(mucol), rawt[:, 3:515], ALU.mult, ALU.add)
                PG(d_, rawt)
                return o
            r = lerp(raw[0 + ct], PF_MU + 0 + ct)
            k = lerp(raw[2 + ct], PF_MU + 2 + ct)
            v = lerp(raw[4 + ct], PF_MU + 4 + ct)
            chk(2.02)
            D3 = slice(3, 515)
            AR = cx.AR[ct]
            ps = P()
            em.mm(ps[:, :], cx.smallw[0:64, ct * 128:(ct + 1) * 128], wdl[0:64, D3])
            sg = G()
            em.act(sg[:, D3], ps[:, :], AF.Sigmoid, bias=pf(PF_W0 + ct))
            PP(ps)
            ps = P()
            em.mm(ps[:, :], cx.smallw[64:128, ct * 128:(ct + 1) * 128], wdl[64:128, D3])
            alr = G()
            em.act(alr[:, D3], ps[:, :], AF.Sigmoid, bias=pf(PF_A0 + ct))
            PP(ps)
            chk(2.03)
            CS = G()
            em.scan(CS[:, D3], cx.rst, sg[:, D3], 0.0, ALU.mult, ALU.add)
            CSx = G()
            em.tt("pool", CSx[:, D3], CS[:, D3], sg[:, D3], ALU.subtract)
            PG(sg)
            eL, enL, eLx, eD = G(), G(), G(), G()
            em.act(eL[:, D3], CS[:, D3], AF.Exp, scale=-C0)
            em.act(enL[:, D3], CS[:, D3], AF.Exp, scale=C0)
            em.act(eLx[:, D3], CSx[:, D3], AF.Exp, scale=-C0)
            PG(CSx)
            chk(2.04)
            sm = cx.small.get()
            em.ts("dve", sm[:, 0:4], lastc(CS), -C0, ALU.mult)
            for ci in range(4):
                cs3 = slice(3 + ci * 128, 3 + (ci + 1) * 128)
                em.act(eD[:, cs3], CS[:, cs3], AF.Exp, bias=sm[:, ci:ci + 1], scale=C0)
            em.copy("dve", sm[:, 4:8], lastc(eL))
            PCt = sm
            PG(CS)
            chk(2.05)
            kk = G()
            em.ts("dve", kk[:, D3], k[:, D3], pf(PF_KK + ct), ALU.mult)
            kk2 = G()
            em.act(kk2[:, D3], kk[:, D3], AF.Square)
            ps = P()
            em.mm(ps[:, :], cx.bones, kk2[:, D3])
            em.act(kk2[:, D3], ps[:, :], AF.Sqrt)
            PP(ps)
            em.ts("dve", kk2[:, D3], kk2[:, D3], 1e-12, ALU.max)
            em.recip(kk2[:, D3], kk2[:, D3])
            em.tt("dve", kk[:, D3], kk[:, D3], kk2[:, D3], ALU.mult)
            PG(kk2)
            chk(2.06)
            f = G()
            em.ts("pool", f[:, D3], alr[:, D3], pf(PF_KA + ct), ALU.mult, cx.pfm[:, NPF + ct:NPF + ct + 1], ALU.add)
            kp = G()
            em.tt("pool", kp[:, D3], k[:, D3], f[:, D3], ALU.mult)
            PG(f, k)
            bv = G()
            em.tt("dve", bv[:, D3], kk[:, D3], alr[:, D3], ALU.mult)
            PG(alr)
            chk(2.07)
            ARv = AR[:, :, :]
            em.stt("dve", AR[:, :, 0:128], kk[:, D3].re("p (c t) -> p c t", c=4), -1.0,
                   eLx[:, D3].re("p (c t) -> p c t", c=4), ALU.mult, ALU.mult)
            em.tt("pool", AR[:, :, 128:256], r[:, D3].re("p (c t) -> p c t", c=4), eL[:, D3].re("p (c t) -> p c t", c=4), ALU.mult)
            PG(kk, eLx, eL)
            chk(2.08)
            bt, kt, bd, kd = G(), G(), G(), G()
            em.tt("dve", bt[:, D3], bv[:, D3], enL[:, D3], ALU.mult)
            em.tt("pool", kt[:, D3], kp[:, D3], enL[:, D3], ALU.mult)
            em.tt("dve", bd[:, D3], bv[:, D3], eD[:, D3], ALU.mult)
            em.tt("pool", kd[:, D3], kp[:, D3], eD[:, D3], ALU.mult)
            PG(bv, enL, eD)
            rk = G()
            em.stt("dve", rk[:, D3], r[:, D3], pf(PF_RK + ct), kp[:, D3], ALU.mult, ALU.mult)
            PG(r, kp)
            chk(2.1)
            for ci in range(4):
                cs3 = slice(3 + ci * 128, 3 + (ci + 1) * 128)
                cs = slice(ci * 128, (ci + 1) * 128)
                ST = cx.ST[ct][stp[ct]]
                STn = cx.ST[ct][1 - stp[ct]]
                stp[ct] = 1 - stp[ct]
                psT = P()
                em.tr(psT[:, 0:128], v[:, cs3], cx.ident)
                chk(2.11)
                em.tr(psT[:, 128:256], bd[:, cs3], cx.ident)
                em.tr(psT[:, 256:384], kd[:, cs3], cx.ident)
                em.tr(psT[:, 384:512], AR[:, ci, 0:128], cx.ident)
                chk(2.12)
                vtm, bdkd, Z0, Z1 = T(), T(), T(), T()
                Zs = [Z0, Z1]
                acol = [slice(0, 64), slice(64, 128)]
                ucol = [slice(64, 128), slice(0, 64)]
                em.copy("act", vtm[:, 0:128], psT[:, 0:128])
                chk(2.13)
                em.copy("act", bdkd[:, 0:256], psT[:, 128:384])
                chk(2.14)
                for h in range(2):
                    em.copy("act", Zs[h][:, acol[h]], psT[:, 384 + h * 64:384 + (h + 1) * 64])
                PP(psT)
                chk(2.2)
                M1s, M2s, Pk, Qk, prevPQ = [], [], [None, None], [None, None], [None, None]
                for h in range(2):
                    pb = slice(64 * h, 64 * h + 64)
                    psA = P()
                    em.mm(psA[:, 0:256], bt[pb, cs3], AR[pb, ci, :])
                    em.mm(psA[:, 256:512], kt[pb, cs3], AR[pb, ci, :])
                    M1, M2_ = T(), T()
                    em.tt("dve", M1[:, :], psA[:, 0:256], cx.mask2, ALU.mult)
                    em.tt("dve", M2_[:, :], psA[:, 256:512], cx.mask2, ALU.mult)
                    PP(psA)
                    psQ = P()
                    em.mm(psQ[:, 0:128], AR[pb, ci, 0:128], bt[pb, cs3])
                    em.mm(psQ[:, 128:192], M2_[:, 0:128], vtm[:, h * 64:(h + 1) * 64])
                    Q1 = T()
                    em.tt("dve", Q1[:, 0:128], psQ[:, 0:128], cx.ls, ALU.mult)
                    em.copy("act", Zs[h][:, ucol[h]], psQ[:, 128:192])
                    PP(psQ)
                    Pk[h], Qk[h], prevPQ[h] = M1[:, 0:128], Q1[:, 0:128], Q1
                    M1s.append(M1); M2s.append(M2_)
                chk(2.3)
                for lev in range(7):
                    for h in range(2):
                        Zh = Zs[h][:, 0:128]
                        psZ = P()
                        em.mm(psZ[:, 0:128], Pk[h], Zh)
                        em.tt("dve", Zh, psZ[:, 0:128], Zh, ALU.add)
                        PP(psZ)
                        if lev < 6:
                            psS = P()
                            em.mm(psS[:, 0:128], Qk[h], Pk[h])
                            PQ = T()
                            if lev < 5:
                                em.mm(psS[:, 128:256], Pk[h], Qk[h])
                                em.copy("act", PQ[:, 0:256], psS[:, 0:256])
                            else:
                                em.copy("act", PQ[:, 0:128], psS[:, 0:128])
                            PP(psS)
                            PT(prevPQ[h])
                            prevPQ[h] = PQ
                            Pk[h], Qk[h] = PQ[:, 0:128], PQ[:, 128:256]
                PT(prevPQ[0], prevPQ[1])
                chk(2.4)
                psX = P()
                em.tr(psX[:, 0:128], Z0[:, 0:128], cx.ident)
                em.tr(psX[:, 128:256], Z1[:, 0:128], cx.ident)
                ApT = T()
                em.copy("act", ApT[0:64, 0:128], psX[0:64, 0:128])
                em.copy("act", ApT[64:128, 0:128], psX[64:128, 128:256])
                PP(psX)
                chk(2.41)
                psU = P()
                em.mm(psU[:, 0:128], ApT[:, 0:128], ST[:, :])
                chk(2.42)
                U2 = T()
                for h in range(2):
                    em.tt("dve", U2[:, h * 64:(h + 1) * 64], psU[:, h * 64:(h + 1) * 64], Zs[h][:, ucol[h]], ALU.add)
                PP(psU)
                PT(ApT, Z0, Z1)
                chk(2.5)
                psY = P()
                em.mm(psY[:, 0:128], AR[:, ci, 128:256], ST[:, :], start=True, stop=False)
                for h in range(2):
                    hs = slice(h * 64, (h + 1) * 64)
                    em.mm(psY[:, hs], M1s[h][:, 128:256], U2[:, hs], start=False, stop=False)
                    em.mm(psY[:, hs], M2s[h][:, 128:256], vtm[:, hs], start=False, stop=True)
                psS = P()
                em.mm(psS[:, 0:128], bdkd[:, 0:128], U2[:, 0:128], start=True, stop=False)
                em.mm(psS[:, 0:128], bdkd[:, 128:256], vtm[:, 0:128], start=False, stop=True)
                for h in range(2):
                    pb = slice(64 * h, 64 * h + 64)
                    hs = slice(h * 64, (h + 1) * 64)
                    em.stt("dve", STn[pb, hs], ST[pb, hs], PCt[pb, 4 + ci:5 + ci], psS[pb, hs], ALU.mult, ALU.add)
                PP(psS)
                PT(M1s[0], M1s[1], M2s[0], M2s[1], bdkd, U2)
                chk(2.6)
                st = cx.small.get()
                for h in range(2):
                    em.bn_stats(st[:, h * 6:(h + 1) * 6], psY[:, h * 64:(h + 1) * 64])
                for h in range(2):
                    em.bn_aggr(st[:, 12 + 2 * h:14 + 2 * h], st[:, h * 6:(h + 1) * 6])
                mv = st[:, 12:16].re("p (h two) -> p h two", h=2)
                em.act(st[:, 16:18], mv[:, :, 1], AF.Sqrt, bias=cx.eps_ln[:, 1:2])
                em.recip(st[:, 18:20], st[:, 16:18])
                yn = T()
                yn3 = yn[:, 0:128].re("p (h c) -> p h c", h=2)
                em.tt("dve", yn3, psY[:, 0:128].re("p (h c) -> p h c", h=2), mv[:, :, 0:1].bc([128, 2, 64]), ALU.subtract)
                PP(psY)
                em.tt("dve", yn3, yn3, st[:, 18:20].un(2).bc([128, 2, 64]), ALU.mult)
                em.tt("pool", yn[:, 0:128], yn[:, 0:128], cx.ptm[:, PT_LNG + ct * 128:PT_LNG + (ct + 1) * 128], ALU.mult)
                em.tt("pool", yn[:, 0:128], yn[:, 0:128], cx.ptm[:, PT_LNB + ct * 128:PT_LNB + (ct + 1) * 128], ALU.add)
                chk(2.7)
                psB = P()
                em.mm(psB[:, 0:2], rk[:, cs3], cx.ind2)
                em.copy("act", st[:, 20:22], psB[:, 0:2])
                PP(psB)
                bo = T()
                em.tt("dve", bo[:, 0:128].re("p (h c) -> p h c", h=2), vtm[:, 0:128].re("p (h c) -> p h c", h=2),
                      st[:, 20:22].un(2).bc([128, 2, 64]), ALU.mult)
                em.tt("dve", yn[:, 0:128], yn[:, 0:128], bo[:, 0:128], ALU.add)
                em.tt("dve", yn[:, 0:128], yn[:, 0:128], rwg[ci][:, ct * 128:(ct + 1) * 128], ALU.mult)
                psT2 = P()
                em.tr(psT2[:, 0:128], yn[:, 0:128], cx.ident)
                em.copy("act", yT[:, ct, cs], psT2[:, 0:128])
                PP(psT2)
                PT(yn, bo, vtm)
                cx.small.put(st)
            cx.small.put(PCt)
            PG(v, bt, kt, bd, kd, rk)
        PG(wdl)
        for ti in range(4):
            PT(rwg[ti])
        chk(3)
        conv = []
        for q_ in range(4):
            rt = raw[7 + q_]
            acc = G()
            cw = lambda j: pf(PF_CW + q_ * 4 + j)
            em.ts("pool", acc[:, 3:515], rt[:, 0:512], cw(0), ALU.mult, pf(PF_CB + q_), ALU.add)
            em.stt("dve", acc[:, 3:515], rt[:, 1:513], cw(1), acc[:, 3:515], ALU.mult, ALU.add)
            em.stt("dve", acc[:, 3:515], rt[:, 2:514], cw(2), acc[:, 3:515], ALU.mult, ALU.add)
            em.stt("dve", acc[:, 3:515], rt[:, 3:515], cw(3), acc[:, 3:515], ALU.mult, ALU.add)
            em.act(acc[:, 3:515], acc[:, 3:515], AF.Silu)
            PG(rt)
            conv.append(acc)
        xs0, xs1, Bc, Cc = conv
        for ci in range(4):
            cs3 = slice(3 + ci * 128, 3 + (ci + 1) * 128)
            cs = slice(ci * 128, (ci + 1) * 128)
            SM = cx.SM[smp]
            SMn = cx.SM[1 - smp]
            smp = 1 - smp
            psT = P()
            em.tr(psT[:, 0:128], xs0[:, cs3], cx.ident)
            em.tr(psT[:, 128:256], xs1[:, cs3], cx.ident)
            em.tr(psT[:, 256:384], Bc[:, cs3], cx.ident)
            xtm, btm = T(), T()
            em.copy("act", xtm[:, :], psT[:, 0:256])
            em.copy("dve", btm[:, 0:128], psT[:, 256:384])
            PP(psT)
            sm = dtr[ci]
            em.act(sm[:, 4:8], sm[:, 0:4], AF.Exp)
            em.act(sm[:, 8:12], sm[:, 4:8], AF.Ln, bias=1.0)
            dt = sm[:, 8:12]
            em.tt("dve", sm[:, 12:16], dt, ah_bc, ALU.mult)
            da = sm[:, 12:16]
            R = G()
            em.tt("dve", R[:, 0:512].re("p (h t) -> p h t", h=4), cx.mi.un(1).bc([128, 4, 128]), da.un(2).bc([128, 4, 128]), ALU.mult)
            psD = P()
            em.mm(psD[:, :], cx.ls, R[:, 0:512])
            Lm = G()
            em.act(Lm[:, 0:512], psD[:, :], AF.Exp)
            PP(psD)
            PG(R)
            psm = P()
            em.mm(psm[:, 0:4], cx.mi, da)
            em.mm(psm[:, 4:8], cx.ls, da)
            em.mm(psm[:, 8:12], cx.ones, da)
            em.act(sm[:, 16:28], psm[:, 0:12], AF.Exp)
            ex = sm[:, 16:28]
            PP(psm)
            psC = P()
            em.mm(psC[:, 0:128], Bc[:, cs3], Cc[:, cs3])
            CBm = T()
            em.tt("dve", CBm[:, 0:128], psC[:, 0:128], cx.mi, ALU.mult)
            PP(psC)
            Mt = G()
            em.tt("dve", Mt[:, 0:512].re("p (h t) -> p h t", h=4), Lm[:, 0:512].re("p (h t) -> p h t", h=4),
                  CBm[:, 0:128].un(1).bc([128, 4, 128]), ALU.mult)
            PG(Lm)
            PT(CBm)
            xdt = T()
            h4 = lambda v_: v_.re("p (h c) -> p h c", h=4)
            em.tt("pool", h4(xdt[:, :]), h4(xtm[:, :]), dt.un(2).bc([128, 4, 64]), ALU.mult)
            psY = P()
            for h in range(4):
                em.mm(psY[:, h * 64:(h + 1) * 64], Mt[:, h * 128:(h + 1) * 128], xdt[:, h * 64:(h + 1) * 64])
            psO = P()
            em.mm(psO[:, 0:256], Cc[:, cs3], SM[:, :])
            t1, t2, y = T(), T(), T()
            em.tt("dve", h4(t1[:, :]), h4(psO[:, 0:256]), ex[:, 0:4].un(2).bc([128, 4, 64]), ALU.mult)
            PP(psO)
            em.tt("pool", t2[:, :], xtm[:, :], cx.ptm[:, PT_DS:PT_DS + 256], ALU.mult)
            em.tt("pool", t2[:, :], t2[:, :], t1[:, :], ALU.add)
            em.tt("dve", y[:, :], psY[:, 0:256], t2[:, :], ALU.add)
            PP(psY)
            PG(Mt)
            xdd = t1
            em.tt("pool", h4(xdd[:, :]), h4(xdt[:, :]), ex[:, 4:8].un(2).bc([128, 4, 64]), ALU.mult)
            psS = P()
            em.mm(psS[:, 0:256], btm[:, 0:128], xdd[:, :])
            em.tt("pool", h4(t2[:, :]), h4(SM[:, :]), ex[:, 8:12].un(2).bc([128, 4, 64]), ALU.mult)
            em.tt("dve", SMn[:, :], t2[:, :], psS[:, 0:256], ALU.add)
            PP(psS)
            PT(xdt, btm)
            em.tt("dve", y[:, :], y[:, :], zs[ci][:, :], ALU.mult)
            em.act(cx.junk[:, :], y[:, :], AF.Square, accum=sm[:, 28:29])
            em.act(sm[:, 29:30], sm[:, 28:29], AF.Sqrt, bias=cx.eps_ln[:, 2:3], scale=1.0 / 256.0)
            em.recip(sm[:, 30:31], sm[:, 29:30])
            em.stt("dve", y[:, :], y[:, :], sm[:, 30:31], cx.ptm[:, PT_NW:PT_NW + 256], ALU.mult, ALU.mult)
            psT2 = P()
            em.tr(psT2[:, 0:128], y[:, 0:128], cx.ident)
            em.tr(psT2[:, 128:256], y[:, 128:256], cx.ident)
            em.copy("act", yT[:, 2:4, cs], psT2[:, 0:256].re("p (a t) -> p a t", a=2))
            PP(psT2)
            PT(t1, t2, y, xtm, zs[ci])
            cx.small.put(sm)
        PG(xs0, xs1, Bc, Cc)
        chk(4)
        qr, kr, g0, g1_, gkr = raw[11], raw[12], raw[13], raw[14], raw[15]
        D3 = slice(3, 515)
        ps = P()
        em.mm(ps[:, :], cx.smallw[0:16, 256:384], gkr[0:16, D3])
        e_ = G()
        em.act(e_[:, D3], ps[:, :], AF.Exp, bias=cx.pfm[:, NPF + 2:NPF + 3], scale=-1.0)
        PP(ps)
        em.act(e_[:, D3], e_[:, D3], AF.Ln, bias=1.0)
        chk(4.1)
        Gc = G()
        em.scan(Gc[:, D3], cx.rst, e_[:, D3], 0.0, ALU.mult, ALU.add)
        PG(e_, gkr)
        eg, eng_ = G(), G()
        em.act(eg[:, D3], Gc[:, D3], AF.Exp, scale=-1.0 / 16.0)
        em.act(eng_[:, D3], Gc[:, D3], AF.Exp, scale=1.0 / 16.0)
        kg, kdT = G(), G()
        QB = cx.QB
        for h in range(2):
            pb = slice(64 * h, 64 * h + 64)
            em.stt("dve", QB[pb, :, h, :], qr[pb, D3].re("p (c t) -> p c t", c=4), 0.125,
                   eg[pb, D3].re("p (c t) -> p c t", c=4), ALU.mult, ALU.mult)
        em.tt("pool", kg[:, D3], kr[:, D3], eng_[:, D3], ALU.mult)
        smg = cx.small.get()
        em.ts("dve", smg[:, 0:4], lastc(Gc), -1.0 / 16.0, ALU.mult)
        for ci in range(4):
            cs3 = slice(3 + ci * 128, 3 + (ci + 1) * 128)
            em.act(kdT[:, cs3], Gc[:, cs3], AF.Exp, bias=smg[:, ci:ci + 1], scale=1.0 / 16.0)
        em.tt("pool", kdT[:, D3], kdT[:, D3], kr[:, D3], ALU.mult)
        em.copy("dve", smg[:, 4:8], lastc(eg))
        PG(Gc, eg, eng_, qr, kr)
        sgs = []
        for gt in (g0, g1_):
            em.act(gt[:, D3], gt[:, D3], AF.Silu)
            em.ts("dve", gt[:, D3], gt[:, D3], pf(PF_GNW), ALU.mult)
            sgs.append(gt)
        chk(4.2)
        for ci in range(4):
            cs3 = slice(3 + ci * 128, 3 + (ci + 1) * 128)
            cs = slice(ci * 128, (ci + 1) * 128)
            SG = cx.SG[sgp]
            SGn = cx.SG[1 - sgp]
            sgp = 1 - sgp
            psT = P()
            em.tr(psT[:, 0:128], kdT[:, cs3], cx.ident)
            kd_ = T()
            em.copy("act", kd_[:, 0:128], psT[:, 0:128])
            PP(psT)
            chk(4.3)
            psA = P()
            em.mm(psA[:, 0:256], kg[:, cs3], QB[:, ci, :, :].re("p h t -> p (h t)"))
            attT = T()
            em.tt("dve", attT[:, :].re("p (h t) -> p h t", h=2), psA[:, 0:256].re("p (h t) -> p h t", h=2),
                  cx.mi.un(1).bc([128, 2, 128]), ALU.mult)
            PP(psA)
            chk(4.4)
            psO = P()
            for h in range(2):
                pb = slice(64 * h, 64 * h + 64)
                hs = slice(h * 128, (h + 1) * 128)
                em.mm(psO[:, hs], vg[ci][:, hs], attT[:, hs], start=True, stop=False)
                em.mm(psO[:, hs], SG[:, :], QB[:, ci, h, :], start=False, stop=True)
            chk(4.5)
            sq = T()
            em.act(sq[:, :], psO[:, 0:256], AF.Square)
            psN = P()
            em.mm(psN[:, 0:256], cx.ones, sq[:, :])
            em.act(sq[:, :], psN[:, 0:256], AF.Sqrt, bias=cx.eps_ln[:, 3:4], scale=1.0 / 128.0)
            PP(psN)
            em.recip(sq[:, :], sq[:, :])
            chk(4.6)
            o = T()
            em.tt("dve", o[:, :], psO[:, 0:256], sq[:, :], ALU.mult)
            PP(psO)
            for h in range(2):
                em.tt("pool", yT[:, 4 + h, cs], o[:, h * 128:(h + 1) * 128], sgs[h][:, cs3], ALU.mult)
            chk(4.7)
            psS = P()
            em.mm(psS[:, 0:256], kd_[:, 0:128], vg[ci][:, :])
            for h in range(2):
                pb = slice(64 * h, 64 * h + 64)
                em.stt("dve", SGn[pb, :], SG[pb, :], smg[pb, 4 + ci:5 + ci], psS[pb, h * 128:(h + 1) * 128], ALU.mult, ALU.add)
            PP(psS)
            PT(kd_, attT, sq, o, vg[ci])
        cx.small.put(smg)
        PG(kg, kdT, g0, g1_)
        em.dma("sp", y_dst(blk), yT[:, :, :])


def pack_phaseA(inp, l, hh):
    w_in = inp["w_in"][l]
    cols = []
    o_rw = 0
    for base in (0, 512, 1024):
        cols.append(np.arange(base + 256 * hh, base + 256 * hh + 256))
    cols.append(np.arange(1536, 1664))
    o_xbc = 2176
    cols.append(np.arange(o_xbc + 256 * hh, o_xbc + 256 * hh + 256))
    cols.append(np.arange(o_xbc + 512 + 128 * hh, o_xbc + 512 + 128 * hh + 128))
    cols.append(np.arange(o_xbc + 768 + 128 * hh, o_xbc + 768 + 128 * hh + 128))
    cols.append(np.arange(3720 + 128 * hh, 3720 + 128 * hh + 128))
    cols.append(np.arange(3976 + 128 * hh, 3976 + 128 * hh + 128))
    cols.append(np.arange(4760 + 256 * hh, 4760 + 256 * hh + 256))
    cols.append(np.arange(4744, 4760))
    cols.append(np.arange(1664 + 256 * hh, 1664 + 256 * hh + 256))
    cols.append(np.arange(3208 + 256 * hh, 3208 + 256 * hh + 256))
    cols.append(np.arange(4232 + 256 * hh, 4232 + 256 * hh + 256))
    cols.append(np.arange(3200 + 4 * hh, 3200 + 4 * hh + 4))
    cols = np.concatenate(cols)
    assert cols.shape[0] == NCOL_A
    wA = np.ascontiguousarray(w_in[:, cols])
    pfm = np.zeros((128, NPF), np.float32)
    mu = inp["rw_mu"][l]
    ch = 256 * hh
    for i in range(2):
        sl = slice(ch + 128 * i, ch + 128 * i + 128)
        pfm[:, PF_MU + 0 + i] = mu[0:512][sl]
        pfm[:, PF_MU + 2 + i] = mu[512:1024][sl]
        pfm[:, PF_MU + 4 + i] = mu[1024:1536][sl]
        pfm[:, PF_W0 + i] = inp["rw_w0"][l][sl]
        pfm[:, PF_A0 + i] = inp["rw_a0"][l][sl]
        pfm[:, PF_KK + i] = inp["rw_k_k"][l][sl]
        pfm[:, PF_KA + i] = inp["rw_k_a"][l][sl]
        pfm[:, PF_RK + i] = inp["rw_r_k"][l].reshape(512)[sl]
    pfm[:, PF_MU + 6] = mu[1536:1664]
    cw = inp["m2_conv_w"][l]
    cb = inp["m2_conv_b"][l]
    xidx = [np.arange(ch, ch + 128), np.arange(ch + 128, ch + 256),
            np.arange(512 + 128 * hh, 512 + 128 * hh + 128), np.arange(768 + 128 * hh, 768 + 128 * hh + 128)]
    for q in range(4):
        for j in range(4):
            pfm[:, PF_CW + q * 4 + j] = cw[j, xidx[q]]
        pfm[:, PF_CB + q] = cb[xidx[q]]
    pfm[:, PF_GKB] = inp["gla_gk_b"][l][128 * hh:128 * hh + 128]
    pfm[:, PF_GNW] = inp["gla_norm_w"][l]
    ptm = np.zeros((1, NPT), np.float32)
    ptm[0, PT_LNG:PT_LNG + 256] = inp["rw_ln_g"][l][ch:ch + 256]
    ptm[0, PT_LNB:PT_LNB + 256] = inp["rw_ln_b"][l][ch:ch + 256]
    ptm[0, PT_NW:PT_NW + 256] = inp["m2_norm_w"][l][ch:ch + 256]
    ptm[0, PT_DS:PT_DS + 256] = np.repeat(inp["m2_d_skip"][l][4 * hh:4 * hh + 4], 64)
    ptm[0, PT_DTB:PT_DTB + 4] = inp["m2_dt_bias"][l][4 * hh:4 * hh + 4]
    ptm[0, PT_ALOG:PT_ALOG + 4] = inp["m2_a_log"][l][4 * hh:4 * hh + 4]
    smallw = np.zeros((128, 384), np.float32)
    smallw[0:64, 0:256] = inp["rw_w_up"][l][:, ch:ch + 256]
    smallw[64:128, 0:256] = inp["rw_a_up"][l][:, ch:ch + 256]
    smallw[0:16, 256:384] = inp["gla_gk_up"][l][:, 128 * hh:128 * hh + 128]
    return wA, pfm, ptm, smallw


def pack_ada(inp, b, L):
    cvec = np.ascontiguousarray(inp["c"][b].reshape(8, 128).T)
    adab_fm = np.stack([np.ascontiguousarray(inp["ada_b"][l][0:2048].reshape(16, 128).T) for l in range(L)])
    adab_fm = np.concatenate([adab_fm, np.zeros((L, 128, 8), np.float32)], axis=2)
    adab_g = np.stack([inp["ada_b"][l][2048:3072].reshape(1, 1024) for l in range(L)])
    return cvec, adab_fm, adab_g


def setup_phaseB(cx, S):
    nc, es, em = cx.nc, cx.es, cx.em
    cx.wM = sb(nc, es, "wM", [128, 8, 3072], BF16)
    cx.wbr = sb(nc, es, "wbr", [128, 12, 1024], BF16)
    cx.wout = sb(nc, es, "wout", [128, 8, 1024], BF16)
    cx.pbt = sb(nc, es, "pbt", [128, 2048])
    cx.small = Pool(nc, es, "smB", 8, [128, 32], F32)
    cx.xtB = [sb(nc, es, "xtB%d" % i, [128, 1024]) for i in range(4)]
    cx.xn = sb(nc, es, "xnB", [128, 1024], BF16)
    cx.lntmp = sb(nc, es, "lntmpB", [128, 1024])
    cx.hT = sb(nc, es, "hTB", [128, 8, TB], BF16)
    cx.yb = sb(nc, es, "yb", [128, 12, TB], BF16)
    cx.mT = sb(nc, es, "mT", [128, 8, TB], BF16)
    cx.wk = Pool(nc, es, "wk", 6, [128, 512], F32)
    cx.res = [sb(nc, es, "res%d" % i, [128, 1024]) for i in range(2)]
    cx.eps_ln = sb(nc, es, "epslnB", [128, 4])
    em.memset("dve", cx.eps_ln[:, 0:1], LN_EPS)


def phaseB(cx, dr, l, S, x_src, y_src, out_dst):
    nc, es, em = cx.nc, cx.es, cx.em
    P, PP = cx.ps.get, cx.ps.put
    for kc in range(8):
        em.dma("pool", cx.wM[:, kc, :], dr["wM"][l][kc * 128:(kc + 1) * 128, :])
        em.dma("pool", cx.wout[:, kc, :], dr["wout"][l][kc * 128:(kc + 1) * 128, :])
    for kt in range(12):
        em.dma("pool", cx.wbr[:, kt, :], dr["wbr"][l][kt * 128:(kt + 1) * 128, :])
    em.dma("sp", cx.pbt[:, :], dr["pbt"][l].partition_broadcast(128))
    g1 = cx.g1[l]
    nblk = S // TB
    for blk in range(nblk):
        for ti in range(4):
            r0 = blk * TB + ti * 128
            em.dma("sp", cx.xtB[ti][:, :], x_src(r0))
            emit_ln_transpose(cx, cx.xtB[ti], cx.hT, ti, cx.sc1[l], cx.sh[l], cx.xn)
        em.dma("sp", cx.yb[:, :, :], y_src(blk * TB))
        for f in range(8):
            fs = slice(f * 128, (f + 1) * 128)
            macc = cx.wk.get()
            for br in range(3):
                psL = P()
                for kc in range(8):
                    em.mm(psL[:, :], cx.wM[:, kc, br * 1024 + f * 128:br * 1024 + (f + 1) * 128], cx.hT[:, kc, :],
                          start=(kc == 0), stop=(kc == 7))
                g = cx.wk.get()
                em.act(g[:, :], psL[:, :], AF.Sigmoid)
                PP(psL)
                psY = P()
                kts = [br * 2, br * 2 + 1, 6 + br * 2, 6 + br * 2 + 1]
                for i, kt in enumerate(kts):
                    em.mm(psY[:, :], cx.wbr[:, kt, fs], cx.yb[:, kt, :], start=(i == 0), stop=(i == 3))
                if br == 0:
                    em.tt("dve", macc[:, :], psY[:, :], g[:, :], ALU.mult)
                elif br == 1:
                    em.tt("dve", g[:, :], psY[:, :], g[:, :], ALU.mult)
                    em.tt("pool", macc[:, :], macc[:, :], g[:, :], ALU.add)
                else:
                    em.tt("dve", g[:, :], psY[:, :], g[:, :], ALU.mult)
                    em.tt("pool", cx.mT[:, f, :], macc[:, :], g[:, :], ALU.add)
                PP(psY)
                cx.wk.put(g)
            cx.wk.put(macc)
        for ti in range(4):
            res = cx.res[ti % 2]
            xt = cx.xtB[ti]
            for half in range(2):
                hs = slice(half * 512, (half + 1) * 512)
                psR = P()
                for kc in range(8):
                    em.mm(psR[:, :], cx.mT[:, kc, ti * 128:(ti + 1) * 128], cx.wout[:, kc, hs], start=(kc == 0), stop=(kc == 7))
                em.tt("dve", res[:, hs], psR[:, :], g1[:, hs], ALU.mult)
                PP(psR)
            em.stt("dve", res[:, :], xt[:, :], ALPHA, res[:, :], ALU.mult, ALU.add)
            st = cx.small.get()
            for hf in range(2):
                em.bn_stats(st[:, hf * 6:(hf + 1) * 6], res[:, hf * 512:(hf + 1) * 512])
            em.bn_aggr(st[:, 12:14], st[:, 0:12].re("p (a b) -> p a b", a=2))
            em.act(st[:, 14:15], st[:, 13:14], AF.Sqrt, bias=cx.eps_ln[:, 0:1])
            em.recip(st[:, 15:16], st[:, 14:15])
            em.stt("dve", st[:, 16:17], st[:, 12:13], -1.0, st[:, 15:16], ALU.mult, ALU.mult)
            em.act(res[:, :], res[:, :], AF.Identity, bias=st[:, 16:17], scale=st[:, 15:16])
            em.tt("dve", res[:, :], res[:, :], cx.pbt[:, 0:1024], ALU.mult)
            em.tt("pool", res[:, :], res[:, :], cx.pbt[:, 1024:2048], ALU.add)
            cx.small.put(st)
            r0 = blk * TB + ti * 128
            em.dma("sp", out_dst[r0:r0 + 128, :], res[:, :])


def pack_phaseB(inp, l):
    wM = np.ascontiguousarray(inp["w_in"][l][:, 5272:8344])
    wb = inp["w_branch"][l]
    rows = []
    for hh in range(2):
        for br in range(3):
            rows.append(wb[br, 256 * hh:256 * hh + 256, :])
    wbr = np.ascontiguousarray(np.concatenate(rows, axis=0))
    wout = np.ascontiguousarray(inp["w_out"][l])
    pbt = np.concatenate([inp["post_g"][l], inp["post_b"][l]]).reshape(1, 2048).astype(np.float32)
    return wM, wbr, wout, pbt


def _din(nc, name, shape, dt=F32):
    return nc.dram_tensor(name, shape, dt, kind="ExternalInput").ap()


def build_prog_A(S):
    nc = bass.Bass("TRN2", target_bir_lowering=False)
    dr = dict(consts=_din(nc, "consts", [128, NCONST]), cvec=_din(nc, "cvec", [128, 8]), adab_fm=_din(nc, "adab_fm", [1, 128, 24]),
              adab_g=_din(nc, "adab_g", [1, 1, 1024]), ada_w=_din(nc, "ada_w", [1, 1024, 3072]), wA=_din(nc, "wA", [1, 1024, NCOL_A]),
              pfm=_din(nc, "pfm", [1, 128, NPF]), ptm=_din(nc, "ptm", [1, 1, NPT]), smallw=_din(nc, "smallw", [1, 128, 384]))
    x = _din(nc, "x", [S, 1024])
    yT = nc.dram_tensor("yT", [768, S], BF16, kind="ExternalOutput").ap()
    with ExitStack() as es:
        em = Em(nc, es)
        cx = setup_common(nc, es, em, dr)
        with ExitStack() as est:
            emit_ada(cx, dr, 1, est)
            em.barrier()
        with ExitStack() as esA:
            cx.es = esA
            setup_phaseA(cx, S)
            ytk = Tk(yT, "yT")
            phaseA(cx, dr, 0, S, lambda r0: x[r0:r0 + 128, :],
                   lambda blk: ytk[:, blk * TB:(blk + 1) * TB].re("(a p) t -> p a t", p=128))
            em.barrier()
    return nc


def build_prog_B(S):
    nc = bass.Bass("TRN2", target_bir_lowering=False)
    dr = dict(consts=_din(nc, "consts", [128, NCONST]), cvec=_din(nc, "cvec", [128, 8]), adab_fm=_din(nc, "adab_fm", [1, 128, 24]),
              adab_g=_din(nc, "adab_g", [1, 1, 1024]), ada_w=_din(nc, "ada_w", [1, 1024, 3072]),
              wM=_din(nc, "wM", [1, 1024, 3072]), wbr=_din(nc, "wbr", [1, 1536, 1024]), wout=_din(nc, "wout", [1, 1024, 1024]),
              pbt=_din(nc, "pbt", [1, 1, 2048]))
    x = _din(nc, "x", [S, 1024])
    yT = _din(nc, "yTin", [1536, S], BF16)
    out = nc.dram_tensor("out", [S, 1024], F32, kind="ExternalOutput").ap()
    with ExitStack() as es:
        em = Em(nc, es)
        cx = setup_common(nc, es, em, dr)
        with ExitStack() as est:
            emit_ada(cx, dr, 1, est)
            em.barrier()
        with ExitStack() as esB:
            cx.es = esB
            setup_phaseB(cx, S)
            phaseB(cx, dr, 0, S, lambda r0: x[r0:r0 + 128, :],
                   lambda c0: yT[:, c0:c0 + TB].rearrange("(a p) t -> p a t", p=128), Tk(out, "out"))
            em.barrier()
    return nc


def build_prog_fused(S, L=DEPTH):
    SB = S // 2
    CY = min(1024, SB)
    NCH = S // CY
    XR = min(512, SB)
    NXC = SB // XR
    nc = bass.Bass("TRN2", target_bir_lowering=False)
    dr = dict(consts=_din(nc, "consts", [128, NCONST]), cvec=_din(nc, "cvec", [128, 8]), adab_fm=_din(nc, "adab_fm", [L, 128, 24]),
              adab_g=_din(nc, "adab_g", [L, 1, 1024]), ada_w=_din(nc, "ada_w", [L, 1024, 3072]), wA=_din(nc, "wA", [L, 1024, NCOL_A]),
              pfm=_din(nc, "pfm", [L, 128, NPF]), ptm=_din(nc, "ptm", [L, 1, NPT]), smallw=_din(nc, "smallw", [L, 128, 384]),
              wM=_din(nc, "wM", [L, 1024, 3072]), wbr=_din(nc, "wbr", [L, 1536, 1024]), wout=_din(nc, "wout", [L, 1024, 1024]),
              pbt=_din(nc, "pbt", [L, 1, 2048]))
    x = _din(nc, "x", [S, 1024])
    xh = _din(nc, "xh", [SB, 1024])
    out = nc.dram_tensor("out", [SB, 1024], F32, kind="ExternalOutput").ap()
    ysend = nc.dram_tensor("ysend", [NCH, 768, CY], BF16).ap()
    yall = nc.dram_tensor("yall", [NCH, 1536, CY], BF16).ap()
    yh = nc.dram_tensor("yh_buf", [NCH // 2, 1536, CY], BF16).ap()
    o0 = nc.dram_tensor("o_half", [SB, 1024], F32).ap()
    x1 = nc.dram_tensor("x_next", [NXC, 2 * XR, 1024], F32).ap()
    groups = [[0, 1], [2, 3], [4, 5], [6, 7]]
    half_elems = (NCH // 2) * 1536 * CY
    FW = 16384
    hrows = half_elems // FW
    yall_f = yall.rearrange("c r t -> (c r t)").rearrange("(a f) -> a f", f=FW)
    yh_f = yh.rearrange("c r t -> (c r t)").rearrange("(a f) -> a f", f=FW)
    with ExitStack() as es:
        em = Em(nc, es)
        ybase = nc.scalar.snap((nc.scalar.partition_id() % 2) * hrows)
        cx = setup_common(nc, es, em, dr)
        with ExitStack() as est:
            emit_ada(cx, dr, L, est)
            em.barrier()

        def x1row(t0):
            row0 = ((t0 % SB) // XR) * (2 * XR) + (t0 // SB) * XR + (t0 % XR)
            return x1.rearrange("c r d -> (c r) d")[row0:row0 + 128, :]

        xsrcA = lambda r0: x[r0:r0 + 128, :]
        for l in range(L):
            ys_tk = Tk(ysend, "ysend%d" % l)
            with ExitStack() as esA:
                cx.es = esA
                setup_phaseA(cx, S)
                phaseA(cx, dr, l, S, xsrcA,
                       lambda blk, ys_tk=ys_tk: ys_tk[(blk * TB) // CY, :, (blk * TB) % CY:(blk * TB) % CY + TB].re("(a p) t -> p a t", p=128))
            em.collectives("AllGather", [(ysend[c], yall[c]) for c in range(NCH)], groups)
            yh_tk = Tk(yh, "yh%d" % l)
            em.dma("act", V(yh_tk, yh_f), yall_f[bass.ds(ybase, hrows), :])
            dst = out if l == L - 1 else o0
            if l == 0:
                xsrc = lambda r0: xh[r0:r0 + 128, :]
            else:
                xsrc = lambda r0: o0[r0:r0 + 128, :]
            with ExitStack() as esB:
                cx.es = esB
                setup_phaseB(cx, SB)
                phaseB(cx, dr, l, SB, xsrc,
                       lambda c0, yh_tk=yh_tk: yh_tk[c0 // CY, :, c0 % CY:c0 % CY + TB].re("(a p) t -> p a t", p=128), Tk(dst, "dst%d" % l))
            if l < L - 1:
                em.collectives("AllGather", [(o0[c * XR:(c + 1) * XR, :], x1[c]) for c in range(NXC)], groups)
                xsrcA = x1row
        em.barrier()
        print("fused ninst", em.ninst, flush=True)
    return nc


def pack_ada_l(inp, b, l):
    cvec = np.ascontiguousarray(inp["c"][b].reshape(8, 128).T)
    fm = np.ascontiguousarray(inp["ada_b"][l][0:2048].reshape(16, 128).T)
    adab_fm = np.concatenate([fm, np.zeros((128, 8), np.float32)], axis=1)[None]
    adab_g = np.ascontiguousarray(inp["ada_b"][l][2048:3072].reshape(1, 1, 1024))
    return cvec, adab_fm, adab_g


_PROGS = {}


def kernel(**inputs):
    inp = {k: np.asarray(v) for k, v in inputs.items()}
    B, S = inp["x"].shape[0], inp["x"].shape[1]
    SB = S // 2
    consts = make_consts()
    if "A" not in _PROGS:
        _PROGS["A"] = build_prog_A(S)
        _PROGS["B"] = build_prog_B(SB)
    progA, progB = _PROGS["A"], _PROGS["B"]
    x_cur = [np.ascontiguousarray(inp["x"][b]) for b in range(B)]
    ncore = 2 * B
    for l in range(DEPTH):
        packs = [pack_phaseA(inp, l, hh) for hh in range(2)]
        mapsA = []
        for core in range(ncore):
            b, hh = core // 2, core % 2
            wA, pfm, ptm, smallw = packs[hh]
            cvec, adab_fm, adab_g = pack_ada_l(inp, b, l)
            mapsA.append(dict(consts=consts, cvec=cvec, adab_fm=adab_fm, adab_g=adab_g, ada_w=inp["ada_w"][l:l + 1],
                              wA=wA[None], pfm=pfm[None], ptm=ptm[None], smallw=smallw[None], x=x_cur[b]))
        resA = run_bass_kernel_spmd(progA, mapsA, core_ids=list(range(ncore)))
        yT = [np.asarray(r["yT"]) for r in resA.results]
        wM, wbr, wout, pbt = pack_phaseB(inp, l)
        mapsB = []
        for core in range(ncore):
            b, th = core // 2, core % 2
            ts = slice(th * SB, (th + 1) * SB)
            yfull = np.ascontiguousarray(np.concatenate([yT[2 * b][:, ts], yT[2 * b + 1][:, ts]], axis=0))
            cvec, adab_fm, adab_g = pack_ada_l(inp, b, l)
            mapsB.append(dict(consts=consts, cvec=cvec, adab_fm=adab_fm, adab_g=adab_g, ada_w=inp["ada_w"][l:l + 1],
                              wM=wM[None], wbr=wbr[None], wout=wout[None], pbt=pbt[None],
                              x=np.ascontiguousarray(x_cur[b][ts]), yTin=yfull))
        resB = run_bass_kernel_spmd(progB, mapsB, core_ids=list(range(ncore)))
        outs = [np.asarray(r["out"]) for r in resB.results]
        x_cur = [np.ascontiguousarray(np.concatenate([outs[2 * b], outs[2 * b + 1]], axis=0)) for b in range(B)]
    return np.stack(x_cur).astype(np.float32)
```

```python
import numpy as np
from contextlib import ExitStack
import concourse.bass as bass
import concourse.mybir as mybir
from concourse.bass_utils import run_bass_kernel_spmd
import ml_dtypes

F32 = mybir.dt.float32
BF16 = mybir.dt.bfloat16
AF = mybir.ActivationFunctionType
ALU = mybir.AluOpType

D = 1024
W = 512
IN_WIDTH = 8344
DEPTH = 2
LN_EPS = 1e-5
RW_GN_EPS = 64e-5
M2_EPS = 1e-5
GLA_EPS = 1e-5
ALPHA = (2.0 * DEPTH) ** 0.25
C0 = float(np.exp(-0.5))
NFM = 16
NCOL_FM = 15 * 128 + 16
NCOL_TM = 772
NCOL_A = NCOL_FM + NCOL_TM
TB = 512
DBG_STOP = 99

CO_ID, CO_MS, CO_MI, CO_LS, CO_ONE, CO_BO, CO_IND, CO_RST = 0, 128, 256, 384, 512, 640, 768, 776
NCONST = CO_RST + 512
PF_MU, PF_W0, PF_A0, PF_KK, PF_KA, PF_RK, PF_CW, PF_CB, PF_GKB, PF_GNW = 0, 7, 9, 11, 13, 15, 17, 33, 37, 38
NPF = 39
PT_LNG, PT_LNB, PT_NW, PT_DS, PT_DTB, PT_ALOG = 0, 256, 512, 768, 1024, 1028
NPT = 1032


def make_consts():
    c = np.zeros((128, NCONST), np.float32)
    i = np.arange(128)
    c[:, CO_ID:CO_ID + 128] = np.eye(128)
    c[:, CO_MS:CO_MS + 128] = (i[:, None] < i[None, :])
    c[:, CO_MI:CO_MI + 128] = (i[:, None] <= i[None, :])
    c[:, CO_LS:CO_LS + 128] = (i[:, None] > i[None, :])
    c[:, CO_ONE:CO_ONE + 128] = 1.0
    c[:, CO_BO:CO_BO + 128] = ((i[:, None] // 64) == (i[None, :] // 64))
    c[:64, CO_IND + 0] = 1.0
    c[64:, CO_IND + 1] = 1.0
    r = np.ones(512, np.float32)
    r[::128] = 0.0
    c[:, CO_RST:CO_RST + 512] = r[None, :]
    return c


class Tk:
    def __init__(self, t, name="", psum=False):
        self.t = t
        self.w = {}
        self.r = {}
        self.name = name
        self.psum = psum

    def __getitem__(self, k):
        return V(self, self.t[k])


class V:
    def __init__(self, tk, ap):
        self.tk = tk
        self.ap = ap

    def __getitem__(self, k):
        return V(self.tk, self.ap[k])

    def re(self, s, **kw):
        return V(self.tk, self.ap.rearrange(s, **kw))

    def bc(self, shape):
        return V(self.tk, self.ap.to_broadcast(shape))

    def un(self, ax):
        return V(self.tk, self.ap.unsqueeze(ax))


def _ap(x):
    return x.ap if isinstance(x, V) else x


class Em:
    NDMA = 24

    def __init__(self, nc, es):
        self.nc = nc
        self.es = es
        self.eng = {"pe": nc.tensor, "act": nc.scalar, "dve": nc.vector, "pool": nc.gpsimd, "sp": nc.sync}
        self.semh = {}
        self.cnt = {}
        self.ep = -1
        self.new_epoch()
        self.dtot = []
        for i in range(self.NDMA):
            self.semh[("d", i)] = es.enter_context(nc.semaphore("s_d%d" % i))
            self.dtot.append(0)
        self.drr = 0
        self.waited = {}
        self.ninst = 0

    def new_epoch(self):
        self.ep += 1
        for e in ("pe", "act", "dve", "pool"):
            k = (e, self.ep)
            self.semh[k] = self.es.enter_context(self.nc.semaphore("s_%s_%d" % (e, self.ep)))
            self.cnt[k] = 0

    def _deps(self, e, Rs, Ws):
        deps = {}
        for v in Rs:
            if isinstance(v, V):
                for k, c in v.tk.w.items():
                    deps[k] = max(deps.get(k, 0), c)
                if v.tk.psum:
                    for k, c in v.tk.r.items():
                        if k[0] != e:
                            deps[k] = max(deps.get(k, 0), c)
        for v in Ws:
            if isinstance(v, V):
                for k, c in v.tk.w.items():
                    deps[k] = max(deps.get(k, 0), c)
                for k, c in v.tk.r.items():
                    deps[k] = max(deps.get(k, 0), c)
        for k, c in deps.items():
            if e == "pe" and k[0] == "pe":
                continue
            if self.waited.get((e, k), 0) >= c:
                continue
            self.eng[e].wait_ge(self.semh[k], c)
            self.waited[(e, k)] = c
            self.ninst += 1

    def op(self, e, fn, Rs, Ws):
        self._deps(e, Rs, Ws)
        inst = fn(self.eng[e])
        k = (e, self.ep)
        self.cnt[k] += 1
        inst.then_inc(self.semh[k], 1)
        c = self.cnt[k]
        for v in Rs:
            if isinstance(v, V):
                v.tk.r[k] = c
        for v in Ws:
            if isinstance(v, V):
                v.tk.w[k] = c
        self.ninst += 1
        return inst

    def dma(self, q, out, in_, **kw):
        self._deps(q, [in_], [out])
        s = self.drr
        self.drr = (self.drr + 1) % self.NDMA
        k = ("d", s)
        if self.waited.get((q, k), 0) < self.dtot[s]:
            self.eng[q].wait_ge(self.semh[k], self.dtot[s])
            self.waited[(q, k)] = self.dtot[s]
        self.eng[q].dma_start(out=_ap(out), in_=_ap(in_), **kw).then_inc(self.semh[k], 16)
        self.dtot[s] += 16
        if isinstance(in_, V):
            in_.tk.r[k] = self.dtot[s]
        if isinstance(out, V):
            out.tk.w[k] = self.dtot[s]
        self.ninst += 1

    def wait_all(self, q, tks):
        for tk in tks:
            for k, c in list(tk.w.items()):
                if self.waited.get((q, k), 0) < c:
                    self.eng[q].wait_ge(self.semh[k], c)
                    self.waited[(q, k)] = c

    def barrier(self):
        for e in ("pe", "act", "dve", "pool", "sp"):
            for k in list(self.semh.keys()):
                c = self.cnt[k] if k in self.cnt else self.dtot[k[1]]
                if c > 0 and self.waited.get((e, k), 0) < c:
                    self.eng[e].wait_ge(self.semh[k], c)
                    self.waited[(e, k)] = c
        self.new_epoch()

    def collectives(self, kind, pairs, groups):
        if not hasattr(self, "ccsem"):
            self.ccsem = self.es.enter_context(self.nc.semaphore("s_cc"))
            self.ccn = 0
        self.barrier()
        for in_ap, out_ap in pairs:
            self.nc.gpsimd.collective_compute(kind, ALU.bypass, replica_groups=groups, ins=[in_ap], outs=[out_ap]).then_inc(self.ccsem, 1)
            self.ccn += 1
        for e in ("pe", "act", "dve", "pool", "sp"):
            self.eng[e].wait_ge(self.ccsem, self.ccn)

    def mm(self, out, lhsT, rhs, start=True, stop=True):
        return self.op("pe", lambda e: e.matmul(out.ap, lhsT.ap, rhs.ap, start=start, stop=stop, skip_group_check=True), [lhsT, rhs], [out])

    def tr(self, out, in_, ident):
        return self.op("pe", lambda e: e.transpose(out.ap, in_.ap, ident.ap), [in_, ident], [out])

    def act(self, out, in_, func, bias=None, scale=1.0, accum=None, eng="act"):
        kw = {}
        if bias is not None:
            kw["bias"] = _ap(bias)
        if accum is not None:
            kw["accum_out"] = accum.ap
        kw["scale"] = _ap(scale)
        Ws = [out] + ([accum] if accum is not None else [])
        return self.op(eng, lambda e: e.activation(out.ap, in_.ap, func, **kw), [in_, bias, scale], Ws)

    def tt(self, eng, out, a, b, op):
        return self.op(eng, lambda e: e.tensor_tensor(out.ap, a.ap, b.ap, op), [a, b], [out])

    def ts(self, eng, out, a, s1, op0, s2=None, op1=None):
        if op1 is None:
            return self.op(eng, lambda e: e.tensor_scalar(out.ap, a.ap, _ap(s1), None, op0), [a, s1], [out])
        return self.op(eng, lambda e: e.tensor_scalar(out.ap, a.ap, _ap(s1), _ap(s2), op0, op1), [a, s1, s2], [out])

    def stt(self, eng, out, a, s, b, op0, op1):
        return self.op(eng, lambda e: e.scalar_tensor_tensor(out.ap, a.ap, _ap(s), b.ap, op0, op1), [a, s, b], [out])

    def copy(self, eng, out, a):
        if eng == "act":
            return self.op(eng, lambda e: e.copy(out.ap, a.ap), [a], [out])
        return self.op(eng, lambda e: e.tensor_copy(out.ap, a.ap), [a], [out])

    def scan(self, out, d0, d1, init, op0, op1):
        return self.op("dve", lambda e: e.tensor_tensor_scan(out.ap, d0.ap, d1.ap, init, op0, op1), [d0, d1], [out])

    def memset(self, eng, out, val):
        return self.op(eng, lambda e: e.memset(out.ap, val), [], [out])

    def recip(self, out, a):
        return self.op("dve", lambda e: e.reciprocal(out.ap, a.ap), [a], [out])

    def bn_stats(self, out, a):
        return self.op("dve", lambda e: e.bn_stats(out.ap, a.ap), [a], [out])

    def bn_aggr(self, out, a):
        return self.op("dve", lambda e: e.bn_aggr(out.ap, a.ap), [a], [out])


class Pool:
    def __init__(self, nc, es, name, n, shape, dtype, psum=False):
        self.free = []
        self.name = name
        _UID[0] += 1
        name = "%s_%d_" % (name, _UID[0])
        for i in range(n):
            if psum:
                t = es.enter_context(nc.psum_tensor("%s%d" % (name, i), shape, dtype))
            else:
                t = es.enter_context(nc.sbuf_tensor("%s%d" % (name, i), shape, dtype))
            self.free.append(Tk(t, "%s%d" % (name, i), psum=psum))
        self.n = n
        self.minfree = n

    def get(self):
        assert self.free, "pool %s exhausted" % self.name
        tk = self.free.pop(0)
        self.minfree = min(self.minfree, len(self.free))
        return tk

    def put(self, *tks):
        for tk in tks:
            assert tk not in self.free
            self.free.append(tk)


_UID = [0]


def sb(nc, es, name, shape, dtype=F32):
    _UID[0] += 1
    return Tk(es.enter_context(nc.sbuf_tensor("sb%d_%s" % (_UID[0], name), shape, dtype)), name)


class Ctx:
    pass


class _Stop(Exception):
    pass


def chk(x):
    if DBG_STOP <= x:
        raise _Stop()


def lastc(t):
    return t[:, 3:515].re("p (c t) -> p c t", c=4)[:, :, 127]


def setup_common(nc, es, em, dr):
    cx = Ctx()
    cx.nc, cx.es, cx.em = nc, es, em
    cx.const = sb(nc, es, "const", [128, NCONST])
    em.dma("sp", cx.const[:, :], dr["consts"])
    cx.identb = sb(nc, es, "identb", [128, 128], BF16)
    em.copy("dve", cx.identb[:, :], cx.const[:, CO_ID:CO_ID + 128])
    cx.ps = Pool(nc, es, "ps", 7, [128, 512], F32, psum=True)
    cx.psb = Tk(es.enter_context(nc.psum_tensor("psb", [128, 1024], BF16)), "psb", psum=True)
    cx.ident = cx.const[:, CO_ID:CO_ID + 128]
    cx.ms = cx.const[:, CO_MS:CO_MS + 128]
    cx.mi = cx.const[:, CO_MI:CO_MI + 128]
    cx.mask2 = cx.const[:, CO_MS:CO_MS + 256]
    cx.ls = cx.const[:, CO_LS:CO_LS + 128]
    cx.ones = cx.const[:, CO_ONE:CO_ONE + 128]
    cx.bones = cx.const[:, CO_BO:CO_BO + 128]
    cx.ind2 = cx.const[:, CO_IND:CO_IND + 2]
    cx.rst = cx.const[:, CO_RST:CO_RST + 512]
    return cx


def emit_ada(cx, dr, L, est):
    nc, es, em = cx.nc, cx.es, cx.em
    cx.sc1 = [sb(nc, es, "sc1fm%d" % l, [128, 8]) for l in range(L)]
    cx.sh = [sb(nc, es, "shfm%d" % l, [128, 8]) for l in range(L)]
    cx.g1 = [sb(nc, es, "g1bc%d" % l, [128, 1024]) for l in range(L)]
    cvec = sb(nc, est, "cvec", [128, 8])
    em.dma("sp", cvec[:, :], dr["cvec"])
    sc = sb(nc, est, "sc", [128, 8])
    em.act(sc[:, :], cvec[:, :], AF.Silu)
    scb = sb(nc, est, "scb", [128, 8, 128])
    em.copy("dve", scb[:, :, :], sc[:, :].un(2).bc([128, 8, 128]))
    adaw = [sb(nc, est, "adaw%d" % i, [128, 3072]) for i in range(2)]
    adab_fm = sb(nc, est, "adabfm", [128, 24])
    adab_bc = sb(nc, est, "adabbc", [128, 1024])
    for l in range(L):
        em.dma("sp", adab_fm[:, :], dr["adab_fm"][l])
        em.dma("sp", adab_bc[:, :], dr["adab_g"][l].partition_broadcast(128))
        psm = cx.ps.get()
        psg0 = cx.ps.get()
        psg1 = cx.ps.get()
        em.memset("dve", psm[:, 0:16], 0.0)
        for kc in range(8):
            aw = adaw[kc % 2]
            em.dma("sp", aw[:, :], dr["ada_w"][l, kc * 128:(kc + 1) * 128, :])
            for j in range(16):
                em.mm(psm[:, j:j + 1], aw[:, j * 128:(j + 1) * 128], sc[:, kc:kc + 1], start=False, stop=(kc == 7))
            em.mm(psg0[:, :], scb[:, kc, :], aw[:, 2048:2560], start=(kc == 0), stop=(kc == 7))
            em.mm(psg1[:, :], scb[:, kc, :], aw[:, 2560:3072], start=(kc == 0), stop=(kc == 7))
        sh, sc1, g1 = cx.sh[l], cx.sc1[l], cx.g1[l]
        em.tt("dve", sh[:, :], psm[:, 0:8], adab_fm[:, 0:8], ALU.add)
        em.tt("dve", sc1[:, :], psm[:, 8:16], adab_fm[:, 8:16], ALU.add)
        em.ts("dve", sc1[:, :], sc1[:, :], 1.0, ALU.add)
        em.tt("dve", g1[:, 0:512], psg0[:, :], adab_bc[:, 0:512], ALU.add)
        em.tt("dve", g1[:, 512:1024], psg1[:, :], adab_bc[:, 512:1024], ALU.add)
        em.ts("dve", g1[:, :], g1[:, :], 1.0, ALU.add)
        cx.ps.put(psm, psg0, psg1)


def emit_ln_transpose(cx, xt, hT, ti, sc1, sh, scr):
    em = cx.em
    st = cx.small.get()
    for hf in range(2):
        em.bn_stats(st[:, hf * 6:(hf + 1) * 6], xt[:, hf * 512:(hf + 1) * 512])
    em.bn_aggr(st[:, 12:14], st[:, 0:12].re("p (a b) -> p a b", a=2))
    em.act(st[:, 14:15], st[:, 13:14], AF.Sqrt, bias=cx.eps_ln[:, 0:1])
    em.recip(st[:, 15:16], st[:, 14:15])
    em.stt("dve", st[:, 16:17], st[:, 12:13], -1.0, st[:, 15:16], ALU.mult, ALU.mult)
    xn = scr
    em.act(xn[:, :], xt[:, :], AF.Identity, bias=st[:, 16:17], scale=st[:, 15:16])
    for kc in range(8):
        em.tr(cx.psb[:, kc * 128:(kc + 1) * 128], xn[:, kc * 128:(kc + 1) * 128], cx.identb[:, :])
    tmp = cx.lntmp
    em.tt("dve", tmp[:, :].re("p (k t) -> p k t", k=8), cx.psb[:, :].re("p (k t) -> p k t", k=8),
          sc1[:, :].un(2).bc([128, 8, 128]), ALU.mult)
    em.tt("pool", hT[:, :, ti * 128:(ti + 1) * 128], tmp[:, :].re("p (k t) -> p k t", k=8),
          sh[:, :].un(2).bc([128, 8, 128]), ALU.add)
    cx.small.put(st)
    return st


def setup_phaseA(cx, S):
    nc, es, em = cx.nc, cx.es, cx.em
    cx.wA = sb(nc, es, "wA", [128, 8, NCOL_A], BF16)
    cx.pfm = sb(nc, es, "pfm", [128, NPF + 8])
    cx.ptm = sb(nc, es, "ptm", [128, NPT + 8])
    cx.smallw = sb(nc, es, "smallw", [128, 384])
    cx.big = Pool(nc, es, "big", 27, [128, 515], F32)
    cx.tm = Pool(nc, es, "tm", 38, [128, 256], F32)
    cx.small = Pool(nc, es, "sm", 8, [128, 32], F32)
    cx.xt = [sb(nc, es, "xt%d" % i, [128, 1024]) for i in range(2)]
    cx.xn = sb(nc, es, "xn", [128, 1024], BF16)
    cx.lntmp = sb(nc, es, "lntmp", [128, 1024])
    cx.hT = sb(nc, es, "hT", [128, 8, TB], BF16)
    cx.AR = [sb(nc, es, "AR%d" % i, [128, 4, 256]) for i in range(2)]
    cx.carry = sb(nc, es, "carry", [128, NFM, 3])
    cx.yT = [sb(nc, es, "yT%d" % i, [128, 6, TB], BF16) for i in range(2)]
    cx.ST = [[sb(nc, es, "ST%d_%d" % (ct, i), [128, 128]) for i in range(2)] for ct in range(2)]
    cx.SM = [sb(nc, es, "SM%d" % i, [128, 256]) for i in range(2)]
    cx.SG = [sb(nc, es, "SG%d" % i, [128, 128]) for i in range(2)]
    cx.eps_ln = sb(nc, es, "epsln", [128, 4])
    em.memset("dve", cx.eps_ln[:, 0:1], LN_EPS)
    em.memset("dve", cx.eps_ln[:, 1:2], RW_GN_EPS)
    em.memset("dve", cx.eps_ln[:, 2:3], M2_EPS)
    em.memset("dve", cx.eps_ln[:, 3:4], GLA_EPS)
    cx.junk = sb(nc, es, "junk", [128, 256])
    cx.QB = sb(nc, es, "QB", [128, 4, 2, 128])
    em.memset("dve", cx.QB[:, :, :, :], 0.0)


def phaseA(cx, dr, l, S, x_src, y_dst):
    try:
        _phaseA(cx, dr, l, S, x_src, y_dst)
    except _Stop:
        cx.em.dma("sp", y_dst(0), cx.yT[0][:, :, :])


def _phaseA(cx, dr, l, S, x_src, y_dst):
    nc, es, em = cx.nc, cx.es, cx.em
    G, PG = cx.big.get, cx.big.put
    T, PT = cx.tm.get, cx.tm.put
    P, PP = cx.ps.get, cx.ps.put
    for kc in range(8):
        em.dma("pool", cx.wA[:, kc, :], dr["wA"][l][kc * 128:(kc + 1) * 128, :])
    em.dma("sp", cx.pfm[:, 0:NPF], dr["pfm"][l])
    em.dma("sp", cx.ptm[:, 0:NPT], dr["ptm"][l].partition_broadcast(128))
    em.dma("sp", cx.smallw[:, :], dr["smallw"][l])
    pf = lambda c: cx.pfm[:, c:c + 1]
    em.ts("dve", cx.pfm[:, NPF:NPF + 2], cx.pfm[:, PF_KA:PF_KA + 2], -1.0, ALU.mult, 1.0, ALU.add)
    em.ts("dve", cx.pfm[:, NPF + 2:NPF + 3], cx.pfm[:, PF_GKB:PF_GKB + 1], -1.0, ALU.mult)
    em.act(cx.ptm[:, NPT:NPT + 4], cx.ptm[:, PT_ALOG:PT_ALOG + 4], AF.Exp)
    em.ts("dve", cx.ptm[:, NPT:NPT + 4], cx.ptm[:, NPT:NPT + 4], -1.0, ALU.mult)
    ah_bc = cx.ptm[:, NPT:NPT + 4]
    chk(0)
    em.memset("dve", cx.carry[:, :, :], 0.0)
    for ct in range(2):
        em.memset("dve", cx.ST[ct][0][:, :], 0.0)
        em.memset("dve", cx.ST[ct][1][:, :], 0.0)
    em.memset("dve", cx.SM[0][:, :], 0.0)
    em.memset("dve", cx.SG[0][:, :], 0.0)
    stp = [0, 0]
    smp = 0
    sgp = 0
    nblk = S // TB
    for blk in range(nblk):
        yT = cx.yT[blk % 2]
        for ti in range(4):
            xt = cx.xt[ti % 2]
            r0 = blk * TB + ti * 128
            em.dma("sp", xt[:, :], x_src(r0))
            emit_ln_transpose(cx, xt, cx.hT, ti, cx.sc1[l], cx.sh[l], cx.xn)
        chk(1)
        raw = []
        for j in range(NFM):
            nco = 128 if j < 15 else 16
            ps = P()
            for kc in range(8):
                em.mm(ps[0:nco, :], cx.wA[:, kc, j * 128:j * 128 + nco], cx.hT[:, kc, :], start=(kc == 0), stop=(kc == 7))
            g = G()
            need_hist = j <= 10
            if need_hist:
                em.copy("pool", g[0:nco, 0:3], cx.carry[0:nco, j, :])
            em.copy("act", g[0:nco, 3:515], ps[0:nco, :])
            if need_hist:
                em.copy("pool", cx.carry[0:nco, j, :], g[0:nco, 512:515])
            PP(ps)
            raw.append(g)
        rwg, zs, vg, dtr = [], [], [], []
        for ti in range(4):
            ps1, ps2 = P(), P()
            for kc in range(8):
                em.mm(ps1[:, :], cx.hT[:, kc, ti * 128:(ti + 1) * 128], cx.wA[:, kc, NCOL_FM:NCOL_FM + 512], start=(kc == 0), stop=(kc == 7))
            for kc in range(8):
                em.mm(ps2[:, 0:260], cx.hT[:, kc, ti * 128:(ti + 1) * 128], cx.wA[:, kc, NCOL_FM + 512:NCOL_FM + 772], start=(kc == 0), stop=(kc == 7))
            a, b_, c_, d_ = T(), T(), T(), cx.small.get()
            em.act(a[:, :], ps1[:, 0:256], AF.Silu)
            em.act(b_[:, :], ps1[:, 256:512], AF.Silu)
            em.copy("dve", c_[:, :], ps2[:, 0:256])
            em.tt("dve", d_[:, 0:4], ps2[:, 256:260], cx.ptm[:, PT_DTB:PT_DTB + 4], ALU.add)
            PP(ps1, ps2)
            rwg.append(a); zs.append(b_); vg.append(c_); dtr.append(d_)
        chk(2)
        wd = raw[6]
        wdl = G()
        dtmp = G()
        em.tt("dve", dtmp[:, 3:515], wd[:, 2:514], wd[:, 3:515], ALU.subtract)
        em.stt("dve", wdl[:, 3:515], dtmp[:, 3:515], pf(PF_MU + 6), wd[:, 3:515], ALU.mult, ALU.add)
        em.act(wdl[0:64, 3:515], wdl[0:64, 3:515], AF.Tanh)
        PG(dtmp, wd)
        chk(2.01)
        for ct in range(2):
            def lerp(rawt, mucol):
                d_ = G()
                o = G()
                em.tt("dve", d_[:, 3:515], rawt[:, 2:514], rawt[:, 3:515], ALU.subtract)
                em.stt("dve", o[:, 3:515], d_[:, 3:515], pf(mucol), rawt[:, 3:515], ALU.mult, ALU.add)
                PG(d_, rawt)
                return o
            r = lerp(raw[0 + ct], PF_MU + 0 + ct)
            k = lerp(raw[2 + ct], PF_MU + 2 + ct)
            v = lerp(raw[4 + ct], PF_MU + 4 + ct)
            chk(2.02)
            D3 = slice(3, 515)
            AR = cx.AR[ct]
            ps = P()
            em.mm(ps[:, :], cx.smallw[0:64, ct * 128:(ct + 1) * 128], wdl[0:64, D3])
            sg = G()
            em.act(sg[:, D3], ps[:, :], AF.Sigmoid, bias=pf(PF_W0 + ct))
            PP(ps)
            ps = P()
            em.mm(ps[:, :], cx.smallw[64:128, ct * 128:(ct + 1) * 128], wdl[64:128, D3])
            alr = G()
            em.act(alr[:, D3], ps[:, :], AF.Sigmoid, bias=pf(PF_A0 + ct))
            PP(ps)
            chk(2.03)
            CS = G()
            em.scan(CS[:, D3], cx.rst, sg[:, D3], 0.0, ALU.mult, ALU.add)
            CSx = G()
            em.tt("pool", CSx[:, D3], CS[:, D3], sg[:, D3], ALU.subtract)
            PG(sg)
            eL, enL, eLx, eD = G(), G(), G(), G()
            em.act(eL[:, D3], CS[:, D3], AF.Exp, scale=-C0)
            em.act(enL[:, D3], CS[:, D3], AF.Exp, scale=C0)
            em.act(eLx[:, D3], CSx[:, D3], AF.Exp, scale=-C0)
            PG(CSx)
            chk(2.04)
            sm = cx.small.get()
            em.ts("dve", sm[:, 0:4], lastc(CS), -C0, ALU.mult)
            for ci in range(4):
                cs3 = slice(3 + ci * 128, 3 + (ci + 1) * 128)
                em.act(eD[:, cs3], CS[:, cs3], AF.Exp, bias=sm[:, ci:ci + 1], scale=C0)
            em.copy("dve", sm[:, 4:8], lastc(eL))
            PCt = sm
            PG(CS)
            chk(2.05)
            kk = G()
            em.ts("dve", kk[:, D3], k[:, D3], pf(PF_KK + ct), ALU.mult)
            kk2 = G()
            em.act(kk2[:, D3], kk[:, D3], AF.Square)
            ps = P()
            em.mm(ps[:, :], cx.bones, kk2[:, D3])
            em.act(kk2[:, D3], ps[:, :], AF.Sqrt)
            PP(ps)
            em.ts("dve", kk2[:, D3], kk2[:, D3], 1e-12, ALU.max)
            em.recip(kk2[:, D3], kk2[:, D3])
            em.tt("dve", kk[:, D3], kk[:, D3], kk2[:, D3], ALU.mult)
            PG(kk2)
            chk(2.06)
            f = G()
            em.ts("pool", f[:, D3], alr[:, D3], pf(PF_KA + ct), ALU.mult, cx.pfm[:, NPF + ct:NPF + ct + 1], ALU.add)
            kp = G()
            em.tt("pool", kp[:, D3], k[:, D3], f[:, D3], ALU.mult)
            PG(f, k)
            bv = G()
            em.tt("dve", bv[:, D3], kk[:, D3], alr[:, D3], ALU.mult)
            PG(alr)
            chk(2.07)
            ARv = AR[:, :, :]
            em.stt("dve", AR[:, :, 0:128], kk[:, D3].re("p (c t) -> p c t", c=4), -1.0,
                   eLx[:, D3].re("p (c t) -> p c t", c=4), ALU.mult, ALU.mult)
            em.tt("pool", AR[:, :, 128:256], r[:, D3].re("p (c t) -> p c t", c=4), eL[:, D3].re("p (c t) -> p c t", c=4), ALU.mult)
            PG(kk, eLx, eL)
            chk(2.08)
            bt, kt, bd, kd = G(), G(), G(), G()
            em.tt("dve", bt[:, D3], bv[:, D3], enL[:, D3], ALU.mult)
            em.tt("pool", kt[:, D3], kp[:, D3], enL[:, D3], ALU.mult)
            em.tt("dve", bd[:, D3], bv[:, D3], eD[:, D3], ALU.mult)
            em.tt("pool", kd[:, D3], kp[:, D3], eD[:, D3], ALU.mult)
            PG(bv, enL, eD)
            rk = G()
            em.stt("dve", rk[:, D3], r[:, D3], pf(PF_RK + ct), kp[:, D3], ALU.mult, ALU.mult)
            PG(r, kp)
            chk(2.1)
            for cp in (0, 2):
                ctxs = {}
                for ci in (cp, cp + 1):
                    cs3 = slice(3 + ci * 128, 3 + (ci + 1) * 128)
                    cs = slice(ci * 128, (ci + 1) * 128)
                    psT = P()
                    em.tr(psT[:, 0:128], v[:, cs3], cx.ident)
                    chk(2.11)
                    em.tr(psT[:, 128:256], bd[:, cs3], cx.ident)
                    em.tr(psT[:, 256:384], kd[:, cs3], cx.ident)
                    em.tr(psT[:, 384:512], AR[:, ci, 0:128], cx.ident)
                    chk(2.12)
                    vtm, bdkd, Z0, Z1 = T(), T(), T(), T()
                    Zs = [Z0, Z1]
                    acol = [slice(0, 64), slice(64, 128)]
                    ucol = [slice(64, 128), slice(0, 64)]
                    em.copy("act", vtm[:, 0:128], psT[:, 0:128])
                    chk(2.13)
                    em.copy("act", bdkd[:, 0:256], psT[:, 128:384])
                    chk(2.14)
                    for h in range(2):
                        em.copy("act", Zs[h][:, acol[h]], psT[:, 384 + h * 64:384 + (h + 1) * 64])
                    PP(psT)
                    chk(2.2)
                    M1s, M2s, Pk, Qk, prevPQ = [], [], [None, None], [None, None], [None, None]
                    for h in range(2):
                        pb = slice(64 * h, 64 * h + 64)
                        psA = P()
                        em.mm(psA[:, 0:256], bt[pb, cs3], AR[pb, ci, :])
                        em.mm(psA[:, 256:512], kt[pb, cs3], AR[pb, ci, :])
                        M1, M2_ = T(), T()
                        em.tt("dve", M1[:, :], psA[:, 0:256], cx.mask2, ALU.mult)
                        em.tt("dve", M2_[:, :], psA[:, 256:512], cx.mask2, ALU.mult)
                        PP(psA)
                        psQ = P()
                        em.mm(psQ[:, 0:128], AR[pb, ci, 0:128], bt[pb, cs3])
                        em.mm(psQ[:, 128:192], M2_[:, 0:128], vtm[:, h * 64:(h + 1) * 64])
                        Q1 = T()
                        em.tt("dve", Q1[:, 0:128], psQ[:, 0:128], cx.ls, ALU.mult)
                        em.copy("act", Zs[h][:, ucol[h]], psQ[:, 128:192])
                        PP(psQ)
                        Pk[h], Qk[h], prevPQ[h] = M1[:, 0:128], Q1[:, 0:128], Q1
                        M1s.append(M1); M2s.append(M2_)
                    chk(2.3)
                    ctxs[ci] = dict(cs3=cs3, cs=cs, vtm=vtm, bdkd=bdkd, Z0=Z0, Z1=Z1, Zs=Zs, acol=acol, ucol=ucol, M1s=M1s, M2s=M2s, Pk=Pk, Qk=Qk, prevPQ=prevPQ)
                for lev in range(7):
                    for ci in (cp, cp + 1):
                        c_ = ctxs[ci]
                        Zs, Pk, Qk, prevPQ = c_['Zs'], c_['Pk'], c_['Qk'], c_['prevPQ']
                        for h in range(2):
                            Zh = Zs[h][:, 0:128]
                            psZ = P()
                            em.mm(psZ[:, 0:128], Pk[h], Zh)
                            em.tt("dve", Zh, psZ[:, 0:128], Zh, ALU.add)
                            PP(psZ)
                            if lev < 6:
                                psS = P()
                                em.mm(psS[:, 0:128], Qk[h], Pk[h])
                                PQ = T()
                                if lev < 5:
                                    em.mm(psS[:, 128:256], Pk[h], Qk[h])
                                    em.copy("act", PQ[:, 0:256], psS[:, 0:256])
                                else:
                                    em.copy("act", PQ[:, 0:128], psS[:, 0:128])
                                PP(psS)
                                PT(prevPQ[h])
                                prevPQ[h] = PQ
                                Pk[h], Qk[h] = PQ[:, 0:128], PQ[:, 128:256]
                for ci in (cp, cp + 1):
                    c_ = ctxs[ci]
                    PT(c_['prevPQ'][0], c_['prevPQ'][1])
                for ci in (cp, cp + 1):
                    c_ = ctxs[ci]
                    cs3, cs, vtm, bdkd, Z0, Z1, Zs, acol, ucol, M1s, M2s = (c_[k] for k in ('cs3', 'cs', 'vtm', 'bdkd', 'Z0', 'Z1', 'Zs', 'acol', 'ucol', 'M1s', 'M2s'))
                    ST = cx.ST[ct][stp[ct]]
                    STn = cx.ST[ct][1 - stp[ct]]
                    stp[ct] = 1 - stp[ct]
                    chk(2.4)
                    psX = P()
                    em.tr(psX[:, 0:128], Z0[:, 0:128], cx.ident)
                    em.tr(psX[:, 128:256], Z1[:, 0:128], cx.ident)
                    ApT = T()
                    em.copy("act", ApT[0:64, 0:128], psX[0:64, 0:128])
                    em.copy("act", ApT[64:128, 0:128], psX[64:128, 128:256])
                    PP(psX)
                    chk(2.41)
                    psU = P()
                    em.mm(psU[:, 0:128], ApT[:, 0:128], ST[:, :])
                    chk(2.42)
                    U2 = T()
                    for h in range(2):
                        em.tt("dve", U2[:, h * 64:(h + 1) * 64], psU[:, h * 64:(h + 1) * 64], Zs[h][:, ucol[h]], ALU.add)
                    PP(psU)
                    PT(ApT, Z0, Z1)
                    chk(2.5)
                    psY = P()
                    em.mm(psY[:, 0:128], AR[:, ci, 128:256], ST[:, :], start=True, stop=False)
                    for h in range(2):
                        hs = slice(h * 64, (h + 1) * 64)
                        em.mm(psY[:, hs], M1s[h][:, 128:256], U2[:, hs], start=False, stop=False)
                        em.mm(psY[:, hs], M2s[h][:, 128:256], vtm[:, hs], start=False, stop=True)
                    psS = P()
                    em.mm(psS[:, 0:128], bdkd[:, 0:128], U2[:, 0:128], start=True, stop=False)
                    em.mm(psS[:, 0:128], bdkd[:, 128:256], vtm[:, 0:128], start=False, stop=True)
                    for h in range(2):
                        pb = slice(64 * h, 64 * h + 64)
                        hs = slice(h * 64, (h + 1) * 64)
                        em.stt("dve", STn[pb, hs], ST[pb, hs], PCt[pb, 4 + ci:5 + ci], psS[pb, hs], ALU.mult, ALU.add)
                    PP(psS)
                    PT(M1s[0], M1s[1], M2s[0], M2s[1], bdkd, U2)
                    chk(2.6)
                    st = cx.small.get()
                    for h in range(2):
                        em.bn_stats(st[:, h * 6:(h + 1) * 6], psY[:, h * 64:(h + 1) * 64])
                    for h in range(2):
                        em.bn_aggr(st[:, 12 + 2 * h:14 + 2 * h], st[:, h * 6:(h + 1) * 6])
                    mv = st[:, 12:16].re("p (h two) -> p h two", h=2)
                    em.act(st[:, 16:18], mv[:, :, 1], AF.Sqrt, bias=cx.eps_ln[:, 1:2])
                    em.recip(st[:, 18:20], st[:, 16:18])
                    yn = T()
                    yn3 = yn[:, 0:128].re("p (h c) -> p h c", h=2)
                    em.tt("dve", yn3, psY[:, 0:128].re("p (h c) -> p h c", h=2), mv[:, :, 0:1].bc([128, 2, 64]), ALU.subtract)
                    PP(psY)
                    em.tt("dve", yn3, yn3, st[:, 18:20].un(2).bc([128, 2, 64]), ALU.mult)
                    em.tt("pool", yn[:, 0:128], yn[:, 0:128], cx.ptm[:, PT_LNG + ct * 128:PT_LNG + (ct + 1) * 128], ALU.mult)
                    em.tt("pool", yn[:, 0:128], yn[:, 0:128], cx.ptm[:, PT_LNB + ct * 128:PT_LNB + (ct + 1) * 128], ALU.add)
                    chk(2.7)
                    psB = P()
                    em.mm(psB[:, 0:2], rk[:, cs3], cx.ind2)
                    em.copy("act", st[:, 20:22], psB[:, 0:2])
                    PP(psB)
                    bo = T()
                    em.tt("dve", bo[:, 0:128].re("p (h c) -> p h c", h=2), vtm[:, 0:128].re("p (h c) -> p h c", h=2),
                          st[:, 20:22].un(2).bc([128, 2, 64]), ALU.mult)
                    em.tt("dve", yn[:, 0:128], yn[:, 0:128], bo[:, 0:128], ALU.add)
                    em.tt("dve", yn[:, 0:128], yn[:, 0:128], rwg[ci][:, ct * 128:(ct + 1) * 128], ALU.mult)
                    psT2 = P()
                    em.tr(psT2[:, 0:128], yn[:, 0:128], cx.ident)
                    em.copy("act", yT[:, ct, cs], psT2[:, 0:128])
                    PP(psT2)
                    PT(yn, bo, vtm)
                    cx.small.put(st)
            cx.small.put(PCt)
            PG(v, bt, kt, bd, kd, rk)
        PG(wdl)
        for ti in range(4):
            PT(rwg[ti])
        chk(3)
        conv = []
        for q_ in range(4):
            rt = raw[7 + q_]
            acc = G()
            cw = lambda j: pf(PF_CW + q_ * 4 + j)
            em.ts("pool", acc[:, 3:515], rt[:, 0:512], cw(0), ALU.mult, pf(PF_CB + q_), ALU.add)
            em.stt("dve", acc[:, 3:515], rt[:, 1:513], cw(1), acc[:, 3:515], ALU.mult, ALU.add)
            em.stt("dve", acc[:, 3:515], rt[:, 2:514], cw(2), acc[:, 3:515], ALU.mult, ALU.add)
            em.stt("dve", acc[:, 3:515], rt[:, 3:515], cw(3), acc[:, 3:515], ALU.mult, ALU.add)
            em.act(acc[:, 3:515], acc[:, 3:515], AF.Silu)
            PG(rt)
            conv.append(acc)
        xs0, xs1, Bc, Cc = conv
        for ci in range(4):
            cs3 = slice(3 + ci * 128, 3 + (ci + 1) * 128)
            cs = slice(ci * 128, (ci + 1) * 128)
            SM = cx.SM[smp]
            SMn = cx.SM[1 - smp]
            smp = 1 - smp
            psT = P()
            em.tr(psT[:, 0:128], xs0[:, cs3], cx.ident)
            em.tr(psT[:, 128:256], xs1[:, cs3], cx.ident)
            em.tr(psT[:, 256:384], Bc[:, cs3], cx.ident)
            xtm, btm = T(), T()
            em.copy("act", xtm[:, :], psT[:, 0:256])
            em.copy("dve", btm[:, 0:128], psT[:, 256:384])
            PP(psT)
            sm = dtr[ci]
            em.act(sm[:, 4:8], sm[:, 0:4], AF.Exp)
            em.act(sm[:, 8:12], sm[:, 4:8], AF.Ln, bias=1.0)
            dt = sm[:, 8:12]
            em.tt("dve", sm[:, 12:16], dt, ah_bc, ALU.mult)
            da = sm[:, 12:16]
            R = G()
            em.tt("dve", R[:, 0:512].re("p (h t) -> p h t", h=4), cx.mi.un(1).bc([128, 4, 128]), da.un(2).bc([128, 4, 128]), ALU.mult)
            psD = P()
            em.mm(psD[:, :], cx.ls, R[:, 0:512])
            Lm = G()
            em.act(Lm[:, 0:512], psD[:, :], AF.Exp)
            PP(psD)
            PG(R)
            psm = P()
            em.mm(psm[:, 0:4], cx.mi, da)
            em.mm(psm[:, 4:8], cx.ls, da)
            em.mm(psm[:, 8:12], cx.ones, da)
            em.act(sm[:, 16:28], psm[:, 0:12], AF.Exp)
            ex = sm[:, 16:28]
            PP(psm)
            psC = P()
            em.mm(psC[:, 0:128], Bc[:, cs3], Cc[:, cs3])
            CBm = T()
            em.tt("dve", CBm[:, 0:128], psC[:, 0:128], cx.mi, ALU.mult)
            PP(psC)
            Mt = G()
            em.tt("dve", Mt[:, 0:512].re("p (h t) -> p h t", h=4), Lm[:, 0:512].re("p (h t) -> p h t", h=4),
                  CBm[:, 0:128].un(1).bc([128, 4, 128]), ALU.mult)
            PG(Lm)
            PT(CBm)
            xdt = T()
            h4 = lambda v_: v_.re("p (h c) -> p h c", h=4)
            em.tt("pool", h4(xdt[:, :]), h4(xtm[:, :]), dt.un(2).bc([128, 4, 64]), ALU.mult)
            psY = P()
            for h in range(4):
                em.mm(psY[:, h * 64:(h + 1) * 64], Mt[:, h * 128:(h + 1) * 128], xdt[:, h * 64:(h + 1) * 64])
            psO = P()
            em.mm(psO[:, 0:256], Cc[:, cs3], SM[:, :])
            t1, t2, y = T(), T(), T()
            em.tt("dve", h4(t1[:, :]), h4(psO[:, 0:256]), ex[:, 0:4].un(2).bc([128, 4, 64]), ALU.mult)
            PP(psO)
            em.tt("pool", t2[:, :], xtm[:, :], cx.ptm[:, PT_DS:PT_DS + 256], ALU.mult)
            em.tt("pool", t2[:, :], t2[:, :], t1[:, :], ALU.add)
            em.tt("dve", y[:, :], psY[:, 0:256], t2[:, :], ALU.add)
            PP(psY)
            PG(Mt)
            xdd = t1
            em.tt("pool", h4(xdd[:, :]), h4(xdt[:, :]), ex[:, 4:8].un(2).bc([128, 4, 64]), ALU.mult)
            psS = P()
            em.mm(psS[:, 0:256], btm[:, 0:128], xdd[:, :])
            em.tt("pool", h4(t2[:, :]), h4(SM[:, :]), ex[:, 8:12].un(2).bc([128, 4, 64]), ALU.mult)
            em.tt("dve", SMn[:, :], t2[:, :], psS[:, 0:256], ALU.add)
            PP(psS)
            PT(xdt, btm)
            em.tt("dve", y[:, :], y[:, :], zs[ci][:, :], ALU.mult)
            em.act(cx.junk[:, :], y[:, :], AF.Square, accum=sm[:, 28:29])
            em.act(sm[:, 29:30], sm[:, 28:29], AF.Sqrt, bias=cx.eps_ln[:, 2:3], scale=1.0 / 256.0)
            em.recip(sm[:, 30:31], sm[:, 29:30])
            em.stt("dve", y[:, :], y[:, :], sm[:, 30:31], cx.ptm[:, PT_NW:PT_NW + 256], ALU.mult, ALU.mult)
            psT2 = P()
            em.tr(psT2[:, 0:128], y[:, 0:128], cx.ident)
            em.tr(psT2[:, 128:256], y[:, 128:256], cx.ident)
            em.copy("act", yT[:, 2:4, cs], psT2[:, 0:256].re("p (a t) -> p a t", a=2))
            PP(psT2)
            PT(t1, t2, y, xtm, zs[ci])
            cx.small.put(sm)
        PG(xs0, xs1, Bc, Cc)
        chk(4)
        qr, kr, g0, g1_, gkr = raw[11], raw[12], raw[13], raw[14], raw[15]
        D3 = slice(3, 515)
        ps = P()
        em.mm(ps[:, :], cx.smallw[0:16, 256:384], gkr[0:16, D3])
        e_ = G()
        em.act(e_[:, D3], ps[:, :], AF.Exp, bias=cx.pfm[:, NPF + 2:NPF + 3], scale=-1.0)
        PP(ps)
        em.act(e_[:, D3], e_[:, D3], AF.Ln, bias=1.0)
        chk(4.1)
        Gc = G()
        em.scan(Gc[:, D3], cx.rst, e_[:, D3], 0.0, ALU.mult, ALU.add)
        PG(e_, gkr)
        eg, eng_ = G(), G()
        em.act(eg[:, D3], Gc[:, D3], AF.Exp, scale=-1.0 / 16.0)
        em.act(eng_[:, D3], Gc[:, D3], AF.Exp, scale=1.0 / 16.0)
        kg, kdT = G(), G()
        QB = cx.QB
        for h in range(2):
            pb = slice(64 * h, 64 * h + 64)
            em.stt("dve", QB[pb, :, h, :], qr[pb, D3].re("p (c t) -> p c t", c=4), 0.125,
                   eg[pb, D3].re("p (c t) -> p c t", c=4), ALU.mult, ALU.mult)
        em.tt("pool", kg[:, D3], kr[:, D3], eng_[:, D3], ALU.mult)
        smg = cx.small.get()
        em.ts("dve", smg[:, 0:4], lastc(Gc), -1.0 / 16.0, ALU.mult)
        for ci in range(4):
            cs3 = slice(3 + ci * 128, 3 + (ci + 1) * 128)
            em.act(kdT[:, cs3], Gc[:, cs3], AF.Exp, bias=smg[:, ci:ci + 1], scale=1.0 / 16.0)
        em.tt("pool", kdT[:, D3], kdT[:, D3], kr[:, D3], ALU.mult)
        em.copy("dve", smg[:, 4:8], lastc(eg))
        PG(Gc, eg, eng_, qr, kr)
        sgs = []
        for gt in (g0, g1_):
            em.act(gt[:, D3], gt[:, D3], AF.Silu)
            em.ts("dve", gt[:, D3], gt[:, D3], pf(PF_GNW), ALU.mult)
            sgs.append(gt)
        chk(4.2)
        for ci in range(4):
            cs3 = slice(3 + ci * 128, 3 + (ci + 1) * 128)
            cs = slice(ci * 128, (ci + 1) * 128)
            SG = cx.SG[sgp]
            SGn = cx.SG[1 - sgp]
            sgp = 1 - sgp
            psT = P()
            em.tr(psT[:, 0:128], kdT[:, cs3], cx.ident)
            kd_ = T()
            em.copy("act", kd_[:, 0:128], psT[:, 0:128])
            PP(psT)
            chk(4.3)
            psA = P()
            em.mm(psA[:, 0:256], kg[:, cs3], QB[:, ci, :, :].re("p h t -> p (h t)"))
            attT = T()
            em.tt("dve", attT[:, :].re("p (h t) -> p h t", h=2), psA[:, 0:256].re("p (h t) -> p h t", h=2),
                  cx.mi.un(1).bc([128, 2, 128]), ALU.mult)
            PP(psA)
            chk(4.4)
            psO = P()
            for h in range(2):
                pb = slice(64 * h, 64 * h + 64)
                hs = slice(h * 128, (h + 1) * 128)
                em.mm(psO[:, hs], vg[ci][:, hs], attT[:, hs], start=True, stop=False)
                em.mm(psO[:, hs], SG[:, :], QB[:, ci, h, :], start=False, stop=True)
            chk(4.5)
            sq = T()
            em.act(sq[:, :], psO[:, 0:256], AF.Square)
            psN = P()
            em.mm(psN[:, 0:256], cx.ones, sq[:, :])
            em.act(sq[:, :], psN[:, 0:256], AF.Sqrt, bias=cx.eps_ln[:, 3:4], scale=1.0 / 128.0)
            PP(psN)
            em.recip(sq[:, :], sq[:, :])
            chk(4.6)
            o = T()
            em.tt("dve", o[:, :], psO[:, 0:256], sq[:, :], ALU.mult)
            PP(psO)
            for h in range(2):
                em.tt("pool", yT[:, 4 + h, cs], o[:, h * 128:(h + 1) * 128], sgs[h][:, cs3], ALU.mult)
            chk(4.7)
            psS = P()
            em.mm(psS[:, 0:256], kd_[:, 0:128], vg[ci][:, :])
            for h in range(2):
                pb = slice(64 * h, 64 * h + 64)
                em.stt("dve", SGn[pb, :], SG[pb, :], smg[pb, 4 + ci:5 + ci], psS[pb, h * 128:(h + 1) * 128], ALU.mult, ALU.add)
            PP(psS)
            PT(kd_, attT, sq, o, vg[ci])
        cx.small.put(smg)
        PG(kg, kdT, g0, g1_)
        em.dma("sp", y_dst(blk), yT[:, :, :])


def pack_phaseA(inp, l, hh):
    w_in = inp["w_in"][l]
    cols = []
    o_rw = 0
    for base in (0, 512, 1024):
        cols.append(np.arange(base + 256 * hh, base + 256 * hh + 256))
    cols.append(np.arange(1536, 1664))
    o_xbc = 2176
    cols.append(np.arange(o_xbc + 256 * hh, o_xbc + 256 * hh + 256))
    cols.append(np.arange(o_xbc + 512 + 128 * hh, o_xbc + 512 + 128 * hh + 128))
    cols.append(np.arange(o_xbc + 768 + 128 * hh, o_xbc + 768 + 128 * hh + 128))
    cols.append(np.arange(3720 + 128 * hh, 3720 + 128 * hh + 128))
    cols.append(np.arange(3976 + 128 * hh, 3976 + 128 * hh + 128))
    cols.append(np.arange(4760 + 256 * hh, 4760 + 256 * hh + 256))
    cols.append(np.arange(4744, 4760))
    cols.append(np.arange(1664 + 256 * hh, 1664 + 256 * hh + 256))
    cols.append(np.arange(3208 + 256 * hh, 3208 + 256 * hh + 256))
    cols.append(np.arange(4232 + 256 * hh, 4232 + 256 * hh + 256))
    cols.append(np.arange(3200 + 4 * hh, 3200 + 4 * hh + 4))
    cols = np.concatenate(cols)
    assert cols.shape[0] == NCOL_A
    wA = np.ascontiguousarray(w_in[:, cols])
    pfm = np.zeros((128, NPF), np.float32)
    mu = inp["rw_mu"][l]
    ch = 256 * hh
    for i in range(2):
        sl = slice(ch + 128 * i, ch + 128 * i + 128)
        pfm[:, PF_MU + 0 + i] = mu[0:512][sl]
        pfm[:, PF_MU + 2 + i] = mu[512:1024][sl]
        pfm[:, PF_MU + 4 + i] = mu[1024:1536][sl]
        pfm[:, PF_W0 + i] = inp["rw_w0"][l][sl]
        pfm[:, PF_A0 + i] = inp["rw_a0"][l][sl]
        pfm[:, PF_KK + i] = inp["rw_k_k"][l][sl]
        pfm[:, PF_KA + i] = inp["rw_k_a"][l][sl]
        pfm[:, PF_RK + i] = inp["rw_r_k"][l].reshape(512)[sl]
    pfm[:, PF_MU + 6] = mu[1536:1664]
    cw = inp["m2_conv_w"][l]
    cb = inp["m2_conv_b"][l]
    xidx = [np.arange(ch, ch + 128), np.arange(ch + 128, ch + 256),
            np.arange(512 + 128 * hh, 512 + 128 * hh + 128), np.arange(768 + 128 * hh, 768 + 128 * hh + 128)]
    for q in range(4):
        for j in range(4):
            pfm[:, PF_CW + q * 4 + j] = cw[j, xidx[q]]
        pfm[:, PF_CB + q] = cb[xidx[q]]
    pfm[:, PF_GKB] = inp["gla_gk_b"][l][128 * hh:128 * hh + 128]
    pfm[:, PF_GNW] = inp["gla_norm_w"][l]
    ptm = np.zeros((1, NPT), np.float32)
    ptm[0, PT_LNG:PT_LNG + 256] = inp["rw_ln_g"][l][ch:ch + 256]
    ptm[0, PT_LNB:PT_LNB + 256] = inp["rw_ln_b"][l][ch:ch + 256]
    ptm[0, PT_NW:PT_NW + 256] = inp["m2_norm_w"][l][ch:ch + 256]
    ptm[0, PT_DS:PT_DS + 256] = np.repeat(inp["m2_d_skip"][l][4 * hh:4 * hh + 4], 64)
    ptm[0, PT_DTB:PT_DTB + 4] = inp["m2_dt_bias"][l][4 * hh:4 * hh + 4]
    ptm[0, PT_ALOG:PT_ALOG + 4] = inp["m2_a_log"][l][4 * hh:4 * hh + 4]
    smallw = np.zeros((128, 384), np.float32)
    smallw[0:64, 0:256] = inp["rw_w_up"][l][:, ch:ch + 256]
    smallw[64:128, 0:256] = inp["rw_a_up"][l][:, ch:ch + 256]
    smallw[0:16, 256:384] = inp["gla_gk_up"][l][:, 128 * hh:128 * hh + 128]
    return wA, pfm, ptm, smallw


def pack_ada(inp, b, L):
    cvec = np.ascontiguousarray(inp["c"][b].reshape(8, 128).T)
    adab_fm = np.stack([np.ascontiguousarray(inp["ada_b"][l][0:2048].reshape(16, 128).T) for l in range(L)])
    adab_fm = np.concatenate([adab_fm, np.zeros((L, 128, 8), np.float32)], axis=2)
    adab_g = np.stack([inp["ada_b"][l][2048:3072].reshape(1, 1024) for l in range(L)])
    return cvec, adab_fm, adab_g


def setup_phaseB(cx, S):
    nc, es, em = cx.nc, cx.es, cx.em
    cx.wM = sb(nc, es, "wM", [128, 8, 3072], BF16)
    cx.wbr = sb(nc, es, "wbr", [128, 12, 1024], BF16)
    cx.wout = sb(nc, es, "wout", [128, 8, 1024], BF16)
    cx.pbt = sb(nc, es, "pbt", [128, 2048])
    cx.small = Pool(nc, es, "smB", 8, [128, 32], F32)
    cx.xtB = [sb(nc, es, "xtB%d" % i, [128, 1024]) for i in range(4)]
    cx.xn = sb(nc, es, "xnB", [128, 1024], BF16)
    cx.lntmp = sb(nc, es, "lntmpB", [128, 1024])
    cx.hT = sb(nc, es, "hTB", [128, 8, TB], BF16)
    cx.yb = sb(nc, es, "yb", [128, 12, TB], BF16)
    cx.mT = sb(nc, es, "mT", [128, 8, TB], BF16)
    cx.wk = Pool(nc, es, "wk", 6, [128, 512], F32)
    cx.res = [sb(nc, es, "res%d" % i, [128, 1024]) for i in range(2)]
    cx.eps_ln = sb(nc, es, "epslnB", [128, 4])
    em.memset("dve", cx.eps_ln[:, 0:1], LN_EPS)


def phaseB(cx, dr, l, S, x_src, y_src, out_dst):
    nc, es, em = cx.nc, cx.es, cx.em
    P, PP = cx.ps.get, cx.ps.put
    for kc in range(8):
        em.dma("pool", cx.wM[:, kc, :], dr["wM"][l][kc * 128:(kc + 1) * 128, :])
        em.dma("pool", cx.wout[:, kc, :], dr["wout"][l][kc * 128:(kc + 1) * 128, :])
    for kt in range(12):
        em.dma("pool", cx.wbr[:, kt, :], dr["wbr"][l][kt * 128:(kt + 1) * 128, :])
    em.dma("sp", cx.pbt[:, :], dr["pbt"][l].partition_broadcast(128))
    g1 = cx.g1[l]
    nblk = S // TB
    for blk in range(nblk):
        for ti in range(4):
            r0 = blk * TB + ti * 128
            em.dma("sp", cx.xtB[ti][:, :], x_src(r0))
            emit_ln_transpose(cx, cx.xtB[ti], cx.hT, ti, cx.sc1[l], cx.sh[l], cx.xn)
        em.dma("sp", cx.yb[:, :, :], y_src(blk * TB))
        for f in range(8):
            fs = slice(f * 128, (f + 1) * 128)
            macc = cx.wk.get()
            for br in range(3):
                psL = P()
                for kc in range(8):
                    em.mm(psL[:, :], cx.wM[:, kc, br * 1024 + f * 128:br * 1024 + (f + 1) * 128], cx.hT[:, kc, :],
                          start=(kc == 0), stop=(kc == 7))
                g = cx.wk.get()
                em.act(g[:, :], psL[:, :], AF.Sigmoid)
                PP(psL)
                psY = P()
                kts = [br * 2, br * 2 + 1, 6 + br * 2, 6 + br * 2 + 1]
                for i, kt in enumerate(kts):
                    em.mm(psY[:, :], cx.wbr[:, kt, fs], cx.yb[:, kt, :], start=(i == 0), stop=(i == 3))
                if br == 0:
                    em.tt("dve", macc[:, :], psY[:, :], g[:, :], ALU.mult)
                elif br == 1:
                    em.tt("dve", g[:, :], psY[:, :], g[:, :], ALU.mult)
                    em.tt("pool", macc[:, :], macc[:, :], g[:, :], ALU.add)
                else:
                    em.tt("dve", g[:, :], psY[:, :], g[:, :], ALU.mult)
                    em.tt("pool", cx.mT[:, f, :], macc[:, :], g[:, :], ALU.add)
                PP(psY)
                cx.wk.put(g)
            cx.wk.put(macc)
        for ti in range(4):
            res = cx.res[ti % 2]
            xt = cx.xtB[ti]
            for half in range(2):
                hs = slice(half * 512, (half + 1) * 512)
                psR = P()
                for kc in range(8):
                    em.mm(psR[:, :], cx.mT[:, kc, ti * 128:(ti + 1) * 128], cx.wout[:, kc, hs], start=(kc == 0), stop=(kc == 7))
                em.tt("dve", res[:, hs], psR[:, :], g1[:, hs], ALU.mult)
                PP(psR)
            em.stt("dve", res[:, :], xt[:, :], ALPHA, res[:, :], ALU.mult, ALU.add)
            st = cx.small.get()
            for hf in range(2):
                em.bn_stats(st[:, hf * 6:(hf + 1) * 6], res[:, hf * 512:(hf + 1) * 512])
            em.bn_aggr(st[:, 12:14], st[:, 0:12].re("p (a b) -> p a b", a=2))
            em.act(st[:, 14:15], st[:, 13:14], AF.Sqrt, bias=cx.eps_ln[:, 0:1])
            em.recip(st[:, 15:16], st[:, 14:15])
            em.stt("dve", st[:, 16:17], st[:, 12:13], -1.0, st[:, 15:16], ALU.mult, ALU.mult)
            em.act(res[:, :], res[:, :], AF.Identity, bias=st[:, 16:17], scale=st[:, 15:16])
            em.tt("dve", res[:, :], res[:, :], cx.pbt[:, 0:1024], ALU.mult)
            em.tt("pool", res[:, :], res[:, :], cx.pbt[:, 1024:2048], ALU.add)
            cx.small.put(st)
            r0 = blk * TB + ti * 128
            em.dma("sp", out_dst[r0:r0 + 128, :], res[:, :])


def pack_phaseB(inp, l):
    wM = np.ascontiguousarray(inp["w_in"][l][:, 5272:8344])
    wb = inp["w_branch"][l]
    rows = []
    for hh in range(2):
        for br in range(3):
            rows.append(wb[br, 256 * hh:256 * hh + 256, :])
    wbr = np.ascontiguousarray(np.concatenate(rows, axis=0))
    wout = np.ascontiguousarray(inp["w_out"][l])
    pbt = np.concatenate([inp["post_g"][l], inp["post_b"][l]]).reshape(1, 2048).astype(np.float32)
    return wM, wbr, wout, pbt


def _din(nc, name, shape, dt=F32):
    return nc.dram_tensor(name, shape, dt, kind="ExternalInput").ap()


def build_prog_A(S):
    nc = bass.Bass("TRN2", target_bir_lowering=False)
    dr = dict(consts=_din(nc, "consts", [128, NCONST]), cvec=_din(nc, "cvec", [128, 8]), adab_fm=_din(nc, "adab_fm", [1, 128, 24]),
              adab_g=_din(nc, "adab_g", [1, 1, 1024]), ada_w=_din(nc, "ada_w", [1, 1024, 3072]), wA=_din(nc, "wA", [1, 1024, NCOL_A]),
              pfm=_din(nc, "pfm", [1, 128, NPF]), ptm=_din(nc, "ptm", [1, 1, NPT]), smallw=_din(nc, "smallw", [1, 128, 384]))
    x = _din(nc, "x", [S, 1024])
    yT = nc.dram_tensor("yT", [768, S], BF16, kind="ExternalOutput").ap()
    with ExitStack() as es:
        em = Em(nc, es)
        cx = setup_common(nc, es, em, dr)
        with ExitStack() as est:
            emit_ada(cx, dr, 1, est)
            em.barrier()
        with ExitStack() as esA:
            cx.es = esA
            setup_phaseA(cx, S)
            ytk = Tk(yT, "yT")
            phaseA(cx, dr, 0, S, lambda r0: x[r0:r0 + 128, :],
                   lambda blk: ytk[:, blk * TB:(blk + 1) * TB].re("(a p) t -> p a t", p=128))
            em.barrier()
    return nc


def build_prog_B(S):
    nc = bass.Bass("TRN2", target_bir_lowering=False)
    dr = dict(consts=_din(nc, "consts", [128, NCONST]), cvec=_din(nc, "cvec", [128, 8]), adab_fm=_din(nc, "adab_fm", [1, 128, 24]),
              adab_g=_din(nc, "adab_g", [1, 1, 1024]), ada_w=_din(nc, "ada_w", [1, 1024, 3072]),
              wM=_din(nc, "wM", [1, 1024, 3072]), wbr=_din(nc, "wbr", [1, 1536, 1024]), wout=_din(nc, "wout", [1, 1024, 1024]),
              pbt=_din(nc, "pbt", [1, 1, 2048]))
    x = _din(nc, "x", [S, 1024])
    yT = _din(nc, "yTin", [1536, S], BF16)
    out = nc.dram_tensor("out", [S, 1024], F32, kind="ExternalOutput").ap()
    with ExitStack() as es:
        em = Em(nc, es)
        cx = setup_common(nc, es, em, dr)
        with ExitStack() as est:
            emit_ada(cx, dr, 1, est)
            em.barrier()
        with ExitStack() as esB:
            cx.es = esB
            setup_phaseB(cx, S)
            phaseB(cx, dr, 0, S, lambda r0: x[r0:r0 + 128, :],
                   lambda c0: yT[:, c0:c0 + TB].rearrange("(a p) t -> p a t", p=128), Tk(out, "out"))
            em.barrier()
    return nc


def build_prog_fused(S, L=DEPTH):
    SB = S // 2
    CY = min(1024, SB)
    NCH = S // CY
    XR = min(512, SB)
    NXC = SB // XR
    nc = bass.Bass("TRN2", target_bir_lowering=False)
    dr = dict(consts=_din(nc, "consts", [128, NCONST]), cvec=_din(nc, "cvec", [128, 8]), adab_fm=_din(nc, "adab_fm", [L, 128, 24]),
              adab_g=_din(nc, "adab_g", [L, 1, 1024]), ada_w=_din(nc, "ada_w", [L, 1024, 3072]), wA=_din(nc, "wA", [L, 1024, NCOL_A]),
              pfm=_din(nc, "pfm", [L, 128, NPF]), ptm=_din(nc, "ptm", [L, 1, NPT]), smallw=_din(nc, "smallw", [L, 128, 384]),
              wM=_din(nc, "wM", [L, 1024, 3072]), wbr=_din(nc, "wbr", [L, 1536, 1024]), wout=_din(nc, "wout", [L, 1024, 1024]),
              pbt=_din(nc, "pbt", [L, 1, 2048]))
    x = _din(nc, "x", [S, 1024])
    xh = _din(nc, "xh", [SB, 1024])
    out = nc.dram_tensor("out", [SB, 1024], F32, kind="ExternalOutput").ap()
    ysend = nc.dram_tensor("ysend", [NCH, 768, CY], BF16).ap()
    yall = nc.dram_tensor("yall", [NCH, 1536, CY], BF16).ap()
    yh = nc.dram_tensor("yh_buf", [NCH // 2, 1536, CY], BF16).ap()
    o0 = nc.dram_tensor("o_half", [SB, 1024], F32).ap()
    x1 = nc.dram_tensor("x_next", [NXC, 2 * XR, 1024], F32).ap()
    groups = [[0, 1], [2, 3], [4, 5], [6, 7]]
    half_elems = (NCH // 2) * 1536 * CY
    FW = 16384
    hrows = half_elems // FW
    yall_f = yall.rearrange("c r t -> (c r t)").rearrange("(a f) -> a f", f=FW)
    yh_f = yh.rearrange("c r t -> (c r t)").rearrange("(a f) -> a f", f=FW)
    with ExitStack() as es:
        em = Em(nc, es)
        ybase = nc.scalar.snap((nc.scalar.partition_id() % 2) * hrows)
        cx = setup_common(nc, es, em, dr)
        with ExitStack() as est:
            emit_ada(cx, dr, L, est)
            em.barrier()

        def x1row(t0):
            row0 = ((t0 % SB) // XR) * (2 * XR) + (t0 // SB) * XR + (t0 % XR)
            return x1.rearrange("c r d -> (c r) d")[row0:row0 + 128, :]

        xsrcA = lambda r0: x[r0:r0 + 128, :]
        for l in range(L):
            ys_tk = Tk(ysend, "ysend%d" % l)
            with ExitStack() as esA:
                cx.es = esA
                setup_phaseA(cx, S)
                phaseA(cx, dr, l, S, xsrcA,
                       lambda blk, ys_tk=ys_tk: ys_tk[(blk * TB) // CY, :, (blk * TB) % CY:(blk * TB) % CY + TB].re("(a p) t -> p a t", p=128))
            em.collectives("AllGather", [(ysend[c], yall[c]) for c in range(NCH)], groups)
            yh_tk = Tk(yh, "yh%d" % l)
            em.dma("act", V(yh_tk, yh_f), yall_f[bass.ds(ybase, hrows), :])
            dst = out if l == L - 1 else o0
            if l == 0:
                xsrc = lambda r0: xh[r0:r0 + 128, :]
            else:
                xsrc = lambda r0: o0[r0:r0 + 128, :]
            with ExitStack() as esB:
                cx.es = esB
                setup_phaseB(cx, SB)
                phaseB(cx, dr, l, SB, xsrc,
                       lambda c0, yh_tk=yh_tk: yh_tk[c0 // CY, :, c0 % CY:c0 % CY + TB].re("(a p) t -> p a t", p=128), Tk(dst, "dst%d" % l))
            if l < L - 1:
                em.collectives("AllGather", [(o0[c * XR:(c + 1) * XR, :], x1[c]) for c in range(NXC)], groups)
                xsrcA = x1row
        em.barrier()
        print("fused ninst", em.ninst, flush=True)
    return nc


def pack_ada_l(inp, b, l):
    cvec = np.ascontiguousarray(inp["c"][b].reshape(8, 128).T)
    fm = np.ascontiguousarray(inp["ada_b"][l][0:2048].reshape(16, 128).T)
    adab_fm = np.concatenate([fm, np.zeros((128, 8), np.float32)], axis=1)[None]
    adab_g = np.ascontiguousarray(inp["ada_b"][l][2048:3072].reshape(1, 1, 1024))
    return cvec, adab_fm, adab_g


_PROGS = {}


def kernel(**inputs):
    inp = {k: np.asarray(v) for k, v in inputs.items()}
    B, S = inp["x"].shape[0], inp["x"].shape[1]
    SB = S // 2
    consts = make_consts()
    if "A" not in _PROGS:
        _PROGS["A"] = build_prog_A(S)
        _PROGS["B"] = build_prog_B(SB)
    progA, progB = _PROGS["A"], _PROGS["B"]
    x_cur = [np.ascontiguousarray(inp["x"][b]) for b in range(B)]
    ncore = 2 * B
    for l in range(DEPTH):
        packs = [pack_phaseA(inp, l, hh) for hh in range(2)]
        mapsA = []
        for core in range(ncore):
            b, hh = core // 2, core % 2
            wA, pfm, ptm, smallw = packs[hh]
            cvec, adab_fm, adab_g = pack_ada_l(inp, b, l)
            mapsA.append(dict(consts=consts, cvec=cvec, adab_fm=adab_fm, adab_g=adab_g, ada_w=inp["ada_w"][l:l + 1],
                              wA=wA[None], pfm=pfm[None], ptm=ptm[None], smallw=smallw[None], x=x_cur[b]))
        resA = run_bass_kernel_spmd(progA, mapsA, core_ids=list(range(ncore)))
        yT = [np.asarray(r["yT"]) for r in resA.results]
        wM, wbr, wout, pbt = pack_phaseB(inp, l)
        mapsB = []
        for core in range(ncore):
            b, th = core // 2, core % 2
            ts = slice(th * SB, (th + 1) * SB)
            yfull = np.ascontiguousarray(np.concatenate([yT[2 * b][:, ts], yT[2 * b + 1][:, ts]], axis=0))
            cvec, adab_fm, adab_g = pack_ada_l(inp, b, l)
            mapsB.append(dict(consts=consts, cvec=cvec, adab_fm=adab_fm, adab_g=adab_g, ada_w=inp["ada_w"][l:l + 1],
                              wM=wM[None], wbr=wbr[None], wout=wout[None], pbt=pbt[None],
                              x=np.ascontiguousarray(x_cur[b][ts]), yTin=yfull))
        resB = run_bass_kernel_spmd(progB, mapsB, core_ids=list(range(ncore)))
        outs = [np.asarray(r["out"]) for r in resB.results]
        x_cur = [np.ascontiguousarray(np.concatenate([outs[2 * b], outs[2 * b + 1]], axis=0)) for b in range(B)]
    return np.stack(x_cur).astype(np.float32)
```
